# Optimizing a Trainium2 kernel written in Bass

```python
import math
import jax, jax.numpy as jnp
from jax import lax
import numpy as np

D_MODEL = 1024
BATCH = 2
SEQ = 16384
DEPTH = 1
DEC_BATCH = 16
DEC_SEQ = 4096
PAST_LEN = 128

N_HEADS_A = 8
DQK_A = 64
DV_A = 2 * DQK_A
ROT_DIM = DQK_A // 4
ROPE_THETA = 500000.0
Q_BLOCK = 128
N_HEADS_M = 4
DQK_M = 128
DV_M = 256
CHUNK = 128
D_FF = 2816
CONV_W = 3
ALPHA = (2.0 * DEPTH) ** 0.25
BETA = (8.0 * DEPTH) ** -0.25
LN_EPS = 1e-5
ADA_SCALE = 0.5

W_AQ = N_HEADS_A * 2 * DQK_A
W_AK = N_HEADS_A * 2 * DQK_A
W_AV = N_HEADS_A * DV_A
W_MQ = N_HEADS_M * DQK_M
W_MK = N_HEADS_M * DQK_M
W_MV = N_HEADS_M * DV_M
W_MO = N_HEADS_M * DV_M
W_MG = 2 * 2 * N_HEADS_M
W_BG = 2 * D_MODEL
SEG_WIDTHS = (W_AQ, W_AK, W_AV, W_MQ, W_MK, W_MV, W_MO, W_MG, W_BG)
SPLITS = [int(s) for s in np.cumsum(SEG_WIDTHS)[:-1]]
D_IN = int(sum(SEG_WIDTHS))

kernel_name = "hybrid_diffattn_mlstm_encoder"


def layer_norm(x, g=None, b=None):
    xf = x.astype(jnp.float32)
    mu = jnp.mean(xf, axis=-1, keepdims=True)
    var = jnp.mean(jnp.square(xf - mu), axis=-1, keepdims=True)
    y = (xf - mu) * lax.rsqrt(var + LN_EPS)
    if g is not None:
        y = y * g.astype(jnp.float32) + b.astype(jnp.float32)
    return y.astype(x.dtype)


def head_norm(x, g):
    xf = x.astype(jnp.float32)
    mu = jnp.mean(xf, axis=-1, keepdims=True)
    var = jnp.mean(jnp.square(xf - mu), axis=-1, keepdims=True)
    return ((xf - mu) * lax.rsqrt(var + LN_EPS) * g.astype(jnp.float32)).astype(x.dtype)


def rms_norm(x, g):
    xf = x.astype(jnp.float32)
    y = xf * lax.rsqrt(jnp.mean(jnp.square(xf), axis=-1, keepdims=True) + LN_EPS)
    return (y * g.astype(jnp.float32)).astype(x.dtype)


def rope_tables(seq):
    inv = ROPE_THETA ** (-jnp.arange(0, ROT_DIM, 2, dtype=jnp.float32) / ROT_DIM)
    ang = jnp.arange(seq, dtype=jnp.float32)[:, None] * inv[None, :]
    return jnp.cos(ang), jnp.sin(ang)


def apply_partial_rope(x, cos, sin):
    half = ROT_DIM // 2
    x1, x2, rest = x[..., :half], x[..., half:ROT_DIM], x[..., ROT_DIM:]
    c = cos[:, None, None, :].astype(x.dtype)
    s = sin[:, None, None, :].astype(x.dtype)
    return jnp.concatenate([x1 * c - x2 * s, x2 * c + x1 * s, rest], axis=-1)


def diff_attention(q, k, v, lam, subln_g, lam_init):
    B, S = q.shape[0], q.shape[1]
    cos, sin = rope_tables(S)
    q = apply_partial_rope(q, cos, sin) * (DQK_A ** -0.5)
    k = apply_partial_rope(k, cos, sin)
    kt = k.transpose(0, 2, 3, 1, 4)
    vt = v.transpose(0, 2, 1, 3)
    nb = S // Q_BLOCK
    qb = q.reshape(B, nb, Q_BLOCK, N_HEADS_A, 2, DQK_A).transpose(1, 0, 3, 4, 2, 5)

    def block(qi):
        s = jnp.einsum('bhiqd,bhikd->bhiqk', qi, kt).astype(jnp.float32)
        p = jax.nn.softmax(s, axis=-1)
        a = p[:, :, 0] - lam * p[:, :, 1]
        return jnp.einsum('bhqk,bhkd->bhqd', a.astype(vt.dtype), vt)

    o = lax.map(block, qb)
    o = o.transpose(1, 0, 3, 2, 4).reshape(B, S, N_HEADS_A, DV_A)
    o = rms_norm(o, subln_g) * (1.0 - lam_init)
    return o.reshape(B, S, N_HEADS_A * DV_A)


def mlstm_dir(q, k, v, ig, lf):
    B, H, S, dk = q.shape
    dv = v.shape[-1]
    nc = S // CHUNK

    def chunks(a):
        return jnp.moveaxis(a.reshape(B, H, nc, CHUNK, *a.shape[3:]), 2, 0)

    tri = jnp.tril(jnp.ones((CHUNK, CHUNK), dtype=bool))

    def step(carry, xs):
        C, n, m = carry
        qc, kc, vc, ic, fc = xs
        b = jnp.cumsum(fc, axis=-1)
        d = b[..., :, None] - b[..., None, :] + ic[..., None, :]
        d = jnp.where(tri, d, -jnp.inf)
        inter = b + m[..., None]
        m_t = jnp.maximum(inter, jnp.max(d, axis=-1))
        w = jnp.exp(d - m_t[..., None])
        w_inter = jnp.exp(inter - m_t)
        qk = jnp.einsum('bhtd,bhsd->bhts', qc, kc) * w
        num = jnp.einsum('bhts,bhsv->bhtv', qk, vc) + w_inter[..., None] * jnp.einsum('bhvd,bhtd->bhtv', C, qc)
        den = jnp.sum(qk, axis=-1) + w_inter * jnp.einsum('bhd,bhtd->bht', n, qc)
        h = num / jnp.maximum(jnp.abs(den), jnp.exp(-m_t))[..., None]
        m_new = m_t[..., -1]
        ws = jnp.exp(b[..., -1:] - b + ic - m_new[..., None])
        decay = jnp.exp(b[..., -1] + m - m_new)
        C = decay[..., None, None] * C + jnp.einsum('bhs,bhsv,bhsd->bhvd', ws, vc, kc)
        n = decay[..., None] * n + jnp.einsum('bhs,bhsd->bhd', ws, kc)
        return (C, n, m_new), h

    init = (jnp.zeros((B, H, dv, dk), jnp.float32),
            jnp.zeros((B, H, dk), jnp.float32),
            jnp.zeros((B, H), jnp.float32))
    _, hs = lax.scan(step, init, (chunks(q), chunks(k), chunks(v), chunks(ig), chunks(lf)))
    return jnp.moveaxis(hs, 0, 2).reshape(B, H, S, dv)


def mlstm_bidir(mq, mk, mv, mo, mg, b_mgate, mnorm_g):
    B, S = mq.shape[0], mq.shape[1]
    q = mq.reshape(B, S, N_HEADS_M, DQK_M).transpose(0, 2, 1, 3).astype(jnp.float32)
    k = mk.reshape(B, S, N_HEADS_M, DQK_M).transpose(0, 2, 1, 3).astype(jnp.float32) * (DQK_M ** -0.5)
    v = mv.reshape(B, S, N_HEADS_M, DV_M).transpose(0, 2, 1, 3).astype(jnp.float32)
    g = mg.reshape(B, S, 2, 2, N_HEADS_M).astype(jnp.float32) + b_mgate.astype(jnp.float32)
    g = g.transpose(2, 3, 0, 4, 1)
    fwd = mlstm_dir(q, k, v, g[0, 0], jax.nn.log_sigmoid(g[0, 1]))
    flip = lambda a: jnp.flip(a, axis=2)
    bwd = flip(mlstm_dir(flip(q), flip(k), flip(v), flip(g[1, 0]), jax.nn.log_sigmoid(flip(g[1, 1]))))
    h = (fwd + bwd).transpose(0, 2, 1, 3)
    h = head_norm(h, mnorm_g).reshape(B, S, N_HEADS_M * DV_M).astype(mo.dtype)
    return jax.nn.sigmoid(mo) * h


def dwconv3(u, w, b):
    up = jnp.pad(u, ((0, 0), (1, 1), (0, 0)))
    return up[:, :-2] * w[0] + up[:, 1:-1] * w[1] + up[:, 2:] * w[2] + b


def encoder_layer(x, c, l, w_ada, b_ada, w_in, lambda_qk, subln_g, b_mgate, mnorm_g,
                  w_pa, w_pm, w_out, ln1_g, ln1_b, w_up, conv_w, conv_b, w_down, ln2_g, ln2_b):
    B, S = x.shape[0], x.shape[1]
    lam_init = 0.8 - 0.6 * math.exp(-0.3 * l)
    ada = jax.nn.silu(c) @ w_ada + b_ada
    sh1, sc1, g1, sh2, sc2, g2 = jnp.split(ada[:, None, :], 6, axis=-1)

    h = layer_norm(x) * (1.0 + sc1) + sh1
    proj = h @ w_in
    aq, ak, av, mq, mk, mv, mo, mg, bg = jnp.split(proj, SPLITS, axis=-1)
    lqk = lambda_qk.astype(jnp.float32)
    lam = jnp.exp(jnp.sum(lqk[0] * lqk[1])) - jnp.exp(jnp.sum(lqk[2] * lqk[3])) + lam_init
    ya = diff_attention(aq.reshape(B, S, N_HEADS_A, 2, DQK_A), ak.reshape(B, S, N_HEADS_A, 2, DQK_A),
                        av.reshape(B, S, N_HEADS_A, DV_A), lam, subln_g, lam_init)
    ym = mlstm_bidir(mq, mk, mv, mo, mg, b_mgate, mnorm_g)
    ga, gm = jnp.split(jax.nn.sigmoid(bg), 2, axis=-1)
    mix = (ga * (ya @ w_pa) + gm * (ym @ w_pm)) @ w_out
    x = layer_norm(ALPHA * x + g1 * mix, ln1_g, ln1_b)

    h = layer_norm(x) * (1.0 + sc2) + sh2
    u = h @ w_up
    ug, uv = jnp.split(u, 2, axis=-1)
    ug = dwconv3(ug, conv_w, conv_b)
    y = (jax.nn.gelu(ug) * uv) @ w_down
    return layer_norm(ALPHA * x + g2 * y, ln2_g, ln2_b)


def run_trunk(x, c, w_ada, b_ada, w_in, lambda_qk, subln_g, b_mgate, mnorm_g,
              w_pa, w_pm, w_out, ln1_g, ln1_b, w_up, conv_w, conv_b, w_down, ln2_g, ln2_b):
    for l in range(DEPTH):
        x = encoder_layer(x, c, l, w_ada[l], b_ada[l], w_in[l], lambda_qk[l], subln_g[l], b_mgate[l],
                          mnorm_g[l], w_pa[l], w_pm[l], w_out[l], ln1_g[l], ln1_b[l], w_up[l],
                          conv_w[l], conv_b[l], w_down[l], ln2_g[l], ln2_b[l])
    return x


def setup_inputs(seed: int = 0) -> dict:
    key = jax.random.key(seed)
    ks = jax.random.split(key, 24)
    f32 = jnp.float32
    D = D_MODEL
    nrm = lambda k, shape, s: jax.random.normal(k, shape, f32) * s
    col_scale = jnp.concatenate([
        jnp.full((W_AQ + W_AK,), 1.0, f32), jnp.full((W_AV,), BETA, f32),
        jnp.full((W_MQ + W_MK,), 1.0, f32), jnp.full((W_MV,), BETA, f32),
        jnp.full((W_MO + W_MG + W_BG,), 1.0, f32)])
    b_mgate = nrm(ks[7], (DEPTH, 2, 2, N_HEADS_M), 0.1)
    f_bias = 3.0 + 3.0 * jax.random.uniform(ks[8], (DEPTH, 2, N_HEADS_M), f32)
    b_mgate = b_mgate.at[:, :, 1].add(f_bias)
    return {
        "x_prompt": nrm(ks[0], (BATCH, SEQ, D), 1.0),
        "x_sample": nrm(ks[1], (DEC_BATCH, DEC_SEQ, D), 1.0),
        "c_prompt": nrm(ks[2], (BATCH, D), 1.0),
        "c_sample": nrm(ks[3], (DEC_BATCH, D), 1.0),
        "w_ada": nrm(ks[4], (DEPTH, D, 6 * D), ADA_SCALE * D ** -0.5),
        "b_ada": nrm(ks[5], (DEPTH, 6 * D), 0.02),
        "w_in": nrm(ks[6], (DEPTH, D, D_IN), D ** -0.5) * col_scale,
        "lambda_qk": nrm(ks[9], (DEPTH, 4, DQK_A), 0.1),
        "subln_g": 1.0 + nrm(ks[10], (DEPTH, DV_A), 0.02),
        "b_mgate": b_mgate,
        "mnorm_g": 1.0 + nrm(ks[11], (DEPTH, N_HEADS_M, DV_M), 0.02),
        "w_pa": nrm(ks[12], (DEPTH, N_HEADS_A * DV_A, D), BETA * (N_HEADS_A * DV_A) ** -0.5),
        "w_pm": nrm(ks[13], (DEPTH, N_HEADS_M * DV_M, D), BETA * (N_HEADS_M * DV_M) ** -0.5),
        "w_out": nrm(ks[14], (DEPTH, D, D), BETA * D ** -0.5),
        "ln1_g": 1.0 + nrm(ks[15], (DEPTH, D), 0.02),
        "ln1_b": nrm(ks[16], (DEPTH, D), 0.02),
        "w_up": nrm(ks[17], (DEPTH, D, 2 * D_FF), D ** -0.5),
        "conv_w": nrm(ks[18], (DEPTH, CONV_W, D_FF), CONV_W ** -0.5),
        "conv_b": nrm(ks[19], (DEPTH, D_FF), 0.02),
        "w_down": nrm(ks[20], (DEPTH, D_FF, D), BETA * D_FF ** -0.5),
        "ln2_g": 1.0 + nrm(ks[21], (DEPTH, D), 0.02),
        "ln2_b": nrm(ks[22], (DEPTH, D), 0.02),
    }


def reference(x_prompt, x_sample, c_prompt, c_sample, w_ada, b_ada, w_in, lambda_qk, subln_g,
              b_mgate, mnorm_g, w_pa, w_pm, w_out, ln1_g, ln1_b, w_up, conv_w, conv_b, w_down,
              ln2_g, ln2_b):
    y_prompt = run_trunk(x_prompt, c_prompt, w_ada, b_ada, w_in, lambda_qk, subln_g, b_mgate, mnorm_g,
                         w_pa, w_pm, w_out, ln1_g, ln1_b, w_up, conv_w, conv_b, w_down, ln2_g, ln2_b)
    y_sample = run_trunk(x_sample, c_sample, w_ada, b_ada, w_in, lambda_qk, subln_g, b_mgate, mnorm_g,
                         w_pa, w_pm, w_out, ln1_g, ln1_b, w_up, conv_w, conv_b, w_down, ln2_g, ln2_b)
    return (y_prompt, y_sample)
```

```python
import concourse.bass as bass
import concourse.mybir as mybir

F32 = mybir.dt.float32
BF16 = mybir.dt.bfloat16
AF = mybir.ActivationFunctionType
ALU = mybir.AluOpType
AX = mybir.AxisListType

COMPUTE = ("pe", "act", "dve", "pool")
QUEUES = ("sp", "act", "pool")
N_DMA_SEMS = 24


class Buf:
    __slots__ = ("name", "writers", "readers")

    def __init__(self, name=""):
        self.name = name
        self.writers = {}
        self.readers = {}


class Op:
    __slots__ = ("eng", "fn", "deps", "sig", "sem", "val", "idx", "is_dma", "key", "pre")

    def __init__(self, eng, fn):
        self.eng = eng
        self.fn = fn
        self.deps = []
        self.sig = False
        self.sem = None
        self.val = None
        self.idx = None
        self.is_dma = False
        self.key = eng
        self.pre = None


class Prog:
    def __init__(self, nc):
        self.nc = nc
        self.streams = {e: [] for e in ("pe", "act", "dve", "pool", "sp")}
        self.dma_count = [0] * N_DMA_SEMS
        self.dma_rr = 0
        self.out_dmas = []
        self.fence = []
        self.last_dma = [None] * N_DMA_SEMS

    def _add(self, op, reads, writes):
        st = self.streams[op.eng]
        op.idx = len(st)
        deps = list(self.fence)
        for b in reads:
            for k, w in b.writers.items():
                deps.append(w)
        for b in writes:
            for k, r in b.readers.items():
                deps.append(r)
            for k, w in b.writers.items():
                deps.append(w)
        seen = set()
        for d in deps:
            if d is op or id(d) in seen:
                continue
            seen.add(id(d))
            if (not d.is_dma) and d.eng == op.eng:
                if not op.is_dma:
                    if op.eng == "pe":
                        continue
                    if op.idx - d.idx > 3:
                        continue
            op.deps.append(d)
        for b in reads:
            b.readers[op.key] = op
        for b in writes:
            b.writers[op.key] = op
        st.append(op)
        return op

    def op(self, eng, fn, reads=(), writes=()):
        return self._add(Op(eng, fn), reads, writes)

    def dma(self, q, out, in_, reads=(), writes=(), is_output=False, slow=False):
        k = self.dma_rr
        self.dma_rr = (self.dma_rr + 1) % N_DMA_SEMS
        self.dma_count[k] += 1
        n = self.dma_count[k]
        o = Op(q, (lambda e, out=out, in_=in_: e.dma_start(out=out, in_=in_, allow_slow_non_contiguous=True)) if slow else (lambda e, out=out, in_=in_: e.dma_start(out=out, in_=in_)))
        o.is_dma = True
        o.key = ("dma", k)
        o.sem = k
        o.val = 16 * n
        o.pre = (k, 16 * (n - 1))
        self._add(o, reads, writes)
        self.last_dma[k] = o
        if is_output:
            self.out_dmas.append(o)
        return o

    def emit(self):
        nc = self.nc
        import contextlib
        with contextlib.ExitStack() as es:
            esem = {e: es.enter_context(nc.semaphore("sem_" + e)) for e in COMPUTE}
            dsem = [es.enter_context(nc.semaphore("dsem%d" % i)) for i in range(N_DMA_SEMS)]
            for e, st in self.streams.items():
                for o in st:
                    for d in o.deps:
                        if not d.is_dma:
                            d.sig = True
            for e, st in self.streams.items():
                c = 0
                for o in st:
                    if o.is_dma:
                        o.sem = dsem[o.sem] if isinstance(o.sem, int) else o.sem
                        continue
                    o.sem = esem[e]
                    if o.sig:
                        c += 1
                        o.val = c
            block = es.enter_context(nc.Block())
            engmap = {"pe": "tensor", "act": "scalar", "dve": "vector", "pool": "gpsimd", "sp": "sync"}
            out_dmas = self.out_dmas

            def make(ename, st, final):
                def body(eng):
                    waited = {}
                    for o in st:
                        if o.pre is not None:
                            s = dsem[o.pre[0]]
                            v = o.pre[1]
                            if v > 0 and waited.get(id(s), 0) < v:
                                eng.wait_ge(s, v)
                                waited[id(s)] = v
                        for d in o.deps:
                            if waited.get(id(d.sem), 0) < d.val:
                                eng.wait_ge(d.sem, d.val)
                                waited[id(d.sem)] = d.val
                        ins = o.fn(eng)
                        if o.is_dma:
                            ins.then_inc(o.sem, 16)
                        elif o.sig:
                            ins.then_inc(o.sem, 1)
                    if final:
                        for o in out_dmas:
                            if waited.get(id(o.sem), 0) < o.val:
                                eng.wait_ge(o.sem, o.val)
                                waited[id(o.sem)] = o.val
                return body

            for e, st in self.streams.items():
                deco = getattr(block, engmap[e])
                deco(make(e, st, e == "sp"))

import math
import numpy as np
import ml_dtypes
from concourse.bass_utils import run_bass_kernel_spmd

D = 1024
HA = 8
HM = 4
DFF = 2816
NF = 22
DIN = 8208
EPS = 1e-5
ALPHA_C = 2.0 ** 0.25
LAM_INIT = 0.8 - 0.6 * math.exp(-0.3 * 0)
LNSC = -0.5 * math.log(128.0)
NEG = -1.0e4
COLS = dict(aq=(0, 1024), ak=(1024, 2048), av=(2048, 3072), mq=(3072, 3584), mk=(3584, 4096),
            mv=(4096, 5120), mo=(5120, 6144), mg=(6144, 6160), bg=(6160, 8208))
ARENA_ELEMS = 94208


class Arena:
    def __init__(self, A, total):
        self.A = A
        self.lo = 0
        self.hi = total
        self.base = 0

    def bf(self, n, persist=False):
        n2 = (n + 15) // 16 * 16
        if persist:
            self.hi -= n2
            off = self.hi
        else:
            off = self.lo
            self.lo += n2
        assert self.lo <= self.hi, ("arena overflow", self.lo, self.hi)
        return self.A[:, off:off + n]

    def f32(self, n, persist=False):
        return self.bf(2 * n, persist).bitcast(F32)

    def mark(self):
        return self.lo

    def reset(self, m=0):
        self.lo = m


class Rot:
    def __init__(self, aps, name="r"):
        self.aps = aps
        self.bufs = [Buf(name + str(i)) for i in range(len(aps))]
        self.i = 0

    def next(self):
        k = self.i % len(self.aps)
        self.i += 1
        return self.aps[k], self.bufs[k]


def build(So, Sc):
    nc = bass.Bass("TRN2", target_bir_lowering=False)
    NJ = 3
    So2 = So + 256 if Sc > So + 256 else So
    SO = [So, So, So2]
    nchc = Sc // 128

    def din(name, shape, dt=F32):
        return nc.dram_tensor(name, list(shape), dt, kind="ExternalInput").ap()

    def dscr(name, shape, dt=BF16):
        return nc.dram_tensor(name, list(shape), dt).ap()

    xs01 = din("xs", [2, So, D])
    xs2 = din("xs2", [So2, D])
    xs = [xs01[0], xs01[1], xs2]
    xctx = din("xctx", [Sc, D])
    cT = din("cT", [128, 8, NJ])
    ropeo01 = din("ropeo", [2, So, 16])
    ropeo2 = din("ropeo2", [So2, 16])
    ropeo = [ropeo01[0], ropeo01[1], ropeo2]
    ropec = din("ropec", [Sc, 16])
    mb_in = din("mb", [4, Sc])
    ma_in = din("ma", [4, Sc])
    w_ada = din("w_ada", [D, 6 * D])
    b_adaT = din("b_adaT", [128, 48])
    b_ada = din("b_ada", [1, 6 * D])
    w_in = din("w_in", [D, DIN])
    lambda_qk = din("lambda_qk", [1, 256])
    subln = din("subln", [128, 1])
    bmgT = din("bmgT", [4, 4])
    mnorm = din("mnorm", [1, 1024])
    w_pa = din("w_pa", [D, D])
    w_pm = din("w_pm", [D, D])
    w_out = din("w_out", [D, D])
    ln1g = din("ln1g", [1, D])
    ln1b = din("ln1b", [1, D])
    w_up = din("w_up", [D, 2 * DFF])
    convwT = din("convwT", [128, NF, 3])
    convbT = din("convbT", [128, NF])
    w_down = din("w_down", [DFF, D])
    ln2g = din("ln2g", [1, D])
    ln2b = din("ln2b", [1, D])
    ident_in = din("ident", [128, 128])
    sel_in = din("sel", [4, 512])
    maskF_in = din("maskF", [128, 128])
    maskB_in = din("maskB", [128, 128])
    yout01 = nc.dram_tensor("y", [2, So, D], F32, kind="ExternalOutput").ap()
    yout2 = nc.dram_tensor("y2", [So2, D], F32, kind="ExternalOutput").ap()
    yout = [yout01[0], yout01[1], yout2]

    Sctx = [So, So, Sc]
    QT = [dscr("QT%d" % j, [HA, 128, SO[j]]) for j in range(NJ)]
    KT = [dscr("KT%d" % j, [HA, 128, Sctx[j]]) for j in range(NJ)]
    VV = [dscr("VV%d" % j, [HA, Sctx[j], 128]) for j in range(NJ)]
    mqT = [dscr("mqT%d" % j, [HM, 128, SO[j]]) for j in range(NJ)]
    mkT = [dscr("mkT%d" % j, [HM, 128, SO[j]]) for j in range(NJ)]
    mkt = [dscr("mkt%d" % j, [SO[j], 512]) for j in range(NJ)]
    mvt = [dscr("mvt%d" % j, [SO[j], 1024]) for j in range(NJ)]
    mktc = dscr("mktc", [Sc, 512])
    mvtc = dscr("mvtc", [Sc, 1024])
    GG = [dscr("GG%d" % j, [16, SO[j]], F32) for j in range(NJ)]
    GGc = dscr("GGc", [16, Sc], F32)
    AMs = dscr("AMs", [2, 4, Sc], F32)
    smo = [dscr("smo%d" % j, [SO[j], 1024]) for j in range(NJ)]
    sbg = [dscr("sbg%d" % j, [SO[j], 2048]) for j in range(NJ)]
    yaT = [dscr("yaT%d" % j, [HA, 128, SO[j]]) for j in range(NJ)]
    ymT = [dscr("ymT%d" % j, [8, 128, SO[j]]) for j in range(NJ)]
    x1s = [dscr("x1s%d" % j, [SO[j], D], F32) for j in range(NJ)]
    h2T = [dscr("h2T%d" % j, [8, 128, SO[j] + 2]) for j in range(NJ)]
    adab = dscr("adab", [NJ, 2, D], F32)
    wb_in = dscr("wb_in", [128, 8, DIN])

    P = Prog(nc)
    A_ = nc.alloc_sbuf_tensor("arena", [128, ARENA_ELEMS], BF16).ap()
    PS = nc.alloc_psum_tensor("psum", [128, 4096], F32).ap()
    ar = Arena(A_, ARENA_ELEMS)
    bankb = [Buf("bank%d" % i) for i in range(8)]

    def bank(i, n=512, off=0):
        return PS[:, i * 512 + off:i * 512 + off + n]

    def bankbf(i):
        return PS[:, i * 512:(i + 1) * 512].bitcast(BF16)

    dbuf = {}

    def DB(name):
        if name not in dbuf:
            dbuf[name] = Buf(name)
        return dbuf[name]

    identf = ar.f32(128, True)
    identb = ar.bf(128, True)
    maskF = ar.bf(128, True)
    maskB = ar.bf(128, True)
    onesb = ar.bf(128, True)
    onesf = ar.f32(128, True)
    self_ = ar.f32(512, True)
    sc1p1 = ar.f32(NJ * 8, True)
    sh1 = ar.f32(NJ * 8, True)
    sc2p1 = ar.f32(NJ * 8, True)
    sh2 = ar.f32(NJ * 8, True)
    small = ar.f32(64, True)
    lamc = small[:, 0:1]
    nlam = small[:, 1:2]
    gsc = small[:, 2:3]
    sublc = small[:, 3:4]
    bmg = small[0:4, 8:12]
    zero4 = small[0:4, 12:13]
    cwT = ar.f32(NF * 3, True)
    cbT = ar.f32(NF, True)
    bconst = Buf("const")
    tmpf = ar.f32(128)
    tmpm = ar.f32(256)
    P.dma("sp", identf, ident_in, writes=[bconst])
    P.dma("sp", tmpm[:, 0:128], maskF_in, writes=[bconst])
    P.dma("sp", tmpm[:, 128:256], maskB_in, writes=[bconst])
    P.dma("sp", self_[0:4, :], sel_in, writes=[bconst])
    P.dma("sp", sublc, subln, writes=[bconst])
    P.dma("sp", bmg, bmgT, writes=[bconst])
    P.dma("sp", cwT, convwT.rearrange("p f k -> p (f k)"), writes=[bconst])
    P.dma("sp", cbT, convbT, writes=[bconst])
    P.op("dve", lambda e: e.tensor_copy(out=identb, in_=identf), reads=[bconst], writes=[bconst])
    P.op("dve", lambda e: e.tensor_copy(out=maskF, in_=tmpm[:, 0:128]), reads=[bconst], writes=[bconst])
    P.op("dve", lambda e: e.tensor_copy(out=maskB, in_=tmpm[:, 128:256]), reads=[bconst], writes=[bconst])
    P.op("dve", lambda e: e.memset(onesb, 1.0), writes=[bconst])
    P.op("dve", lambda e: e.memset(onesf, 1.0), writes=[bconst])
    P.op("dve", lambda e: e.memset(zero4, 0.0), writes=[bconst])
    lq = ar.f32(256)
    lt = ar.f32(128)
    l2 = ar.f32(2)
    P.dma("sp", lq, lambda_qk.partition_broadcast(128), writes=[bconst])
    lqv = lq.rearrange("p (a b d) -> p a b d", a=2, b=2)
    P.op("dve", lambda e: e.tensor_tensor(out=lt.rearrange("p (a d) -> p a d", a=2), in0=lqv[:, :, 0, :], in1=lqv[:, :, 1, :], op=ALU.mult), reads=[bconst], writes=[bconst])
    P.op("dve", lambda e: e.tensor_reduce(out=l2, in_=lt.rearrange("p (a d) -> p a d", a=2), axis=AX.X, op=ALU.add), reads=[bconst], writes=[bconst])
    P.op("act", lambda e: e.activation(out=l2, in_=l2, func=AF.Exp), reads=[bconst], writes=[bconst])
    P.op("dve", lambda e: e.tensor_tensor(out=lamc, in0=l2[:, 0:1], in1=l2[:, 1:2], op=ALU.subtract), reads=[bconst], writes=[bconst])
    P.op("dve", lambda e: e.tensor_scalar(out=nlam, in0=lamc, scalar1=LAM_INIT, scalar2=-1.0, op0=ALU.add, op1=ALU.mult), reads=[bconst], writes=[bconst])
    P.op("dve", lambda e: e.tensor_scalar(out=gsc, in0=sublc, scalar1=(1.0 - LAM_INIT), scalar2=None, op0=ALU.mult), reads=[bconst], writes=[bconst])

    bar_bufs = {e: Buf("bar_" + e) for e in COMPUTE}
    bar_t = ar.f32(8, True)
    bar_tb = ar.bf(16, True)

    def barrier():
        s1 = []
        s1.append(P.op("dve", lambda e: e.memset(bar_t[0:1, 0:1], 0.0), writes=[bar_bufs["dve"]]))
        s1.append(P.op("pool", lambda e: e.memset(bar_t[0:1, 2:3], 0.0), writes=[bar_bufs["pool"]]))
        s1.append(P.op("act", lambda e: e.activation(out=bar_t[0:1, 4:5], in_=identf[0:1, 0:1], func=AF.Copy), writes=[bar_bufs["act"]]))
        s1.append(P.op("pe", lambda e: e.matmul(PS[0:1, 7 * 512:7 * 512 + 1], lhsT=identb[0:1, 0:1], rhs=identb[0:1, 0:1], start=True, stop=True),
                       reads=[bconst], writes=[bar_bufs["pe"], bankb[7]]))
        P.fence = list(s1) + [o for o in P.last_dma if o is not None]
        ar.reset()

    def convert_w_in():
        for k in range(8):
            P.dma("pool", wb_in[:, k, :], w_in[k * 128:(k + 1) * 128, :], writes=[DB("wb_in")])

    def phase0():
        cTt = ar.f32(8 * NJ)
        scb = ar.bf(8 * NJ)
        b0 = Buf("p0")
        P.dma("sp", cTt, cT.rearrange("p k j -> p (k j)"), writes=[b0])
        P.op("act", lambda e: e.activation(out=scb, in_=cTt, func=AF.Silu), reads=[b0], writes=[b0])
        scbv = scb.rearrange("p (k j) -> p k j", j=NJ)
        badT = ar.f32(48)
        P.dma("sp", badT, b_adaT, writes=[b0])
        wr = Rot([ar.bf(8 * 512) for _ in range(2)], "wada")
        rowt = Rot([ar.f32(512) for _ in range(2)], "rowt")
        brow = Rot([ar.f32(512) for _ in range(2)], "brow")
        dst = {0: (sh1, 0.0), 1: (sc1p1, 1.0), 3: (sh2, 0.0), 4: (sc2p1, 1.0)}
        for gi in range(12):
            wt, wb = wr.next()
            wv = wt.rearrange("p (k n) -> p k n", k=8)
            P.dma("pool", wv, w_ada[:, gi * 512:(gi + 1) * 512].rearrange("(k p) n -> p k n", p=128), writes=[wb])
            v = gi // 2
            if v in dst:
                tgt, addc = dst[v]
                for q in range(4):
                    kc = (gi % 2) * 4 + q
                    for k in range(8):
                        P.op("pe", lambda e, k=k, q=q, wv=wv: e.matmul(bank(0, NJ, 0), lhsT=wv[:, k, q * 128:(q + 1) * 128], rhs=scbv[:, k, :], start=(k == 0), stop=(k == 7)),
                             reads=[wb, b0], writes=[bankb[0]])
                    tv = tgt.rearrange("p (j k) -> p j k", j=NJ)[:, :, kc]
                    col = v * 8 + kc
                    P.op("dve", lambda e, tv=tv, col=col, addc=addc: e.tensor_scalar(out=tv, in0=bank(0, NJ, 0), scalar1=badT[:, col:col + 1], scalar2=addc, op0=ALU.add, op1=ALU.add),
                         reads=[bankb[0], b0], writes=[bconst])
            else:
                gidx = 0 if v == 2 else 1
                half = gi % 2
                bt, bb = brow.next()
                P.dma("sp", bt[0:1, :], b_ada[:, gi * 512:(gi + 1) * 512], writes=[bb])
                for j in range(NJ):
                    for k in range(8):
                        P.op("pe", lambda e, k=k, j=j, wv=wv: e.matmul(bank(1)[0:1, :], lhsT=scbv[:, k, j:j + 1], rhs=wv[:, k, :], start=(k == 0), stop=(k == 7)),
                             reads=[wb, b0], writes=[bankb[1]])
                    rt, rb = rowt.next()
                    P.op("dve", lambda e, rt=rt, bt=bt: e.tensor_tensor(out=rt[0:1, :], in0=bank(1)[0:1, :], in1=bt[0:1, :], op=ALU.add), reads=[bankb[1], bb], writes=[rb])
                    P.dma("sp", adab[j, gidx:gidx + 1, half * 512:(half + 1) * 512], rt[0:1, :], reads=[rb], writes=[DB("adab")])

    def ln_to_hT(xt, xb_, hT_dst, scp, shp, j, tr_bank, extra_reads, st, xn, xnb):
        stats, mv, rs = st
        P.op("dve", lambda e: e.bn_stats(out=stats[:, 0:6], in_=xt[:, 0:512]), reads=[xb_] + extra_reads, writes=[xnb])
        P.op("dve", lambda e: e.bn_stats(out=stats[:, 6:12], in_=xt[:, 512:1024]), reads=[xb_], writes=[xnb])
        P.op("dve", lambda e: e.bn_aggr(out=mv, in_=stats), reads=[xnb], writes=[xnb])
        P.op("act", lambda e: e.activation(out=rs, in_=mv[:, 1:2], func=AF.Sqrt, bias=EPS), reads=[xnb], writes=[xnb])
        P.op("dve", lambda e: e.reciprocal(out=rs, in_=rs), reads=[xnb], writes=[xnb])
        P.op("dve", lambda e: e.tensor_scalar(out=xn, in0=xt, scalar1=mv[:, 0:1], scalar2=rs, op0=ALU.subtract, op1=ALU.mult), reads=[xb_, xnb], writes=[xnb])
        pb = bankbf(tr_bank)
        for k in range(8):
            P.op("pe", lambda e, k=k: e.transpose(pb[:, k * 128:(k + 1) * 128], xn[:, k * 128:(k + 1) * 128], identb), reads=[xnb, bconst], writes=[bankb[tr_bank]])
        for k in range(8):
            P.op("act", lambda e, k=k: e.activation(out=hT_dst[1][:, k, :], in_=pb[:, k * 128:(k + 1) * 128], func=AF.Identity,
                                                    scale=scp[:, j * 8 + k:j * 8 + k + 1], bias=shp[:, j * 8 + k:j * 8 + k + 1]),
                 reads=[bankb[tr_bank], bconst], writes=[hT_dst[0]])

    def phaseA(j, xsrc, S, own, ropetab):
        TB = min(1024, S)
        nt = TB // 128
        xr = Rot([ar.f32(1024) for _ in range(2)], "x")
        xnr = Rot([ar.bf(1024) for _ in range(2)], "xn")
        str_ = Rot([(ar.f32(12), ar.f32(2), ar.f32(1)) for _ in range(2)], "st")
        hTr = Rot([ar.bf(8 * TB) for _ in range(2)], "hT")
        wr = Rot([ar.bf(8 * 512) for _ in range(3)], "w")
        stq = Rot([ar.bf(4 * TB) for _ in range(2)], "stq")
        stt = Rot([ar.bf(512) for _ in range(4)], "stt")
        rpb = Rot([ar.bf(512) for _ in range(2)], "rpb")
        rtm = Rot([ar.f32(4 * 64) for _ in range(2)], "rtm")
        ropr = Rot([ar.f32(nt * 16) for _ in range(2)], "rop")
        gst = Rot([ar.f32(4 * TB) for _ in range(2)], "gst")
        pbank = [2, 3, 4, 5]
        pbi = [0]
        tbk = [6, 7]
        tbi = [0]
        groups = []
        if own:
            names = ["aq", "ak", "av", "mq", "mk", "mv", "mo", "mg", "bg"]
        else:
            names = ["ak", "av", "mk", "mv", "mg"]
        for nm in names:
            lo, hi = COLS[nm]
            ng = max(1, (hi - lo) // 512)
            for g in range(ng):
                groups.append((nm, g, lo + g * 512, min(512, hi - lo)))
        Kd = KT[j]
        Vd = VV[j]
        for blk in range((S + TB - 1) // TB):
            t0 = blk * TB
            tb = min(TB, S - t0)
            nt = tb // 128
            HB = min(512, tb)
            assert tb % HB == 0
            hT, hTb = hTr.next()
            hTv = hT[:, 0:8 * tb].rearrange("p (k t) -> p k t", k=8)
            rop, ropb = ropr.next()
            ropv = rop[:, 0:nt * 16].rearrange("p (t c) -> p t c", c=16)
            P.dma("sp", ropv, ropetab[t0:t0 + tb, :].rearrange("(t p) c -> p t c", p=128), writes=[ropb])
            for ti in range(nt):
                xt, xb_ = xr.next()
                xn, xnb = xnr.next()
                st, _ = str_.next()
                P.dma("sp", xt, xsrc[t0 + ti * 128:t0 + (ti + 1) * 128, :], writes=[xb_])
                ln_to_hT(xt, xb_, (hTb, hTv[:, :, ti * 128:(ti + 1) * 128]), sc1p1, sh1, j, ti % 2, [], st, xn, xnb)
            for (nm, g, c0, ncol) in groups:
                wt, wb = wr.next()
                wv = wt.rearrange("p (k n) -> p k n", k=8)
                P.dma("sp", wv[:, :, 0:ncol], wb_in[:, :, c0:c0 + ncol], reads=[DB("wb_in")], writes=[wb])
                if nm in ("aq", "ak", "av", "mk", "mv", "mo", "bg"):
                    if nm in ("aq", "ak"):
                        sq, sqb = stq.next()
                        sqv = sq[:, 0:4 * tb].rearrange("p (h t) -> p h t", h=4)
                    for ti in range(nt):
                        pb_i = pbank[pbi[0] % 4]
                        pbi[0] += 1
                        for k in range(8):
                            P.op("pe", lambda e, k=k, ti=ti, pb_i=pb_i, wv=wv, hTv=hTv: e.matmul(bank(pb_i), lhsT=hTv[:, k, ti * 128:(ti + 1) * 128], rhs=wv[:, k, :], start=(k == 0), stop=(k == 7)),
                                 reads=[hTb, wb], writes=[bankb[pb_i]])
                        tok = t0 + ti * 128
                        if nm in ("aq", "ak"):
                            rb_, rbb = rpb.next()
                            tm, tmb = rtm.next()
                            psv = bank(pb_i).rearrange("p (h d) -> p h d", h=8)
                            rv = rb_.rearrange("p (h d) -> p h d", h=8)
                            tmv = tm.rearrange("p (a h d) -> p a h d", a=4, h=8)
                            cosb = ropv[:, ti, 0:8].unsqueeze(1).to_broadcast([128, 8, 8])
                            sinb = ropv[:, ti, 8:16].unsqueeze(1).to_broadcast([128, 8, 8])
                            rd = [bankb[pb_i], ropb]
                            P.op("dve", lambda e, psv=psv, tmv=tmv, cosb=cosb: e.tensor_tensor(out=tmv[:, 0], in0=psv[:, :, 0:8], in1=cosb, op=ALU.mult), reads=rd, writes=[tmb])
                            P.op("dve", lambda e, psv=psv, tmv=tmv, sinb=sinb: e.tensor_tensor(out=tmv[:, 1], in0=psv[:, :, 8:16], in1=sinb, op=ALU.mult), reads=rd, writes=[tmb])
                            P.op("dve", lambda e, psv=psv, tmv=tmv, cosb=cosb: e.tensor_tensor(out=tmv[:, 2], in0=psv[:, :, 8:16], in1=cosb, op=ALU.mult), reads=rd, writes=[tmb])
                            P.op("dve", lambda e, psv=psv, tmv=tmv, sinb=sinb: e.tensor_tensor(out=tmv[:, 3], in0=psv[:, :, 0:8], in1=sinb, op=ALU.mult), reads=rd, writes=[tmb])
                            P.op("dve", lambda e, rv=rv, psv=psv: e.tensor_copy(out=rv[:, :, 16:64], in_=psv[:, :, 16:64]), reads=rd, writes=[rbb])
                            P.op("dve", lambda e, rv=rv, tmv=tmv: e.tensor_tensor(out=rv[:, :, 0:8], in0=tmv[:, 0], in1=tmv[:, 1], op=ALU.subtract), reads=[tmb], writes=[rbb])
                            P.op("dve", lambda e, rv=rv, tmv=tmv: e.tensor_tensor(out=rv[:, :, 8:16], in0=tmv[:, 2], in1=tmv[:, 3], op=ALU.add), reads=[tmb], writes=[rbb])
                            tb_i = tbk[tbi[0] % 2]
                            tbi[0] += 1
                            tpb = bankbf(tb_i)
                            for hh in range(4):
                                P.op("pe", lambda e, hh=hh, tpb=tpb, rb_=rb_: e.transpose(tpb[:, hh * 128:(hh + 1) * 128], rb_[:, hh * 128:(hh + 1) * 128], identb),
                                     reads=[rbb, bconst], writes=[bankb[tb_i]])
                            P.op("act", lambda e, sqv=sqv, tpb=tpb, ti=ti: e.activation(out=sqv[:, :, ti * 128:(ti + 1) * 128], in_=tpb[:, 0:512].rearrange("p (h t) -> p h t", h=4), func=AF.Copy),
                                 reads=[bankb[tb_i]], writes=[sqb])
                        else:
                            s_, sb_ = stt.next()
                            if nm in ("mo", "bg"):
                                P.op("act", lambda e, s_=s_, pb_i=pb_i: e.activation(out=s_, in_=bank(pb_i), func=AF.Sigmoid), reads=[bankb[pb_i]], writes=[sb_])
                            else:
                                P.op("act", lambda e, s_=s_, pb_i=pb_i: e.activation(out=s_, in_=bank(pb_i), func=AF.Copy), reads=[bankb[pb_i]], writes=[sb_])
                            if nm == "av":
                                P.dma("sp", Vd[4 * g:4 * g + 4, tok:tok + 128, :].rearrange("h t d -> t h d"), s_.rearrange("p (h d) -> p h d", h=4), reads=[sb_], writes=[DB("VV%d" % j)])
                            elif nm == "mv":
                                dd = mvt[j] if own else mvtc
                                P.dma("sp", dd[tok:tok + 128, g * 512:(g + 1) * 512], s_, reads=[sb_], writes=[DB("mvt%d%d" % (j, own))])
                            elif nm == "mk":
                                dd = mkt[j] if own else mktc
                                P.dma("sp", dd[tok:tok + 128, :], s_, reads=[sb_], writes=[DB("mkt%d%d" % (j, own))])
                            elif nm == "mo":
                                P.dma("sp", smo[j][tok:tok + 128, g * 512:(g + 1) * 512], s_, reads=[sb_], writes=[DB("smo%d" % j)])
                            elif nm == "bg":
                                P.dma("sp", sbg[j][tok:tok + 128, g * 512:(g + 1) * 512], s_, reads=[sb_], writes=[DB("sbg%d" % j)])
                    if nm == "aq":
                        P.dma("sp", QT[j][4 * g:4 * g + 4, :, t0:t0 + tb].rearrange("h p t -> p h t"), sqv, reads=[sqb], writes=[DB("QT%d" % j)])
                    elif nm == "ak":
                        P.dma("sp", Kd[4 * g:4 * g + 4, :, t0:t0 + tb].rearrange("h p t -> p h t"), sqv, reads=[sqb], writes=[DB("KT%d" % j)])
                if (nm == "mq") or (nm == "mk" and own):
                    sq, sqb = stq.next()
                    sqv = sq[:, 0:4 * tb].rearrange("p (h t) -> p h t", h=4)
                    for hd in range(4):
                        for hf in range(tb // HB):
                            pb_i = pbank[pbi[0] % 4]
                            pbi[0] += 1
                            for k in range(8):
                                P.op("pe", lambda e, HB=HB, k=k, hd=hd, hf=hf, pb_i=pb_i, wv=wv, hTv=hTv: e.matmul(bank(pb_i, HB), lhsT=wv[:, k, hd * 128:(hd + 1) * 128], rhs=hTv[:, k, hf * HB:(hf + 1) * HB], start=(k == 0), stop=(k == 7)),
                                     reads=[hTb, wb], writes=[bankb[pb_i]])
                            P.op("dve", lambda e, HB=HB, hd=hd, hf=hf, pb_i=pb_i, sqv=sqv: e.tensor_copy(out=sqv[:, hd, hf * HB:(hf + 1) * HB], in_=bank(pb_i, HB)), reads=[bankb[pb_i]], writes=[sqb])
                    dd = mqT[j] if nm == "mq" else mkT[j]
                    P.dma("sp", dd[:, :, t0:t0 + tb].rearrange("h p t -> p h t"), sqv, reads=[sqb], writes=[DB(("mqT%d" if nm == "mq" else "mkT%d") % j)])
                if nm == "mg":
                    gs, gsb = gst.next()
                    gsv = gs[:, 0:4 * tb].rearrange("p (r t) -> p r t", r=4)
                    for r in range(4):
                        for hf in range(tb // HB):
                            pb_i = pbank[pbi[0] % 4]
                            pbi[0] += 1
                            for k in range(8):
                                P.op("pe", lambda e, HB=HB, k=k, r=r, hf=hf, pb_i=pb_i, wv=wv, hTv=hTv: e.matmul(bank(pb_i, HB)[0:4, :], lhsT=wv[:, k, 4 * r:4 * r + 4], rhs=hTv[:, k, hf * HB:(hf + 1) * HB], start=(k == 0), stop=(k == 7)),
                                     reads=[hTb, wb], writes=[bankb[pb_i]])
                            P.op("act", lambda e, HB=HB, r=r, hf=hf, pb_i=pb_i, gsv=gsv: e.activation(out=gsv[0:4, r, hf * HB:(hf + 1) * HB], in_=bank(pb_i, HB)[0:4, :], func=AF.Identity, bias=bmg[:, r:r + 1]),
                                 reads=[bankb[pb_i], bconst], writes=[gsb])
                    dd = GG[j] if own else GGc
                    P.dma("sp", dd[:, t0:t0 + tb].rearrange("(r h) t -> h r t", h=4), gsv[0:4], reads=[gsb], writes=[DB("GG%d%d" % (j, own))])

    def phaseB(j):
        Sk = Sctx[j]
        nkt = Sk // 128
        Sq = SO[j]
        QBM = min(512, Sq)
        ktr = Rot([ar.bf(Sk) for _ in range(2)], "kt")
        vr = Rot([ar.bf(Sk) for _ in range(2)], "v")
        qr = Rot([ar.bf(QBM) for _ in range(2)], "q")
        ptr = Rot([ar.bf(2 * QBM) for _ in range(3)], "pt")
        tf0 = [ar.f32(QBM) for _ in range(5)]
        tfb = Buf("tfB")
        yst = Rot([ar.bf(QBM) for _ in range(2)], "yst")
        accs = [ar.f32(2 * QBM) for _ in range(2)]
        accb = [Buf("accA"), Buf("accB")]
        for h in range(HA):
            kt_, ktb = ktr.next()
            v_, vb = vr.next()
            P.dma("sp", kt_, KT[j][h], reads=[DB("KT%d" % j)], writes=[ktb])
            vv = v_.rearrange("p (t d) -> p t d", d=128)
            P.dma("sp", vv, VV[j][h].rearrange("(t p) d -> p t d", p=128), reads=[DB("VV%d" % j)], writes=[vb])
            for qb in range((Sq + QBM - 1) // QBM):
                q0 = qb * QBM
                QB = min(QBM, Sq - q0)
                q_, qb_ = qr.next()
                q_ = q_[:, 0:QB]
                P.dma("sp", q_, QT[j][h, :, q0:q0 + QB], reads=[DB("QT%d" % j)], writes=[qb_])

                def qk(u):
                    sb = u % 2
                    for m in range(2):
                        P.op("pe", lambda e, m=m, sb=sb, u=u, kt_=kt_, q_=q_, QB=QB: e.matmul(bank(2 * sb + m, QB), lhsT=kt_[64 * m:64 * m + 64, u * 128:(u + 1) * 128], rhs=q_[64 * m:64 * m + 64, :], start=True, stop=True),
                             reads=[ktb, qb_], writes=[bankb[2 * sb + m]])
                pts = {}

                def ex(u):
                    sb = u % 2
                    pt, ptb = ptr.next()
                    pt = pt[:, 0:2 * QB]
                    pts[u] = (pt, ptb)
                    ptv = pt.rearrange("p (m q) -> p m q", m=2)
                    src = PS[:, 2 * sb * 512:(2 * sb + 2) * 512].rearrange("p (m q) -> p m q", m=2)[:, :, 0:QB]
                    P.op("act", lambda e, ptv=ptv, src=src: e.activation(out=ptv, in_=src, func=AF.Exp, scale=0.125), reads=[bankb[2 * sb], bankb[2 * sb + 1]], writes=[ptb])

                dve_units = [u for u in range(nkt) if u % 5 != 0]
                pe_units = [u for u in range(nkt) if u % 5 == 0]
                dcount = [0]

                def pv(u):
                    pt, ptb = pts.pop(u)
                    for m in range(2):
                        P.op("pe", lambda e, m=m, u=u, pt=pt, vv=vv, QB=QB: e.matmul(bank(4 + m, QB), lhsT=vv[:, u, :], rhs=pt[:, m * QB:(m + 1) * QB], start=(u == 0), stop=(u == nkt - 1)),
                             reads=[vb, ptb], writes=[bankb[4 + m]])
                    if u % 5 == 0:
                        for m in range(2):
                            P.op("pe", lambda e, m=m, u=u, pt=pt, QB=QB: e.matmul(bank(6 + m, QB), lhsT=onesb, rhs=pt[:, m * QB:(m + 1) * QB], start=(u == 0), stop=(u == pe_units[-1] and not dve_units)),
                                 reads=[bconst, ptb], writes=[bankb[6 + m]])
                    else:
                        ai = dcount[0] % 2
                        first = dcount[0] < 2
                        dcount[0] += 1
                        acc = accs[ai][:, 0:2 * QB]
                        if first:
                            P.op("dve", lambda e, acc=acc, pt=pt: e.tensor_copy(out=acc, in_=pt), reads=[ptb], writes=[accb[ai]])
                        else:
                            P.op("dve", lambda e, acc=acc, pt=pt: e.tensor_tensor(out=acc, in0=pt, in1=acc, op=ALU.add), reads=[ptb, accb[ai]], writes=[accb[ai]])
                qk(0)
                for u in range(nkt):
                    ex(u)
                    if u + 1 < nkt:
                        qk(u + 1)
                    pv(u)
                nacc = min(2, len(dve_units))
                for ai in range(nacc):
                    acc = accs[ai][:, 0:2 * QB]
                    for m in range(2):
                        P.op("pe", lambda e, m=m, acc=acc, QB=QB, ai=ai: e.matmul(bank(6 + m, QB), lhsT=onesf, rhs=acc[:, m * QB:(m + 1) * QB], start=False, stop=(ai == nacc - 1)),
                             reads=[bconst, accb[ai]], writes=[bankb[6 + m]])
                r0, r1, a0, a1, sq = [t[:, 0:QB] for t in tf0]
                P.op("dve", lambda e, r0=r0, r1=r1, a0=a0, a1=a1, sq=sq, QB=QB: e.reciprocal(out=r0, in_=bank(6, QB)), reads=[bankb[6]], writes=[tfb])
                P.op("dve", lambda e, r0=r0, r1=r1, a0=a0, a1=a1, sq=sq, QB=QB: e.reciprocal(out=r1, in_=bank(7, QB)), reads=[bankb[7]], writes=[tfb])
                P.op("dve", lambda e, r0=r0, r1=r1, a0=a0, a1=a1, sq=sq, QB=QB: e.tensor_tensor(out=a0, in0=bank(4, QB), in1=r0, op=ALU.mult), reads=[bankb[4], tfb], writes=[tfb])
                P.op("dve", lambda e, r0=r0, r1=r1, a0=a0, a1=a1, sq=sq, QB=QB: e.tensor_tensor(out=a1, in0=bank(5, QB), in1=r1, op=ALU.mult), reads=[bankb[5], tfb], writes=[tfb])
                P.op("dve", lambda e, r0=r0, r1=r1, a0=a0, a1=a1, sq=sq, QB=QB: e.scalar_tensor_tensor(out=a0, in0=a1, scalar=nlam, in1=a0, op0=ALU.mult, op1=ALU.add), reads=[tfb, bconst], writes=[tfb])
                P.op("dve", lambda e, r0=r0, r1=r1, a0=a0, a1=a1, sq=sq, QB=QB: e.tensor_tensor(out=sq, in0=a0, in1=a0, op=ALU.mult), reads=[tfb], writes=[tfb])
                P.op("pe", lambda e, r0=r0, r1=r1, a0=a0, a1=a1, sq=sq, QB=QB: e.matmul(bank(6, QB), lhsT=onesf, rhs=sq, start=True, stop=True), reads=[tfb, bconst], writes=[bankb[6]])
                P.op("act", lambda e, r0=r0, r1=r1, a0=a0, a1=a1, sq=sq, QB=QB: e.activation(out=r0, in_=bank(6, QB), func=AF.Sqrt, scale=1.0 / 128.0, bias=EPS), reads=[bankb[6]], writes=[tfb])
                P.op("dve", lambda e, r0=r0, r1=r1, a0=a0, a1=a1, sq=sq, QB=QB: e.reciprocal(out=r0, in_=r0), reads=[tfb], writes=[tfb])
                ys, ysb = yst.next()
                ys = ys[:, 0:QB]
                P.op("dve", lambda e, ys=ys, a0=a0, r0=r0: e.scalar_tensor_tensor(out=ys, in0=a0, scalar=gsc, in1=r0, op0=ALU.mult, op1=ALU.mult), reads=[tfb, bconst], writes=[ysb])
                P.dma("sp", yaT[j][h, :, q0:q0 + QB], ys, reads=[ysb], writes=[DB("yaT%d" % j)])

    def phaseC(j):
        is_p = (j == 2)
        So = SO[j]
        nch = So // 128
        UG = ar.f32(nch * 16)
        UGv = UG.rearrange("p (c q) -> p c q", q=16)
        DECB = ar.f32(8 * nch)
        Cf = [ar.f32(257) for _ in range(8)]
        Cb = [ar.bf(257) for _ in range(8)]
        ini = ar.f32(16)
        Minit = [ini[0:4, 0:1], ini[0:4, 1:2]]
        Binit = [ini[0:4, 2:3], ini[0:4, 3:4]]
        bC = Buf("Cstate")
        bini = Buf("ini")
        bUG = Buf("UG")
        P.op("dve", lambda e: e.memset(ini, 0.0), writes=[bini])
        m0 = ar.mark()
        def c2():
            PC = min(2048, Sc)
            npc = Sc // PC
            It = ar.f32(PC)
            Ft = ar.f32(PC)
            Pt = ar.f32(PC)
            Mt = ar.f32(PC)
            T1 = ar.f32(PC)
            T2 = ar.f32(PC)
            pm = ar.f32(2 * npc)
            bs = ar.f32(2 * npc)
            carry = ar.f32(4)
            tot = ar.f32(4)
            bg_ = Buf("c2g")
            bsm = Buf("c2s")
            P.op("dve", lambda e: e.memset(carry, 0.0), writes=[bsm])
            for d in range(2):
                msk = mb_in if d == 0 else ma_in
                for pc in range(npc):
                    sl = slice(pc * PC, (pc + 1) * PC)
                    P.dma("sp", It[0:4, :], GGc[d * 8:d * 8 + 4, sl], reads=[DB("GG20")], writes=[bg_])
                    P.dma("sp", Ft[0:4, :], GGc[d * 8 + 4:d * 8 + 8, sl], reads=[DB("GG20")], writes=[bg_])
                    P.dma("sp", Mt[0:4, :], msk[:, sl], writes=[bg_])
                    P.op("act", lambda e: e.activation(out=Ft[0:4, :], in_=Ft[0:4, :], func=AF.Exp, scale=-1.0), reads=[bg_], writes=[bg_])
                    P.op("act", lambda e: e.activation(out=Ft[0:4, :], in_=Ft[0:4, :], func=AF.Ln, bias=1.0), reads=[bg_], writes=[bg_])
                    init = 0.0 if pc == 0 else carry[0:4, d:d + 1]
                    P.op("dve", lambda e, init=init: e.tensor_tensor_scan(out=Pt[0:4, :], data0=onesf[0:4, 0:1].to_broadcast([4, PC]), data1=Ft[0:4, :], initial=init, op0=ALU.mult, op1=ALU.add),
                         reads=[bg_, bconst, bsm], writes=[bg_])
                    P.op("dve", lambda e, d=d: e.tensor_copy(out=carry[0:4, d:d + 1], in_=Pt[0:4, PC - 1:PC]), reads=[bg_], writes=[bsm])
                    P.op("dve", lambda e: e.tensor_tensor(out=T1[0:4, :], in0=Ft[0:4, :], in1=Mt[0:4, :], op=ALU.mult), reads=[bg_], writes=[bg_])
                    P.op("dve", lambda e, d=d, pc=pc: e.tensor_reduce(out=bs[0:4, d * npc + pc:d * npc + pc + 1], in_=T1[0:4, :], axis=AX.X, op=ALU.add), reads=[bg_], writes=[bsm])
                    if d == 0:
                        P.op("dve", lambda e: e.tensor_tensor(out=It[0:4, :], in0=It[0:4, :], in1=Pt[0:4, :], op=ALU.add), reads=[bg_], writes=[bg_])
                    else:
                        P.op("dve", lambda e: e.tensor_tensor(out=It[0:4, :], in0=It[0:4, :], in1=Ft[0:4, :], op=ALU.add), reads=[bg_], writes=[bg_])
                        P.op("dve", lambda e: e.tensor_tensor(out=It[0:4, :], in0=It[0:4, :], in1=Pt[0:4, :], op=ALU.subtract), reads=[bg_], writes=[bg_])
                    P.op("dve", lambda e: e.tensor_tensor(out=It[0:4, :], in0=It[0:4, :], in1=Mt[0:4, :], op=ALU.mult), reads=[bg_], writes=[bg_])
                    P.op("dve", lambda e: e.tensor_scalar(out=T2[0:4, :], in0=Mt[0:4, :], scalar1=-NEG, scalar2=NEG, op0=ALU.mult, op1=ALU.add), reads=[bg_], writes=[bg_])
                    P.op("dve", lambda e: e.tensor_tensor(out=It[0:4, :], in0=It[0:4, :], in1=T2[0:4, :], op=ALU.add), reads=[bg_], writes=[bg_])
                    P.op("dve", lambda e, d=d, pc=pc: e.tensor_reduce(out=pm[0:4, d * npc + pc:d * npc + pc + 1], in_=It[0:4, :], axis=AX.X, op=ALU.max), reads=[bg_], writes=[bsm])
                    P.dma("sp", AMs[d, :, sl], It[0:4, :], reads=[bg_], writes=[DB("AMs")])
            bias2 = ar.f32(4)
            for d in range(2):
                P.op("dve", lambda e, d=d: e.tensor_reduce(out=Binit[d], in_=bs[0:4, d * npc:(d + 1) * npc], axis=AX.X, op=ALU.add), reads=[bsm], writes=[bini])
                P.op("dve", lambda e, d=d: e.tensor_reduce(out=Minit[d], in_=pm[0:4, d * npc:(d + 1) * npc], axis=AX.X, op=ALU.max), reads=[bsm], writes=[bini])
            P.op("dve", lambda e: e.tensor_tensor(out=Minit[1], in0=Minit[1], in1=carry[0:4, 1:2], op=ALU.add), reads=[bini, bsm], writes=[bini])
            for d in range(2):
                P.op("dve", lambda e, d=d: e.tensor_scalar(out=Minit[d], in0=Minit[d], scalar1=0.0, scalar2=None, op0=ALU.max), reads=[bini], writes=[bini])
            P.op("dve", lambda e: e.tensor_scalar(out=bias2[0:4, 0:1], in0=Minit[0], scalar1=-1.0, scalar2=LNSC, op0=ALU.mult, op1=ALU.add), reads=[bini], writes=[bsm])
            P.op("dve", lambda e: e.tensor_tensor(out=bias2[0:4, 1:2], in0=carry[0:4, 1:2], in1=Minit[1], op=ALU.subtract), reads=[bini, bsm], writes=[bsm])
            P.op("dve", lambda e: e.tensor_scalar(out=bias2[0:4, 1:2], in0=bias2[0:4, 1:2], scalar1=LNSC, scalar2=None, op0=ALU.add), reads=[bsm], writes=[bsm])
            wtr = Rot([ar.f32(4) for _ in range(3)], "wt")
            kr = Rot([ar.bf(512) for _ in range(3)], "kc")
            var_ = [ar.bf(4 * 257) for _ in range(3)]
            vbufs = [Buf("vc%d" % i) for i in range(3)]
            for t, vb_ in zip(var_, vbufs):
                P.op("dve", lambda e, t=t: e.memset(t, 1.0), writes=[vb_])
            kur = Rot([ar.bf(128) for _ in range(4)], "ku")
            w0 = ar.f32(PC)
            bw = Buf("wrow")
            for d in range(2):
                for pc in range(npc):
                    sl = slice(pc * PC, (pc + 1) * PC)
                    P.dma("sp", w0[0:4, :], AMs[d, :, sl], reads=[DB("AMs")], writes=[bw])
                    P.op("act", lambda e, d=d: e.activation(out=w0[0:4, :], in_=w0[0:4, :], func=AF.Exp, bias=bias2[0:4, d:d + 1]), reads=[bw, bsm], writes=[bw])
                    for cl in range(PC // 128):
                        cc = pc * (PC // 128) + cl
                        wt_, wtb = wtr.next()
                        P.op("pe", lambda e, cl=cl: e.transpose(PS[:, 7 * 512:7 * 512 + 4], w0[0:4, cl * 128:(cl + 1) * 128], identf[0:4, 0:4]), reads=[bw, bconst], writes=[bankb[7]])
                        P.op("dve", lambda e, wt_=wt_: e.tensor_copy(out=wt_, in_=PS[:, 7 * 512:7 * 512 + 4]), reads=[bankb[7]], writes=[wtb])
                        k_, kb_ = kr.next()
                        P.dma("sp", k_, mktc[cc * 128:(cc + 1) * 128, :], reads=[DB("mkt20")], writes=[kb_])
                        vi = cc % 3
                        va = var_[vi].rearrange("p (h f) -> p h f", h=4)
                        P.dma("sp", va[:, :, 0:256], mvtc[cc * 128:(cc + 1) * 128, :].rearrange("p (h f) -> p h f", h=4), reads=[DB("mvt20")], writes=[vbufs[vi]])
                        for hd in range(4):
                            ku, kub = kur.next()
                            P.op("dve", lambda e, ku=ku, k_=k_, hd=hd, wt_=wt_: e.tensor_scalar(out=ku, in0=k_[:, hd * 128:(hd + 1) * 128], scalar1=wt_[:, hd:hd + 1], scalar2=None, op0=ALU.mult),
                                 reads=[kb_, wtb], writes=[kub])
                            P.op("pe", lambda e, ku=ku, va=va, hd=hd, cc=cc: e.matmul(bank(hd, 257), lhsT=ku, rhs=va[:, hd, :], start=(cc == 0), stop=(cc == nchc - 1)),
                                 reads=[kub, vbufs[vi]], writes=[bankb[hd]])
                for hd in range(4):
                    P.op("dve", lambda e, hd=hd, d=d: e.tensor_copy(out=Cf[d * 4 + hd], in_=bank(hd, 257)), reads=[bankb[hd]], writes=[bC])
        if is_p:
            c2()
            barrier_local(m0)
        else:
            for bi in range(8):
                P.op("dve", lambda e, bi=bi: e.memset(Cf[bi], 0.0), writes=[bC])
        It = ar.f32(So)
        Ft = ar.f32(So)
        Pt = ar.f32(So)
        cm = ar.f32(nch)
        Me = ar.f32(nch)
        Me2 = ar.f32(nch)
        Mp = ar.f32(nch)
        dec = [ar.f32(nch), ar.f32(nch)]
        tt = ar.f32(4)
        bg_ = Buf("c1g")
        bsm = Buf("c1s")
        bdec = Buf("dec")
        for d in range(2):
            P.dma("sp", It[0:4, :], GG[j][d * 8:d * 8 + 4, :], reads=[DB("GG%d1" % j)], writes=[bg_])
            P.dma("sp", Ft[0:4, :], GG[j][d * 8 + 4:d * 8 + 8, :], reads=[DB("GG%d1" % j)], writes=[bg_])
            P.op("act", lambda e: e.activation(out=Ft[0:4, :], in_=Ft[0:4, :], func=AF.Exp, scale=-1.0), reads=[bg_], writes=[bg_])
            P.op("act", lambda e: e.activation(out=Ft[0:4, :], in_=Ft[0:4, :], func=AF.Ln, bias=1.0), reads=[bg_], writes=[bg_])
            P.op("dve", lambda e: e.tensor_tensor_scan(out=Pt[0:4, :], data0=onesf[0:4, 0:1].to_broadcast([4, So]), data1=Ft[0:4, :], initial=0.0, op0=ALU.mult, op1=ALU.add),
                 reads=[bg_, bconst], writes=[bg_])
            if d == 0:
                P.op("dve", lambda e: e.tensor_scalar(out=Pt[0:4, :], in0=Pt[0:4, :], scalar1=Binit[0], scalar2=None, op0=ALU.add), reads=[bg_, bini], writes=[bg_])
            else:
                P.op("dve", lambda e: e.tensor_tensor(out=tt[0:4, 0:1], in0=Pt[0:4, So - 1:So], in1=Binit[1], op=ALU.add), reads=[bg_, bini], writes=[bsm])
                P.op("dve", lambda e: e.tensor_tensor(out=Pt[0:4, :], in0=Ft[0:4, :], in1=Pt[0:4, :], op=ALU.subtract), reads=[bg_, bsm], writes=[bg_])
                P.op("dve", lambda e: e.tensor_scalar(out=Pt[0:4, :], in0=Pt[0:4, :], scalar1=tt[0:4, 0:1], scalar2=None, op0=ALU.add), reads=[bg_, bsm], writes=[bg_])
            P.op("dve", lambda e: e.tensor_tensor(out=It[0:4, :], in0=It[0:4, :], in1=Pt[0:4, :], op=ALU.add), reads=[bg_], writes=[bg_])
            P.op("dve", lambda e: e.tensor_reduce(out=cm[0:4, :], in_=It[0:4, :].rearrange("p (c t) -> p c t", t=128), axis=AX.X, op=ALU.max), reads=[bg_], writes=[bsm])
            if d == 0:
                P.op("dve", lambda e: e.tensor_tensor_scan(out=Me[0:4, :], data0=cm[0:4, :], data1=cm[0:4, :], initial=Minit[0], op0=ALU.max, op1=ALU.max), reads=[bsm, bini], writes=[bsm])
                if nch > 1:
                    P.op("dve", lambda e: e.tensor_copy(out=Mp[0:4, 1:nch], in_=Me[0:4, 0:nch - 1]), reads=[bsm], writes=[bsm])
                P.op("dve", lambda e: e.tensor_copy(out=Mp[0:4, 0:1], in_=Minit[0]), reads=[bsm, bini], writes=[bsm])
            else:
                src, dst_ = cm, Me2
                sh = 1
                while sh < nch:
                    P.op("dve", lambda e, src=src, dst_=dst_, sh=sh: e.tensor_tensor(out=dst_[0:4, 0:nch - sh], in0=src[0:4, 0:nch - sh], in1=src[0:4, sh:nch], op=ALU.max), reads=[bsm], writes=[bsm])
                    P.op("dve", lambda e, src=src, dst_=dst_, sh=sh: e.tensor_copy(out=dst_[0:4, nch - sh:nch], in_=src[0:4, nch - sh:nch]), reads=[bsm], writes=[bsm])
                    src, dst_ = dst_, src
                    sh *= 2
                P.op("dve", lambda e, src=src: e.tensor_scalar(out=Me[0:4, :], in0=src[0:4, :], scalar1=Minit[1], scalar2=None, op0=ALU.max), reads=[bsm, bini], writes=[bsm])
                if nch > 1:
                    P.op("dve", lambda e: e.tensor_copy(out=Mp[0:4, 0:nch - 1], in_=Me[0:4, 1:nch]), reads=[bsm], writes=[bsm])
                P.op("dve", lambda e: e.tensor_copy(out=Mp[0:4, nch - 1:nch], in_=Minit[1]), reads=[bsm, bini], writes=[bsm])
            P.op("dve", lambda e: e.tensor_tensor(out=Mp[0:4, :], in0=Mp[0:4, :], in1=Me[0:4, :], op=ALU.subtract), reads=[bsm], writes=[bsm])
            P.op("act", lambda e, d=d: e.activation(out=dec[d][0:4, :], in_=Mp[0:4, :], func=AF.Exp), reads=[bsm], writes=[bdec])
            Meb = Me[0:4, :].unsqueeze(2).to_broadcast([4, nch, 128])
            P.op("dve", lambda e, Meb=Meb: e.tensor_tensor(out=Pt[0:4, :].rearrange("p (c t) -> p c t", t=128), in0=Pt[0:4, :].rearrange("p (c t) -> p c t", t=128), in1=Meb, op=ALU.subtract), reads=[bg_, bsm], writes=[bg_])
            P.op("dve", lambda e, Meb=Meb: e.tensor_tensor(out=It[0:4, :].rearrange("p (c t) -> p c t", t=128), in0=It[0:4, :].rearrange("p (c t) -> p c t", t=128), in1=Meb, op=ALU.subtract), reads=[bg_, bsm], writes=[bg_])
            P.op("act", lambda e: e.activation(out=Ft[0:4, :], in_=Pt[0:4, :], func=AF.Exp), reads=[bg_], writes=[bg_])
            P.op("act", lambda e: e.activation(out=It[0:4, :], in_=It[0:4, :], func=AF.Exp, bias=LNSC), reads=[bg_], writes=[bg_])
            for c in range(nch):
                P.op("pe", lambda e, c=c, d=d: e.transpose(PS[:, d * 512 + c * 8:d * 512 + c * 8 + 4], It[0:4, c * 128:(c + 1) * 128], identf[0:4, 0:4]), reads=[bg_, bconst], writes=[bankb[d]])
                P.op("pe", lambda e, c=c, d=d: e.transpose(PS[:, d * 512 + c * 8 + 4:d * 512 + c * 8 + 8], Ft[0:4, c * 128:(c + 1) * 128], identf[0:4, 0:4]), reads=[bg_, bconst], writes=[bankb[d]])
            for hd in range(4):
                P.op("pe", lambda e, hd=hd, d=d: e.matmul(PS[:, 1024 + (d * 4 + hd) * nch:1024 + (d * 4 + hd + 1) * nch], lhsT=self_[0:4, hd * 128:(hd + 1) * 128], rhs=dec[d][0:4, :], start=True, stop=True),
                     reads=[bdec, bconst], writes=[bankb[2]])
        assert nch * 8 <= 512
        for d in range(2):
            P.op("dve", lambda e, d=d: e.tensor_copy(out=UGv[:, :, d * 8:(d + 1) * 8], in_=PS[:, d * 512:d * 512 + nch * 8].rearrange("p (c q) -> p c q", q=8)), reads=[bankb[d]], writes=[bUG])
        P.op("dve", lambda e: e.tensor_copy(out=DECB, in_=PS[:, 1024:1024 + 8 * nch]), reads=[bankb[2]], writes=[bUG])
        barrier_local(m0)
        qTt = ar.bf(So)
        kTt = ar.bf(So)
        ktk = ar.bf(So)
        vau = ar.bf(nch * 257)
        hsum = ar.f32(nch * 256)
        smt = ar.bf(nch * 256)
        yms = ar.bf(2 * So)
        mnb = ar.f32(1024)
        bh = Buf("headdata")
        bhs = Buf("hsum")
        bym = Buf("yms")
        bmn = Buf("mnb")
        P.dma("sp", mnb, mnorm.partition_broadcast(128), writes=[bmn])
        vav = vau.rearrange("p (c f) -> p c f", f=257)
        ktkv = ktk.rearrange("p (c f) -> p c f", f=128)
        hsv = hsum.rearrange("p (c f) -> p c f", f=256)
        smv = smt.rearrange("p (c f) -> p c f", f=256)
        ymv = yms.rearrange("p (i t) -> p i t", i=2)
        ptr = Rot([ar.bf(128) for _ in range(4)], "ptm")
        kur = Rot([ar.bf(128) for _ in range(4)], "kum")
        dr = Rot([ar.f32(2) for _ in range(4)], "den")
        fin = Rot([(ar.f32(6), ar.f32(2), ar.f32(1), ar.f32(256), ar.bf(256)) for _ in range(2)], "fin")
        for hd in range(HM):
            P.dma("sp", qTt, mqT[j][hd], reads=[DB("mqT%d" % j)], writes=[bh])
            P.dma("sp", kTt, mkT[j][hd], reads=[DB("mkT%d" % j)], writes=[bh])
            P.dma("sp", ktkv, mkt[j][:, hd * 128:(hd + 1) * 128].rearrange("(c p) f -> p c f", p=128), reads=[DB("mkt%d1" % j)], writes=[bh])
            P.op("dve", lambda e: e.memset(vau, 1.0), reads=[], writes=[bh])
            P.dma("sp", vav[:, :, 0:256], mvt[j][:, hd * 256:(hd + 1) * 256].rearrange("(c p) f -> p c f", p=128), reads=[DB("mvt%d1" % j)], writes=[bh])
            P.dma("sp", smv, smo[j][:, hd * 256:(hd + 1) * 256].rearrange("(c p) f -> p c f", p=128), reads=[DB("smo%d" % j)], writes=[bh])
            P.op("pool", lambda e: e.memset(hsum, 0.0), writes=[bhs])
            for i in range(nch):
                for d in range(2):
                    c = i if d == 0 else nch - 1 - i
                    bi = d * 4 + hd
                    ci = bi * nch + c
                    msk = maskF if d == 0 else maskB
                    ucol = UGv[:, c, d * 8 + hd:d * 8 + hd + 1]
                    gcol = UGv[:, c, d * 8 + 4 + hd:d * 8 + 4 + hd + 1]
                    cs = slice(c * 128, (c + 1) * 128)
                    bS, bX, bU = 2, 3 + d, 5 + d
                    P.op("dve", lambda e, bi=bi, ci=ci: e.tensor_scalar(out=Cb[bi], in0=Cf[bi], scalar1=DECB[:, ci:ci + 1], scalar2=None, op0=ALU.mult), reads=[bC, bUG], writes=[bC])
                    P.op("dve", lambda e, bi=bi, ci=ci: e.tensor_scalar(out=Cf[bi], in0=Cf[bi], scalar1=DECB[:, ci:ci + 1], scalar2=None, op0=ALU.mult), reads=[bC, bUG], writes=[bC])
                    P.op("pe", lambda e, cs=cs, d=d: e.matmul(PS[:, bS * 512 + d * 128:bS * 512 + (d + 1) * 128], lhsT=kTt[:, cs], rhs=qTt[:, cs], start=True, stop=True), reads=[bh], writes=[bankb[bS]])
                    pt, ptb = ptr.next()
                    P.op("dve", lambda e, pt=pt, d=d, ucol=ucol, msk=msk: e.scalar_tensor_tensor(out=pt, in0=PS[:, bS * 512 + d * 128:bS * 512 + (d + 1) * 128], scalar=ucol, in1=msk, op0=ALU.mult, op1=ALU.mult),
                         reads=[bankb[bS], bUG, bconst], writes=[ptb])
                    ku, kub = kur.next()
                    P.op("dve", lambda e, ku=ku, c=c, ucol=ucol: e.tensor_scalar(out=ku, in0=ktkv[:, c, :], scalar1=ucol, scalar2=None, op0=ALU.mult), reads=[bh, bUG], writes=[kub])
                    P.op("pe", lambda e, pt=pt, c=c, bX=bX: e.matmul(bank(bX, 257), lhsT=pt, rhs=vav[:, c, :], start=True, stop=False), reads=[ptb, bh], writes=[bankb[bX]])
                    P.op("pe", lambda e, cs=cs, bi=bi, bX=bX: e.matmul(bank(bX, 257), lhsT=qTt[:, cs], rhs=Cb[bi], start=False, stop=True), reads=[bh, bC], writes=[bankb[bX]])
                    P.op("pe", lambda e, ku=ku, c=c, bU=bU: e.matmul(bank(bU, 257), lhsT=ku, rhs=vav[:, c, :], start=True, stop=True), reads=[kub, bh], writes=[bankb[bU]])
                    dn, dnb = dr.next()
                    P.op("dve", lambda e, dn=dn, bX=bX: e.tensor_scalar(out=dn[:, 0:1], in0=bank(bX, 257)[:, 256:257], scalar1=-1.0, scalar2=None, op0=ALU.mult), reads=[bankb[bX]], writes=[dnb])
                    P.op("dve", lambda e, dn=dn, bX=bX: e.tensor_tensor(out=dn[:, 0:1], in0=dn[:, 0:1], in1=bank(bX, 257)[:, 256:257], op=ALU.max), reads=[bankb[bX], dnb], writes=[dnb])
                    P.op("dve", lambda e, dn=dn, gcol=gcol: e.tensor_tensor(out=dn[:, 0:1], in0=dn[:, 0:1], in1=gcol, op=ALU.max), reads=[dnb, bUG], writes=[dnb])
                    P.op("dve", lambda e, dn=dn: e.reciprocal(out=dn[:, 1:2], in_=dn[:, 0:1]), reads=[dnb], writes=[dnb])
                    P.op("dve", lambda e, dn=dn, bX=bX, c=c: e.scalar_tensor_tensor(out=hsv[:, c, :], in0=bank(bX, 256), scalar=dn[:, 1:2], in1=hsv[:, c, :], op0=ALU.mult, op1=ALU.add),
                         reads=[bankb[bX], dnb, bhs], writes=[bhs])
                    P.op("dve", lambda e, bi=bi, bU=bU: e.tensor_tensor(out=Cf[bi], in0=Cf[bi], in1=bank(bU, 257), op=ALU.add), reads=[bC, bankb[bU]], writes=[bC])
            for c in range(nch):
                (stats, mv, rs, hn, yb), fb = fin.next()
                P.op("dve", lambda e, stats=stats, c=c: e.bn_stats(out=stats, in_=hsv[:, c, :]), reads=[bhs], writes=[fb])
                P.op("dve", lambda e, stats=stats, mv=mv: e.bn_aggr(out=mv, in_=stats), reads=[fb], writes=[fb])
                P.op("act", lambda e, rs=rs, mv=mv: e.activation(out=rs, in_=mv[:, 1:2], func=AF.Sqrt, bias=EPS), reads=[fb], writes=[fb])
                P.op("dve", lambda e, rs=rs: e.reciprocal(out=rs, in_=rs), reads=[fb], writes=[fb])
                P.op("dve", lambda e, hn=hn, mv=mv, rs=rs, c=c: e.tensor_scalar(out=hn, in0=hsv[:, c, :], scalar1=mv[:, 0:1], scalar2=rs, op0=ALU.subtract, op1=ALU.mult), reads=[bhs, fb], writes=[fb])
                P.op("dve", lambda e, hn=hn, hd=hd: e.tensor_tensor(out=hn, in0=hn, in1=mnb[:, hd * 256:(hd + 1) * 256], op=ALU.mult), reads=[fb, bmn], writes=[fb])
                P.op("dve", lambda e, hn=hn, yb=yb, c=c: e.tensor_tensor(out=yb, in0=hn, in1=smv[:, c, :], op=ALU.mult), reads=[fb, bh], writes=[fb])
                tb_i = 7
                tpb = bankbf(tb_i)
                for i2 in range(2):
                    P.op("pe", lambda e, i2=i2, yb=yb, tpb=tpb: e.transpose(tpb[:, i2 * 128:(i2 + 1) * 128], yb[:, i2 * 128:(i2 + 1) * 128], identb), reads=[fb, bconst], writes=[bankb[tb_i]])
                P.op("act", lambda e, c=c, tpb=tpb: e.activation(out=ymv[:, :, c * 128:(c + 1) * 128], in_=tpb[:, 0:256].rearrange("p (i t) -> p i t", i=2), func=AF.Copy), reads=[bankb[tb_i]], writes=[bym])
            P.dma("sp", ymT[j][2 * hd:2 * hd + 2].rearrange("i p t -> p i t"), ymv, reads=[bym], writes=[DB("ymT%d" % j)])

    def barrier_local(m):
        barrier()
        ar.reset(m)

    def phaseD1(j, wpa, wpm, wo, bw, g1b, l1g, l1b, bD):
        So = SO[j]
        nch = So // 128
        yar = Rot([ar.bf(8 * 128) for _ in range(3)], "ya")
        ymr = Rot([ar.bf(8 * 128) for _ in range(3)], "ym")
        sgr = Rot([ar.bf(2048) for _ in range(3)], "sg")
        xr = Rot([ar.f32(1024) for _ in range(3)], "xd")
        tr_ = Rot([ar.f32(1024) for _ in range(3)], "td")
        mir = Rot([ar.bf(1024) for _ in range(3)], "mi")
        mTr = Rot([ar.bf(1024) for _ in range(3)], "mT")
        rr = Rot([ar.f32(1024) for _ in range(3)], "rd")
        x1r = Rot([ar.f32(1024) for _ in range(3)], "x1")
        xnr = Rot([ar.bf(1024) for _ in range(3)], "xn2")
        str_ = Rot([(ar.f32(12), ar.f32(2), ar.f32(1)) for _ in range(6)], "st2")
        h2r = Rot([ar.bf(1024) for _ in range(3)], "h2")
        P.dma("sp", g1b, adab[j, 0:1, :].partition_broadcast(128), reads=[DB("adab")], writes=[bD])
        wpav = wpa.rearrange("p (k n) -> p k n", k=8)
        wpmv = wpm.rearrange("p (k n) -> p k n", k=8)
        wov = wo.rearrange("p (k n) -> p k n", k=8)
        for t in range(nch):
            tok = t * 128
            ya, yab = yar.next()
            ym, ymb = ymr.next()
            sg, sgb = sgr.next()
            xt, xb_ = xr.next()
            yav = ya.rearrange("p (h t) -> p h t", h=8)
            ymv = ym.rearrange("p (h t) -> p h t", h=8)
            P.dma("sp", yav, yaT[j][:, :, tok:tok + 128].rearrange("h p t -> p h t"), reads=[DB("yaT%d" % j)], writes=[yab])
            P.dma("sp", ymv, ymT[j][:, :, tok:tok + 128].rearrange("h p t -> p h t"), reads=[DB("ymT%d" % j)], writes=[ymb])
            P.dma("sp", sg, sbg[j][tok:tok + 128, :], reads=[DB("sbg%d" % j)], writes=[sgb])
            P.dma("sp", xt, xs[j][tok:tok + 128, :], writes=[xb_])
            for n in range(2):
                for k in range(8):
                    P.op("pe", lambda e, n=n, k=k, yav=yav: e.matmul(bank(n), lhsT=yav[:, k, :], rhs=wpav[:, k, n * 512:(n + 1) * 512], start=(k == 0), stop=(k == 7)), reads=[yab, bw], writes=[bankb[n]])
            for n in range(2):
                for k in range(8):
                    P.op("pe", lambda e, n=n, k=k, ymv=ymv: e.matmul(bank(2 + n), lhsT=ymv[:, k, :], rhs=wpmv[:, k, n * 512:(n + 1) * 512], start=(k == 0), stop=(k == 7)), reads=[ymb, bw], writes=[bankb[2 + n]])
            tm, tmb = tr_.next()
            mi, mib = mir.next()
            P.op("dve", lambda e, tm=tm, sg=sg: e.tensor_tensor(out=tm, in0=PS[:, 0:1024], in1=sg[:, 0:1024], op=ALU.mult), reads=[bankb[0], bankb[1], sgb], writes=[tmb])
            P.op("dve", lambda e, mi=mi, sg=sg: e.tensor_tensor(out=mi, in0=PS[:, 1024:2048], in1=sg[:, 1024:2048], op=ALU.mult), reads=[bankb[2], bankb[3], sgb], writes=[mib])
            P.op("dve", lambda e, mi=mi, tm=tm: e.tensor_tensor(out=mi, in0=mi, in1=tm, op=ALU.add), reads=[tmb, mib], writes=[mib])
            mT, mTb = mTr.next()
            tpb = bankbf(6)
            for k in range(8):
                P.op("pe", lambda e, k=k, mi=mi: e.transpose(tpb[:, k * 128:(k + 1) * 128], mi[:, k * 128:(k + 1) * 128], identb), reads=[mib, bconst], writes=[bankb[6]])
            P.op("act", lambda e, mT=mT: e.activation(out=mT, in_=tpb, func=AF.Copy), reads=[bankb[6]], writes=[mTb])
            mTv = mT.rearrange("p (k t) -> p k t", k=8)
            for n in range(2):
                for k in range(8):
                    P.op("pe", lambda e, n=n, k=k, mTv=mTv: e.matmul(bank(4 + n), lhsT=mTv[:, k, :], rhs=wov[:, k, n * 512:(n + 1) * 512], start=(k == 0), stop=(k == 7)), reads=[mTb, bw], writes=[bankb[4 + n]])
            r_, rb_ = rr.next()
            P.op("dve", lambda e, r_=r_: e.tensor_tensor(out=r_, in0=PS[:, 2048:3072], in1=g1b, op=ALU.mult), reads=[bankb[4], bankb[5], bD], writes=[rb_])
            P.op("dve", lambda e, r_=r_, xt=xt: e.scalar_tensor_tensor(out=r_, in0=xt, scalar=ALPHA_C, in1=r_, op0=ALU.mult, op1=ALU.add), reads=[xb_, rb_], writes=[rb_])
            x1, x1b = x1r.next()
            st, _ = str_.next()
            stats, mv, rs = st
            P.op("dve", lambda e, stats=stats, r_=r_: e.bn_stats(out=stats[:, 0:6], in_=r_[:, 0:512]), reads=[rb_], writes=[x1b])
            P.op("dve", lambda e, stats=stats, r_=r_: e.bn_stats(out=stats[:, 6:12], in_=r_[:, 512:1024]), reads=[rb_], writes=[x1b])
            P.op("dve", lambda e, stats=stats, mv=mv: e.bn_aggr(out=mv, in_=stats), reads=[x1b], writes=[x1b])
            P.op("act", lambda e, rs=rs, mv=mv: e.activation(out=rs, in_=mv[:, 1:2], func=AF.Sqrt, bias=EPS), reads=[x1b], writes=[x1b])
            P.op("dve", lambda e, rs=rs: e.reciprocal(out=rs, in_=rs), reads=[x1b], writes=[x1b])
            P.op("dve", lambda e, x1=x1, r_=r_, mv=mv, rs=rs: e.tensor_scalar(out=x1, in0=r_, scalar1=mv[:, 0:1], scalar2=rs, op0=ALU.subtract, op1=ALU.mult), reads=[rb_, x1b], writes=[x1b])
            P.op("dve", lambda e, x1=x1: e.tensor_tensor(out=x1, in0=x1, in1=l1g, op=ALU.mult), reads=[x1b, bD], writes=[x1b])
            P.op("dve", lambda e, x1=x1: e.tensor_tensor(out=x1, in0=x1, in1=l1b, op=ALU.add), reads=[x1b, bD], writes=[x1b])
            P.dma("sp", x1s[j][tok:tok + 128, :], x1, reads=[x1b], writes=[DB("x1s%d" % j)])
            xn, xnb = xnr.next()
            st2, _ = str_.next()
            h2, h2b = h2r.next()
            h2v = h2.rearrange("p (k t) -> p k t", k=8)
            ln_to_hT(x1, x1b, (h2b, h2v), sc2p1, sh2, j, 7, [], st2, xn, xnb)
            P.dma("sp", h2T[j][:, :, 1 + tok:1 + tok + 128].rearrange("k p t -> p k t"), h2v, reads=[h2b], writes=[DB("h2T%d" % j)])

    def phaseD2(j, wup, wdn, bw, g2b, l2g, l2b, bD):
        So = SO[j]
        TB2 = min(256, So)
        ntb = TB2 // 128
        hr = Rot([ar.bf(8 * (TB2 + 2)) for _ in range(1)], "h2l")
        zr = Rot([ar.bf(NF * TB2) for _ in range(1)], "zT")
        ar_ = Rot([ar.f32(TB2) for _ in range(3)], "ca")
        br_ = Rot([ar.f32(TB2) for _ in range(3)], "cb")
        x1r = Rot([ar.f32(1024) for _ in range(1)], "x1l")
        rr = Rot([ar.f32(1024) for _ in range(1)], "r2")
        yr = Rot([ar.f32(1024) for _ in range(1)], "yo")
        str_ = Rot([(ar.f32(12), ar.f32(2), ar.f32(1)) for _ in range(2)], "st3")
        P.dma("sp", g2b, adab[j, 1:2, :].partition_broadcast(128), reads=[DB("adab")], writes=[bD])
        wupv = wup.rearrange("p (k n) -> p k n", k=8)
        wdnv = wdn.rearrange("p (f n) -> p f n", f=NF)
        GC = 2.0 * math.sqrt(2.0 / math.pi)
        pbi = 0
        for blk in range(So // TB2):
            t0 = blk * TB2
            hh, hb = hr.next()
            hv = hh.rearrange("p (k t) -> p k t", k=8)
            P.dma("sp", hv, h2T[j][:, :, t0:t0 + TB2 + 2].rearrange("k p t -> p k t"), reads=[DB("h2T%d" % j)], writes=[hb])
            z, zb = zr.next()
            zv = z.rearrange("p (f t) -> p f t", f=NF)
            for f in range(NF):
                bg_i = (pbi % 3) * 2
                pbi += 1
                for k in range(8):
                    P.op("pe", lambda e, f=f, k=k, bg_i=bg_i, hv=hv: e.matmul(bank(bg_i, TB2 + 2), lhsT=wupv[:, k, f * 128:(f + 1) * 128], rhs=hv[:, k, :], start=(k == 0), stop=(k == 7)), reads=[bw, hb], writes=[bankb[bg_i]])
                for k in range(8):
                    P.op("pe", lambda e, f=f, k=k, bg_i=bg_i, hv=hv: e.matmul(bank(bg_i + 1, TB2), lhsT=wupv[:, k, DFF + f * 128:DFF + (f + 1) * 128], rhs=hv[:, k, 1:TB2 + 1], start=(k == 0), stop=(k == 7)), reads=[bw, hb], writes=[bankb[bg_i + 1]])
                a, ab = ar_.next()
                b2, bb = br_.next()
                ug = bank(bg_i, TB2 + 2)
                P.op("dve", lambda e, a=a, ug=ug, f=f: e.tensor_scalar(out=a, in0=ug[:, 0:TB2], scalar1=cwT[:, 3 * f:3 * f + 1], scalar2=cbT[:, f:f + 1], op0=ALU.mult, op1=ALU.add), reads=[bankb[bg_i], bconst], writes=[ab])
                P.op("dve", lambda e, a=a, ug=ug, f=f: e.scalar_tensor_tensor(out=a, in0=ug[:, 1:TB2 + 1], scalar=cwT[:, 3 * f + 1:3 * f + 2], in1=a, op0=ALU.mult, op1=ALU.add), reads=[bankb[bg_i], bconst, ab], writes=[ab])
                P.op("dve", lambda e, a=a, ug=ug, f=f: e.scalar_tensor_tensor(out=a, in0=ug[:, 2:TB2 + 2], scalar=cwT[:, 3 * f + 2:3 * f + 3], in1=a, op0=ALU.mult, op1=ALU.add), reads=[bankb[bg_i], bconst, ab], writes=[ab])
                P.op("act", lambda e, a=a, b2=b2: e.activation(out=b2, in_=a, func=AF.Square, scale=math.sqrt(0.044715)), reads=[ab], writes=[bb])
                P.op("dve", lambda e, a=a, b2=b2: e.scalar_tensor_tensor(out=b2, in0=b2, scalar=1.0, in1=a, op0=ALU.add, op1=ALU.mult), reads=[ab, bb], writes=[bb])
                P.op("act", lambda e, b2=b2: e.activation(out=b2, in_=b2, func=AF.Sigmoid, scale=GC), reads=[bb], writes=[bb])
                P.op("dve", lambda e, a=a, b2=b2: e.tensor_tensor(out=b2, in0=b2, in1=a, op=ALU.mult), reads=[ab, bb], writes=[bb])
                P.op("dve", lambda e, b2=b2, f=f, bg_i=bg_i, zv=zv: e.tensor_tensor(out=zv[:, f, :], in0=bank(bg_i + 1, TB2), in1=b2, op=ALU.mult), reads=[bankb[bg_i + 1], bb], writes=[zb])
            for ti in range(ntb):
                tok = t0 + ti * 128
                x1, x1b = x1r.next()
                P.dma("sp", x1, x1s[j][tok:tok + 128, :], reads=[DB("x1s%d" % j)], writes=[x1b])
                for n in range(2):
                    for f in range(NF):
                        P.op("pe", lambda e, n=n, f=f, ti=ti, zv=zv: e.matmul(bank(6 + n), lhsT=zv[:, f, ti * 128:(ti + 1) * 128], rhs=wdnv[:, f, n * 512:(n + 1) * 512], start=(f == 0), stop=(f == NF - 1)), reads=[zb, bw], writes=[bankb[6 + n]])
                r_, rb_ = rr.next()
                P.op("dve", lambda e, r_=r_: e.tensor_tensor(out=r_, in0=PS[:, 3072:4096], in1=g2b, op=ALU.mult), reads=[bankb[6], bankb[7], bD], writes=[rb_])
                P.op("dve", lambda e, r_=r_, x1=x1: e.scalar_tensor_tensor(out=r_, in0=x1, scalar=ALPHA_C, in1=r_, op0=ALU.mult, op1=ALU.add), reads=[x1b, rb_], writes=[rb_])
                yo, yob = yr.next()
                (stats, mv, rs), _ = str_.next()
                P.op("dve", lambda e, stats=stats, r_=r_: e.bn_stats(out=stats[:, 0:6], in_=r_[:, 0:512]), reads=[rb_], writes=[yob])
                P.op("dve", lambda e, stats=stats, r_=r_: e.bn_stats(out=stats[:, 6:12], in_=r_[:, 512:1024]), reads=[rb_], writes=[yob])
                P.op("dve", lambda e, stats=stats, mv=mv: e.bn_aggr(out=mv, in_=stats), reads=[yob], writes=[yob])
                P.op("act", lambda e, rs=rs, mv=mv: e.activation(out=rs, in_=mv[:, 1:2], func=AF.Sqrt, bias=EPS), reads=[yob], writes=[yob])
                P.op("dve", lambda e, rs=rs: e.reciprocal(out=rs, in_=rs), reads=[yob], writes=[yob])
                P.op("dve", lambda e, yo=yo, r_=r_, mv=mv, rs=rs: e.tensor_scalar(out=yo, in0=r_, scalar1=mv[:, 0:1], scalar2=rs, op0=ALU.subtract, op1=ALU.mult), reads=[rb_, yob], writes=[yob])
                P.op("dve", lambda e, yo=yo: e.tensor_tensor(out=yo, in0=yo, in1=l2g, op=ALU.mult), reads=[yob, bD], writes=[yob])
                P.op("dve", lambda e, yo=yo: e.tensor_tensor(out=yo, in0=yo, in1=l2b, op=ALU.add), reads=[yob, bD], writes=[yob])
                P.dma("sp", yout[j][tok:tok + 128, :], yo, reads=[yob], writes=[DB("yout")], is_output=True)

    phase0()
    convert_w_in()
    barrier()
    for j in range(NJ):
        phaseA(j, xs[j], SO[j], True, ropeo[j])
        barrier()
    phaseA(2, xctx, Sc, False, ropec)
    barrier()
    for j in range(NJ):
        phaseB(j)
        barrier()
    for j in range(NJ):
        phaseC(j)
        barrier()
    wpa = ar.bf(8 * 1024)
    wpm = ar.bf(8 * 1024)
    wo = ar.bf(8 * 1024)
    g1b = ar.f32(1024)
    l1g = ar.f32(1024)
    l1b = ar.f32(1024)
    bw = Buf("wD1")
    bD = Buf("bD1")
    for wt_, src in ((wpa, w_pa), (wpm, w_pm), (wo, w_out)):
        P.dma("pool", wt_.rearrange("p (k n) -> p k n", k=8), src.rearrange("(k p) n -> p k n", p=128), writes=[bw])
    P.dma("sp", l1g, ln1g.partition_broadcast(128), writes=[bD])
    P.dma("sp", l1b, ln1b.partition_broadcast(128), writes=[bD])
    zt = ar.bf(16)
    bz = Buf("zt")
    P.op("dve", lambda e: e.memset(zt, 0.0), writes=[bz])
    for j in range(NJ):
        P.dma("sp", h2T[j][:, :, 0:1].rearrange("k p t -> p k t"), zt[:, 0:8].rearrange("p (k t) -> p k t", t=1), reads=[bz], writes=[DB("h2T%d" % j)], slow=True)
        P.dma("sp", h2T[j][:, :, SO[j] + 1:SO[j] + 2].rearrange("k p t -> p k t"), zt[:, 8:16].rearrange("p (k t) -> p k t", t=1), reads=[bz], writes=[DB("h2T%d" % j)], slow=True)
    mD = ar.mark()
    for j in range(NJ):
        phaseD1(j, wpa, wpm, wo, bw, g1b, l1g, l1b, bD)
        barrier()
        ar.reset(mD)
    ar.reset()
    wup = ar.bf(8 * 2 * DFF)
    wdn = ar.bf(NF * 1024)
    g2b = ar.f32(1024)
    l2g = ar.f32(1024)
    l2b = ar.f32(1024)
    bw2 = Buf("wD2")
    bD2 = Buf("bD2")
    wupv = wup.rearrange("p (k n) -> p k n", k=8)
    for k in range(8):
        P.dma("pool", wupv[:, k, :], w_up[k * 128:(k + 1) * 128, :], writes=[bw2])
    P.dma("pool", wdn.rearrange("p (f n) -> p f n", f=NF), w_down.rearrange("(f p) n -> p f n", p=128), writes=[bw2])
    P.dma("sp", l2g, ln2g.partition_broadcast(128), writes=[bD2])
    P.dma("sp", l2b, ln2b.partition_broadcast(128), writes=[bD2])
    mD = ar.mark()
    for j in range(NJ):
        phaseD2(j, wup, wdn, bw2, g2b, l2g, l2b, bD2)
        barrier()
        ar.reset(mD)
    P.emit()
    return nc


def rope_tab(pos):
    inv = (500000.0 ** (-np.arange(0, 16, 2, dtype=np.float32) / np.float32(16))).astype(np.float32)
    ang = pos.astype(np.float32)[:, None] * inv[None, :]
    return np.concatenate([np.cos(ang), np.sin(ang)], axis=1).astype(np.float32)


_NC_CACHE = {}


def kernel(x_prompt, x_sample, c_prompt, c_sample, w_ada, b_ada, w_in, lambda_qk, subln_g,
           b_mgate, mnorm_g, w_pa, w_pm, w_out, ln1_g, ln1_b, w_up, conv_w, conv_b, w_down,
           ln2_g, ln2_b):
    f = np.float32
    x_prompt = np.asarray(x_prompt, f)
    x_sample = np.asarray(x_sample, f)
    Bp, Sc, _ = x_prompt.shape
    Bs, So, _ = x_sample.shape
    ncores = 8
    assert Bs == 2 * ncores and Bp * Sc == ncores * So
    nq = Sc // So
    key = (So, Sc)
    if key not in _NC_CACHE:
        _NC_CACHE[key] = build(So, Sc)
    nc = _NC_CACHE[key]
    c_prompt = np.asarray(c_prompt, f)
    c_sample = np.asarray(c_sample, f)
    tri = np.triu(np.ones((128, 128), f))
    sel = np.zeros((4, 4, 128), f)
    for h in range(4):
        sel[h, h, :] = 1.0
    shared = {
        "w_ada": np.ascontiguousarray(np.asarray(w_ada, f)[0]),
        "b_adaT": np.ascontiguousarray(np.asarray(b_ada, f)[0].reshape(48, 128).T),
        "b_ada": np.ascontiguousarray(np.asarray(b_ada, f)[0].reshape(1, -1)),
        "w_in": np.ascontiguousarray(np.asarray(w_in, f)[0]),
        "lambda_qk": np.ascontiguousarray(np.asarray(lambda_qk, f)[0].reshape(1, 256)),
        "subln": np.ascontiguousarray(np.asarray(subln_g, f)[0].reshape(128, 1)),
        "bmgT": np.ascontiguousarray(np.asarray(b_mgate, f)[0].reshape(4, 4).T),
        "mnorm": np.ascontiguousarray(np.asarray(mnorm_g, f)[0].reshape(1, 1024)),
        "w_pa": np.ascontiguousarray(np.asarray(w_pa, f)[0]),
        "w_pm": np.ascontiguousarray(np.asarray(w_pm, f)[0]),
        "w_out": np.ascontiguousarray(np.asarray(w_out, f)[0]),
        "ln1g": np.ascontiguousarray(np.asarray(ln1_g, f)[0].reshape(1, -1)),
        "ln1b": np.ascontiguousarray(np.asarray(ln1_b, f)[0].reshape(1, -1)),
        "w_up": np.ascontiguousarray(np.asarray(w_up, f)[0]),
        "convwT": np.ascontiguousarray(np.asarray(conv_w, f)[0].reshape(3, NF, 128).transpose(2, 1, 0)),
        "convbT": np.ascontiguousarray(np.asarray(conv_b, f)[0].reshape(NF, 128).T),
        "w_down": np.ascontiguousarray(np.asarray(w_down, f)[0]),
        "ln2g": np.ascontiguousarray(np.asarray(ln2_g, f)[0].reshape(1, -1)),
        "ln2b": np.ascontiguousarray(np.asarray(ln2_b, f)[0].reshape(1, -1)),
        "ident": np.eye(128, dtype=f),
        "sel": sel.reshape(4, 512),
        "maskF": tri,
        "maskB": np.ascontiguousarray(tri.T),
        "ropec": rope_tab(np.arange(Sc)),
    }
    in_maps = []
    offs = []
    So2 = So + 256 if Sc > So + 256 else So
    for c in range(ncores):
        pb, pq = c // nq, c % nq
        m = dict(shared)
        m["xs"] = np.ascontiguousarray(np.stack([x_sample[2 * c], x_sample[2 * c + 1]]))
        st2 = min(max(pq * So - 128, 0), Sc - So2)
        offs.append(pq * So - st2)
        m["xs2"] = np.ascontiguousarray(x_prompt[pb, st2:st2 + So2])
        m["xctx"] = np.ascontiguousarray(x_prompt[pb])
        cj = np.stack([c_sample[2 * c], c_sample[2 * c + 1], c_prompt[pb]])
        m["cT"] = np.ascontiguousarray(cj.reshape(3, 8, 128).transpose(2, 1, 0))
        ro = rope_tab(np.arange(So))
        m["ropeo"] = np.ascontiguousarray(np.stack([ro, ro]))
        m["ropeo2"] = rope_tab(np.arange(st2, st2 + So2))
        pos = np.arange(Sc)
        m["mb"] = np.ascontiguousarray(np.tile((pos < st2).astype(f)[None, :], (4, 1)))
        m["ma"] = np.ascontiguousarray(np.tile((pos >= st2 + So2).astype(f)[None, :], (4, 1)))
        in_maps.append(m)
    res = run_bass_kernel_spmd(nc, in_maps, core_ids=list(range(ncores)))
    y_prompt = np.zeros((Bp, Sc, D), f)
    y_sample = np.zeros((Bs, So, D), f)
    for c in range(ncores):
        y = res.results[c]["y"]
        y2 = res.results[c]["y2"]
        pb, pq = c // nq, c % nq
        y_sample[2 * c] = y[0]
        y_sample[2 * c + 1] = y[1]
        y_prompt[pb, pq * So:(pq + 1) * So] = y2[offs[c]:offs[c] + So]
    return (y_prompt, y_sample)
```

```python
import concourse.bass as bass
import concourse.mybir as mybir

F32 = mybir.dt.float32
BF16 = mybir.dt.bfloat16
AF = mybir.ActivationFunctionType
ALU = mybir.AluOpType
AX = mybir.AxisListType

COMPUTE = ("pe", "act", "dve", "pool")
QUEUES = ("sp", "act", "pool")
N_DMA_SEMS = 24


class Buf:
    __slots__ = ("name", "writers", "readers")

    def __init__(self, name=""):
        self.name = name
        self.writers = {}
        self.readers = {}


class Op:
    __slots__ = ("eng", "fn", "deps", "sig", "sem", "val", "idx", "is_dma", "key", "pre")

    def __init__(self, eng, fn):
        self.eng = eng
        self.fn = fn
        self.deps = []
        self.sig = False
        self.sem = None
        self.val = None
        self.idx = None
        self.is_dma = False
        self.key = eng
        self.pre = None


class Prog:
    def __init__(self, nc):
        self.nc = nc
        self.streams = {e: [] for e in ("pe", "act", "dve", "pool", "sp")}
        self.dma_count = [0] * N_DMA_SEMS
        self.dma_rr = 0
        self.out_dmas = []
        self.fence = []
        self.last_dma = [None] * N_DMA_SEMS

    def _add(self, op, reads, writes):
        st = self.streams[op.eng]
        op.idx = len(st)
        deps = list(self.fence)
        for b in reads:
            for k, w in b.writers.items():
                deps.append(w)
        for b in writes:
            for k, r in b.readers.items():
                deps.append(r)
            for k, w in b.writers.items():
                deps.append(w)
        seen = set()
        for d in deps:
            if d is op or id(d) in seen:
                continue
            seen.add(id(d))
            if (not d.is_dma) and d.eng == op.eng:
                if not op.is_dma:
                    if op.eng == "pe":
                        continue
                    if op.idx - d.idx > 3:
                        continue
            op.deps.append(d)
        for b in reads:
            b.readers[op.key] = op
        for b in writes:
            b.writers[op.key] = op
        st.append(op)
        return op

    def op(self, eng, fn, reads=(), writes=()):
        return self._add(Op(eng, fn), reads, writes)

    def dma(self, q, out, in_, reads=(), writes=(), is_output=False, slow=False):
        k = self.dma_rr
        self.dma_rr = (self.dma_rr + 1) % N_DMA_SEMS
        self.dma_count[k] += 1
        n = self.dma_count[k]
        o = Op(q, (lambda e, out=out, in_=in_: e.dma_start(out=out, in_=in_, allow_slow_non_contiguous=True)) if slow else (lambda e, out=out, in_=in_: e.dma_start(out=out, in_=in_)))
        o.is_dma = True
        o.key = ("dma", k)
        o.sem = k
        o.val = 16 * n
        o.pre = (k, 16 * (n - 1))
        self._add(o, reads, writes)
        self.last_dma[k] = o
        if is_output:
            self.out_dmas.append(o)
        return o

    def emit(self):
        nc = self.nc
        import contextlib
        with contextlib.ExitStack() as es:
            esem = {e: es.enter_context(nc.semaphore("sem_" + e)) for e in COMPUTE}
            dsem = [es.enter_context(nc.semaphore("dsem%d" % i)) for i in range(N_DMA_SEMS)]
            for e, st in self.streams.items():
                for o in st:
                    for d in o.deps:
                        if not d.is_dma:
                            d.sig = True
            for e, st in self.streams.items():
                c = 0
                for o in st:
                    if o.is_dma:
                        o.sem = dsem[o.sem] if isinstance(o.sem, int) else o.sem
                        continue
                    o.sem = esem[e]
                    if o.sig:
                        c += 1
                        o.val = c
            block = es.enter_context(nc.Block())
            engmap = {"pe": "tensor", "act": "scalar", "dve": "vector", "pool": "gpsimd", "sp": "sync"}
            out_dmas = self.out_dmas

            def make(ename, st, final):
                def body(eng):
                    waited = {}
                    for o in st:
                        if o.pre is not None:
                            s = dsem[o.pre[0]]
                            v = o.pre[1]
                            if v > 0 and waited.get(id(s), 0) < v:
                                eng.wait_ge(s, v)
                                waited[id(s)] = v
                        for d in o.deps:
                            if waited.get(id(d.sem), 0) < d.val:
                                eng.wait_ge(d.sem, d.val)
                                waited[id(d.sem)] = d.val
                        ins = o.fn(eng)
                        if o.is_dma:
                            ins.then_inc(o.sem, 16)
                        elif o.sig:
                            ins.then_inc(o.sem, 1)
                    if final:
                        for o in out_dmas:
                            if waited.get(id(o.sem), 0) < o.val:
                                eng.wait_ge(o.sem, o.val)
                                waited[id(o.sem)] = o.val
                return body

            for e, st in self.streams.items():
                deco = getattr(block, engmap[e])
                deco(make(e, st, e == "sp"))

import math
import numpy as np
import ml_dtypes
from concourse.bass_utils import run_bass_kernel_spmd

D = 1024
HA = 8
HM = 4
DFF = 2816
NF = 22
DIN = 8208
EPS = 1e-5
ALPHA_C = 2.0 ** 0.25
LAM_INIT = 0.8 - 0.6 * math.exp(-0.3 * 0)
LNSC = -0.5 * math.log(128.0)
NEG = -1.0e4
COLS = dict(aq=(0, 1024), ak=(1024, 2048), av=(2048, 3072), mq=(3072, 3584), mk=(3584, 4096),
            mv=(4096, 5120), mo=(5120, 6144), mg=(6144, 6160), bg=(6160, 8208))
ARENA_ELEMS = 94208


class Arena:
    def __init__(self, A, total):
        self.A = A
        self.lo = 0
        self.hi = total
        self.base = 0

    def bf(self, n, persist=False):
        n2 = (n + 15) // 16 * 16
        if persist:
            self.hi -= n2
            off = self.hi
        else:
            off = self.lo
            self.lo += n2
        assert self.lo <= self.hi, ("arena overflow", self.lo, self.hi)
        return self.A[:, off:off + n]

    def f32(self, n, persist=False):
        return self.bf(2 * n, persist).bitcast(F32)

    def mark(self):
        return self.lo

    def reset(self, m=0):
        self.lo = m


class Rot:
    def __init__(self, aps, name="r"):
        self.aps = aps
        self.bufs = [Buf(name + str(i)) for i in range(len(aps))]
        self.i = 0

    def next(self):
        k = self.i % len(self.aps)
        self.i += 1
        return self.aps[k], self.bufs[k]


def build(So, Sc):
    nc = bass.Bass("TRN2", target_bir_lowering=False)
    NJ = 3
    So2 = So + 256 if Sc > So + 256 else So
    SO = [So, So, So2]
    nchc = Sc // 128

    def din(name, shape, dt=F32):
        return nc.dram_tensor(name, list(shape), dt, kind="ExternalInput").ap()

    def dscr(name, shape, dt=BF16):
        return nc.dram_tensor(name, list(shape), dt).ap()

    xs01 = din("xs", [2, So, D])
    xs2 = din("xs2", [So2, D])
    xs = [xs01[0], xs01[1], xs2]
    xctx = din("xctx", [Sc, D])
    cT = din("cT", [128, 8, NJ])
    ropeo01 = din("ropeo", [2, So, 16])
    ropeo2 = din("ropeo2", [So2, 16])
    ropeo = [ropeo01[0], ropeo01[1], ropeo2]
    ropec = din("ropec", [Sc, 16])
    mb_in = din("mb", [4, Sc])
    ma_in = din("ma", [4, Sc])
    w_ada = din("w_ada", [D, 6 * D])
    b_adaT = din("b_adaT", [128, 48])
    b_ada = din("b_ada", [1, 6 * D])
    w_in = din("w_in", [D, DIN])
    lambda_qk = din("lambda_qk", [1, 256])
    subln = din("subln", [128, 1])
    bmgT = din("bmgT", [4, 4])
    mnorm = din("mnorm", [1, 1024])
    w_pa = din("w_pa", [D, D])
    w_pm = din("w_pm", [D, D])
    w_out = din("w_out", [D, D])
    ln1g = din("ln1g", [1, D])
    ln1b = din("ln1b", [1, D])
    w_up = din("w_up", [D, 2 * DFF])
    convwT = din("convwT", [128, NF, 3])
    convbT = din("convbT", [128, NF])
    w_down = din("w_down", [DFF, D])
    ln2g = din("ln2g", [1, D])
    ln2b = din("ln2b", [1, D])
    ident_in = din("ident", [128, 128])
    sel_in = din("sel", [4, 512])
    maskF_in = din("maskF", [128, 128])
    maskB_in = din("maskB", [128, 128])
    yout01 = nc.dram_tensor("y", [2, So, D], F32, kind="ExternalOutput").ap()
    yout2 = nc.dram_tensor("y2", [So2, D], F32, kind="ExternalOutput").ap()
    yout = [yout01[0], yout01[1], yout2]

    Sctx = [So, So, Sc]
    QT = [dscr("QT%d" % j, [HA, 128, SO[j]]) for j in range(NJ)]
    KT = [dscr("KT%d" % j, [HA, 128, Sctx[j]]) for j in range(NJ)]
    VV = [dscr("VV%d" % j, [HA, Sctx[j], 128]) for j in range(NJ)]
    mqT = [dscr("mqT%d" % j, [HM, 128, SO[j]]) for j in range(NJ)]
    mkT = [dscr("mkT%d" % j, [HM, 128, SO[j]]) for j in range(NJ)]
    mkt = [dscr("mkt%d" % j, [SO[j], 512]) for j in range(NJ)]
    mvt = [dscr("mvt%d" % j, [SO[j], 1024]) for j in range(NJ)]
    mktc = dscr("mktc", [Sc, 512])
    mvtc = dscr("mvtc", [Sc, 1024])
    GG = [dscr("GG%d" % j, [16, SO[j]], F32) for j in range(NJ)]
    GGc = dscr("GGc", [16, Sc], F32)
    AMs = dscr("AMs", [2, 4, Sc], F32)
    smo = [dscr("smo%d" % j, [SO[j], 1024]) for j in range(NJ)]
    sbg = [dscr("sbg%d" % j, [SO[j], 2048]) for j in range(NJ)]
    yaT = [dscr("yaT%d" % j, [HA, 128, SO[j]]) for j in range(NJ)]
    ymT = [dscr("ymT%d" % j, [8, 128, SO[j]]) for j in range(NJ)]
    x1s = [dscr("x1s%d" % j, [SO[j], D], F32) for j in range(NJ)]
    h2T = [dscr("h2T%d" % j, [8, 128, SO[j] + 2]) for j in range(NJ)]
    adab = dscr("adab", [NJ, 2, D], F32)
    wb_in = dscr("wb_in", [128, 8, DIN])

    P = Prog(nc)
    A_ = nc.alloc_sbuf_tensor("arena", [128, ARENA_ELEMS], BF16).ap()
    PS = nc.alloc_psum_tensor("psum", [128, 4096], F32).ap()
    ar = Arena(A_, ARENA_ELEMS)
    bankb = [Buf("bank%d" % i) for i in range(8)]

    def bank(i, n=512, off=0):
        return PS[:, i * 512 + off:i * 512 + off + n]

    def bankbf(i):
        return PS[:, i * 512:(i + 1) * 512].bitcast(BF16)

    dbuf = {}

    def DB(name):
        if name not in dbuf:
            dbuf[name] = Buf(name)
        return dbuf[name]

    identf = ar.f32(128, True)
    identb = ar.bf(128, True)
    maskF = ar.bf(128, True)
    maskB = ar.bf(128, True)
    onesb = ar.bf(128, True)
    onesf = ar.f32(128, True)
    self_ = ar.f32(512, True)
    sc1p1 = ar.f32(NJ * 8, True)
    sh1 = ar.f32(NJ * 8, True)
    sc2p1 = ar.f32(NJ * 8, True)
    sh2 = ar.f32(NJ * 8, True)
    small = ar.f32(64, True)
    lamc = small[:, 0:1]
    nlam = small[:, 1:2]
    gsc = small[:, 2:3]
    sublc = small[:, 3:4]
    bmg = small[0:4, 8:12]
    zero4 = small[0:4, 12:13]
    cwT = ar.f32(NF * 3, True)
    cbT = ar.f32(NF, True)
    bconst = Buf("const")
    tmpf = ar.f32(128)
    tmpm = ar.f32(256)
    P.dma("sp", identf, ident_in, writes=[bconst])
    P.dma("sp", tmpm[:, 0:128], maskF_in, writes=[bconst])
    P.dma("sp", tmpm[:, 128:256], maskB_in, writes=[bconst])
    P.dma("sp", self_[0:4, :], sel_in, writes=[bconst])
    P.dma("sp", sublc, subln, writes=[bconst])
    P.dma("sp", bmg, bmgT, writes=[bconst])
    P.dma("sp", cwT, convwT.rearrange("p f k -> p (f k)"), writes=[bconst])
    P.dma("sp", cbT, convbT, writes=[bconst])
    P.op("dve", lambda e: e.tensor_copy(out=identb, in_=identf), reads=[bconst], writes=[bconst])
    P.op("dve", lambda e: e.tensor_copy(out=maskF, in_=tmpm[:, 0:128]), reads=[bconst], writes=[bconst])
    P.op("dve", lambda e: e.tensor_copy(out=maskB, in_=tmpm[:, 128:256]), reads=[bconst], writes=[bconst])
    P.op("dve", lambda e: e.memset(onesb, 1.0), writes=[bconst])
    P.op("dve", lambda e: e.memset(onesf, 1.0), writes=[bconst])
    P.op("dve", lambda e: e.memset(zero4, 0.0), writes=[bconst])
    lq = ar.f32(256)
    lt = ar.f32(128)
    l2 = ar.f32(2)
    P.dma("sp", lq, lambda_qk.partition_broadcast(128), writes=[bconst])
    lqv = lq.rearrange("p (a b d) -> p a b d", a=2, b=2)
    P.op("dve", lambda e: e.tensor_tensor(out=lt.rearrange("p (a d) -> p a d", a=2), in0=lqv[:, :, 0, :], in1=lqv[:, :, 1, :], op=ALU.mult), reads=[bconst], writes=[bconst])
    P.op("dve", lambda e: e.tensor_reduce(out=l2, in_=lt.rearrange("p (a d) -> p a d", a=2), axis=AX.X, op=ALU.add), reads=[bconst], writes=[bconst])
    P.op("act", lambda e: e.activation(out=l2, in_=l2, func=AF.Exp), reads=[bconst], writes=[bconst])
    P.op("dve", lambda e: e.tensor_tensor(out=lamc, in0=l2[:, 0:1], in1=l2[:, 1:2], op=ALU.subtract), reads=[bconst], writes=[bconst])
    P.op("dve", lambda e: e.tensor_scalar(out=nlam, in0=lamc, scalar1=LAM_INIT, scalar2=-1.0, op0=ALU.add, op1=ALU.mult), reads=[bconst], writes=[bconst])
    P.op("dve", lambda e: e.tensor_scalar(out=gsc, in0=sublc, scalar1=(1.0 - LAM_INIT), scalar2=None, op0=ALU.mult), reads=[bconst], writes=[bconst])

    bar_bufs = {e: Buf("bar_" + e) for e in COMPUTE}
    bar_t = ar.f32(8, True)
    bar_tb = ar.bf(16, True)

    def barrier():
        s1 = []
        s1.append(P.op("dve", lambda e: e.memset(bar_t[0:1, 0:1], 0.0), writes=[bar_bufs["dve"]]))
        s1.append(P.op("pool", lambda e: e.memset(bar_t[0:1, 2:3], 0.0), writes=[bar_bufs["pool"]]))
        s1.append(P.op("act", lambda e: e.activation(out=bar_t[0:1, 4:5], in_=identf[0:1, 0:1], func=AF.Copy), writes=[bar_bufs["act"]]))
        s1.append(P.op("pe", lambda e: e.matmul(PS[0:1, 7 * 512:7 * 512 + 1], lhsT=identb[0:1, 0:1], rhs=identb[0:1, 0:1], start=True, stop=True),
                       reads=[bconst], writes=[bar_bufs["pe"], bankb[7]]))
        P.fence = list(s1) + [o for o in P.last_dma if o is not None]
        ar.reset()

    def convert_w_in():
        for k in range(8):
            P.dma("pool", wb_in[:, k, :], w_in[k * 128:(k + 1) * 128, :], writes=[DB("wb_in")])

    def phase0():
        cTt = ar.f32(8 * NJ)
        scb = ar.bf(8 * NJ)
        b0 = Buf("p0")
        P.dma("sp", cTt, cT.rearrange("p k j -> p (k j)"), writes=[b0])
        P.op("act", lambda e: e.activation(out=scb, in_=cTt, func=AF.Silu), reads=[b0], writes=[b0])
        scbv = scb.rearrange("p (k j) -> p k j", j=NJ)
        badT = ar.f32(48)
        P.dma("sp", badT, b_adaT, writes=[b0])
        wr = Rot([ar.bf(8 * 512) for _ in range(2)], "wada")
        rowt = Rot([ar.f32(512) for _ in range(2)], "rowt")
        brow = Rot([ar.f32(512) for _ in range(2)], "brow")
        dst = {0: (sh1, 0.0), 1: (sc1p1, 1.0), 3: (sh2, 0.0), 4: (sc2p1, 1.0)}
        for gi in range(12):
            wt, wb = wr.next()
            wv = wt.rearrange("p (k n) -> p k n", k=8)
            P.dma("pool", wv, w_ada[:, gi * 512:(gi + 1) * 512].rearrange("(k p) n -> p k n", p=128), writes=[wb])
            v = gi // 2
            if v in dst:
                tgt, addc = dst[v]
                for q in range(4):
                    kc = (gi % 2) * 4 + q
                    for k in range(8):
                        P.op("pe", lambda e, k=k, q=q, wv=wv: e.matmul(bank(0, NJ, 0), lhsT=wv[:, k, q * 128:(q + 1) * 128], rhs=scbv[:, k, :], start=(k == 0), stop=(k == 7)),
                             reads=[wb, b0], writes=[bankb[0]])
                    tv = tgt.rearrange("p (j k) -> p j k", j=NJ)[:, :, kc]
                    col = v * 8 + kc
                    P.op("dve", lambda e, tv=tv, col=col, addc=addc: e.tensor_scalar(out=tv, in0=bank(0, NJ, 0), scalar1=badT[:, col:col + 1], scalar2=addc, op0=ALU.add, op1=ALU.add),
                         reads=[bankb[0], b0], writes=[bconst])
            else:
                gidx = 0 if v == 2 else 1
                half = gi % 2
                bt, bb = brow.next()
                P.dma("sp", bt[0:1, :], b_ada[:, gi * 512:(gi + 1) * 512], writes=[bb])
                for j in range(NJ):
                    for k in range(8):
                        P.op("pe", lambda e, k=k, j=j, wv=wv: e.matmul(bank(1)[0:1, :], lhsT=scbv[:, k, j:j + 1], rhs=wv[:, k, :], start=(k == 0), stop=(k == 7)),
                             reads=[wb, b0], writes=[bankb[1]])
                    rt, rb = rowt.next()
                    P.op("dve", lambda e, rt=rt, bt=bt: e.tensor_tensor(out=rt[0:1, :], in0=bank(1)[0:1, :], in1=bt[0:1, :], op=ALU.add), reads=[bankb[1], bb], writes=[rb])
                    P.dma("sp", adab[j, gidx:gidx + 1, half * 512:(half + 1) * 512], rt[0:1, :], reads=[rb], writes=[DB("adab")])

    def ln_to_hT(xt, xb_, hT_dst, scp, shp, j, tr_bank, extra_reads, st, xn, xnb):
        stats, mv, rs = st
        P.op("dve", lambda e: e.bn_stats(out=stats[:, 0:6], in_=xt[:, 0:512]), reads=[xb_] + extra_reads, writes=[xnb])
        P.op("dve", lambda e: e.bn_stats(out=stats[:, 6:12], in_=xt[:, 512:1024]), reads=[xb_], writes=[xnb])
        P.op("dve", lambda e: e.bn_aggr(out=mv, in_=stats), reads=[xnb], writes=[xnb])
        P.op("act", lambda e: e.activation(out=rs, in_=mv[:, 1:2], func=AF.Sqrt, bias=EPS), reads=[xnb], writes=[xnb])
        P.op("dve", lambda e: e.reciprocal(out=rs, in_=rs), reads=[xnb], writes=[xnb])
        P.op("dve", lambda e: e.tensor_scalar(out=xn, in0=xt, scalar1=mv[:, 0:1], scalar2=rs, op0=ALU.subtract, op1=ALU.mult), reads=[xb_, xnb], writes=[xnb])
        pb = bankbf(tr_bank)
        for k in range(8):
            P.op("pe", lambda e, k=k: e.transpose(pb[:, k * 128:(k + 1) * 128], xn[:, k * 128:(k + 1) * 128], identb), reads=[xnb, bconst], writes=[bankb[tr_bank]])
        for k in range(8):
            P.op("act", lambda e, k=k: e.activation(out=hT_dst[1][:, k, :], in_=pb[:, k * 128:(k + 1) * 128], func=AF.Identity,
                                                    scale=scp[:, j * 8 + k:j * 8 + k + 1], bias=shp[:, j * 8 + k:j * 8 + k + 1]),
                 reads=[bankb[tr_bank], bconst], writes=[hT_dst[0]])

    def phaseA(j, xsrc, S, own, ropetab):
        TB = min(1024, S)
        nt = TB // 128
        xr = Rot([ar.f32(1024) for _ in range(2)], "x")
        xnr = Rot([ar.bf(1024) for _ in range(2)], "xn")
        str_ = Rot([(ar.f32(12), ar.f32(2), ar.f32(1)) for _ in range(2)], "st")
        hTr = Rot([ar.bf(8 * TB) for _ in range(2)], "hT")
        wr = Rot([ar.bf(8 * 512) for _ in range(3)], "w")
        stq = Rot([ar.bf(4 * TB) for _ in range(2)], "stq")
        stt = Rot([ar.bf(nt * 512) for _ in range(3)], "stt")
        rpb = Rot([ar.bf(512) for _ in range(2)], "rpb")
        rtm = Rot([ar.f32(4 * 64) for _ in range(2)], "rtm")
        ropr = Rot([ar.f32(nt * 16) for _ in range(2)], "rop")
        gst = Rot([ar.f32(4 * TB) for _ in range(2)], "gst")
        pbank = [2, 3, 4, 5]
        pbi = [0]
        tbk = [6, 7]
        tbi = [0]
        groups = []
        if own and j == 2:
            names = ["aq", "mq", "mk", "mv", "mo", "mg", "bg"]
        elif own:
            names = ["aq", "ak", "av", "mq", "mk", "mv", "mo", "mg", "bg"]
        else:
            names = ["ak", "av", "mk", "mv", "mg"]
        for nm in names:
            lo, hi = COLS[nm]
            ng = max(1, (hi - lo) // 512)
            for g in range(ng):
                groups.append((nm, g, lo + g * 512, min(512, hi - lo)))
        Kd = KT[j]
        Vd = VV[j]
        for blk in range((S + TB - 1) // TB):
            t0 = blk * TB
            tb = min(TB, S - t0)
            nt = tb // 128
            HB = min(512, tb)
            assert tb % HB == 0
            hT, hTb = hTr.next()
            hTv = hT[:, 0:8 * tb].rearrange("p (k t) -> p k t", k=8)
            rop, ropb = ropr.next()
            ropv = rop[:, 0:nt * 16].rearrange("p (t c) -> p t c", c=16)
            P.dma("sp", ropv, ropetab[t0:t0 + tb, :].rearrange("(t p) c -> p t c", p=128), writes=[ropb])
            for ti in range(nt):
                xt, xb_ = xr.next()
                xn, xnb = xnr.next()
                st, _ = str_.next()
                P.dma("sp", xt, xsrc[t0 + ti * 128:t0 + (ti + 1) * 128, :], writes=[xb_])
                ln_to_hT(xt, xb_, (hTb, hTv[:, :, ti * 128:(ti + 1) * 128]), sc1p1, sh1, j, ti % 2, [], st, xn, xnb)
            for (nm, g, c0, ncol) in groups:
                wt, wb = wr.next()
                wv = wt.rearrange("p (k n) -> p k n", k=8)
                P.dma("sp", wv[:, :, 0:ncol], wb_in[:, :, c0:c0 + ncol], reads=[DB("wb_in")], writes=[wb])
                if nm in ("aq", "ak", "av", "mk", "mv", "mo", "bg"):
                    if nm in ("aq", "ak"):
                        sq, sqb = stq.next()
                        sqv = sq[:, 0:4 * tb].rearrange("p (h t) -> p h t", h=4)
                    else:
                        sblk, sb_ = stt.next()
                        sv = sblk[:, 0:nt * 512].rearrange("p (t c) -> p t c", c=512)
                    for ti in range(nt):
                        pb_i = pbank[pbi[0] % 4]
                        pbi[0] += 1
                        for k in range(8):
                            P.op("pe", lambda e, k=k, ti=ti, pb_i=pb_i, wv=wv, hTv=hTv: e.matmul(bank(pb_i), lhsT=hTv[:, k, ti * 128:(ti + 1) * 128], rhs=wv[:, k, :], start=(k == 0), stop=(k == 7)),
                                 reads=[hTb, wb], writes=[bankb[pb_i]])
                        tok = t0 + ti * 128
                        if nm in ("aq", "ak"):
                            rb_, rbb = rpb.next()
                            tm, tmb = rtm.next()
                            psv = bank(pb_i).rearrange("p (h d) -> p h d", h=8)
                            rv = rb_.rearrange("p (h d) -> p h d", h=8)
                            tmv = tm.rearrange("p (a h d) -> p a h d", a=4, h=8)
                            cosb = ropv[:, ti, 0:8].unsqueeze(1).to_broadcast([128, 8, 8])
                            sinb = ropv[:, ti, 8:16].unsqueeze(1).to_broadcast([128, 8, 8])
                            rd = [bankb[pb_i], ropb]
                            P.op("dve", lambda e, psv=psv, tmv=tmv, cosb=cosb: e.tensor_tensor(out=tmv[:, 0], in0=psv[:, :, 0:8], in1=cosb, op=ALU.mult), reads=rd, writes=[tmb])
                            P.op("dve", lambda e, psv=psv, tmv=tmv, sinb=sinb: e.tensor_tensor(out=tmv[:, 1], in0=psv[:, :, 8:16], in1=sinb, op=ALU.mult), reads=rd, writes=[tmb])
                            P.op("dve", lambda e, psv=psv, tmv=tmv, cosb=cosb: e.tensor_tensor(out=tmv[:, 2], in0=psv[:, :, 8:16], in1=cosb, op=ALU.mult), reads=rd, writes=[tmb])
                            P.op("dve", lambda e, psv=psv, tmv=tmv, sinb=sinb: e.tensor_tensor(out=tmv[:, 3], in0=psv[:, :, 0:8], in1=sinb, op=ALU.mult), reads=rd, writes=[tmb])
                            P.op("dve", lambda e, rv=rv, psv=psv: e.tensor_copy(out=rv[:, :, 16:64], in_=psv[:, :, 16:64]), reads=rd, writes=[rbb])
                            P.op("dve", lambda e, rv=rv, tmv=tmv: e.tensor_tensor(out=rv[:, :, 0:8], in0=tmv[:, 0], in1=tmv[:, 1], op=ALU.subtract), reads=[tmb], writes=[rbb])
                            P.op("dve", lambda e, rv=rv, tmv=tmv: e.tensor_tensor(out=rv[:, :, 8:16], in0=tmv[:, 2], in1=tmv[:, 3], op=ALU.add), reads=[tmb], writes=[rbb])
                            tb_i = tbk[tbi[0] % 2]
                            tbi[0] += 1
                            tpb = bankbf(tb_i)
                            for hh in range(4):
                                P.op("pe", lambda e, hh=hh, tpb=tpb, rb_=rb_: e.transpose(tpb[:, hh * 128:(hh + 1) * 128], rb_[:, hh * 128:(hh + 1) * 128], identb),
                                     reads=[rbb, bconst], writes=[bankb[tb_i]])
                            P.op("act", lambda e, sqv=sqv, tpb=tpb, ti=ti: e.activation(out=sqv[:, :, ti * 128:(ti + 1) * 128], in_=tpb[:, 0:512].rearrange("p (h t) -> p h t", h=4), func=AF.Copy),
                                 reads=[bankb[tb_i]], writes=[sqb])
                        else:
                            s_ = sv[:, ti, :]
                            if nm in ("mo", "bg"):
                                P.op("act", lambda e, s_=s_, pb_i=pb_i: e.activation(out=s_, in_=bank(pb_i), func=AF.Sigmoid), reads=[bankb[pb_i]], writes=[sb_])
                            else:
                                P.op("act", lambda e, s_=s_, pb_i=pb_i: e.activation(out=s_, in_=bank(pb_i), func=AF.Copy), reads=[bankb[pb_i]], writes=[sb_])
                    if nm == "av":
                        for hh in range(4):
                            P.dma("pool", Vd[4 * g + hh, t0:t0 + tb, :].rearrange("(t p) d -> p t d", p=128), sv[:, :, hh * 128:(hh + 1) * 128], reads=[sb_], writes=[DB("VV%d" % j)])
                    elif nm == "mv":
                        dd = mvt[j] if own else mvtc
                        P.dma("pool", dd[t0:t0 + tb, g * 512:(g + 1) * 512].rearrange("(t p) c -> p t c", p=128), sv, reads=[sb_], writes=[DB("mvt%d%d" % (j, own))])
                    elif nm == "mk":
                        dd = mkt[j] if own else mktc
                        P.dma("pool", dd[t0:t0 + tb, :].rearrange("(t p) c -> p t c", p=128), sv, reads=[sb_], writes=[DB("mkt%d%d" % (j, own))])
                    elif nm == "mo":
                        P.dma("pool", smo[j][t0:t0 + tb, g * 512:(g + 1) * 512].rearrange("(t p) c -> p t c", p=128), sv, reads=[sb_], writes=[DB("smo%d" % j)])
                    elif nm == "bg":
                        P.dma("pool", sbg[j][t0:t0 + tb, g * 512:(g + 1) * 512].rearrange("(t p) c -> p t c", p=128), sv, reads=[sb_], writes=[DB("sbg%d" % j)])
                    if nm == "aq":
                        P.dma("pool", QT[j][4 * g:4 * g + 4, :, t0:t0 + tb].rearrange("h p t -> p h t"), sqv, reads=[sqb], writes=[DB("QT%d" % j)])
                    elif nm == "ak":
                        P.dma("pool", Kd[4 * g:4 * g + 4, :, t0:t0 + tb].rearrange("h p t -> p h t"), sqv, reads=[sqb], writes=[DB("KT%d" % j)])
                if (nm == "mq") or (nm == "mk" and own):
                    sq, sqb = stq.next()
                    sqv = sq[:, 0:4 * tb].rearrange("p (h t) -> p h t", h=4)
                    for hd in range(4):
                        for hf in range(tb // HB):
                            pb_i = pbank[pbi[0] % 4]
                            pbi[0] += 1
                            for k in range(8):
                                P.op("pe", lambda e, HB=HB, k=k, hd=hd, hf=hf, pb_i=pb_i, wv=wv, hTv=hTv: e.matmul(bank(pb_i, HB), lhsT=wv[:, k, hd * 128:(hd + 1) * 128], rhs=hTv[:, k, hf * HB:(hf + 1) * HB], start=(k == 0), stop=(k == 7)),
                                     reads=[hTb, wb], writes=[bankb[pb_i]])
                            P.op("dve", lambda e, HB=HB, hd=hd, hf=hf, pb_i=pb_i, sqv=sqv: e.tensor_copy(out=sqv[:, hd, hf * HB:(hf + 1) * HB], in_=bank(pb_i, HB)), reads=[bankb[pb_i]], writes=[sqb])
                    dd = mqT[j] if nm == "mq" else mkT[j]
                    P.dma("pool", dd[:, :, t0:t0 + tb].rearrange("h p t -> p h t"), sqv, reads=[sqb], writes=[DB(("mqT%d" if nm == "mq" else "mkT%d") % j)])
                if nm == "mg":
                    gs, gsb = gst.next()
                    gsv = gs[:, 0:4 * tb].rearrange("p (r t) -> p r t", r=4)
                    for r in range(4):
                        for hf in range(tb // HB):
                            pb_i = pbank[pbi[0] % 4]
                            pbi[0] += 1
                            for k in range(8):
                                P.op("pe", lambda e, HB=HB, k=k, r=r, hf=hf, pb_i=pb_i, wv=wv, hTv=hTv: e.matmul(bank(pb_i, HB)[0:4, :], lhsT=wv[:, k, 4 * r:4 * r + 4], rhs=hTv[:, k, hf * HB:(hf + 1) * HB], start=(k == 0), stop=(k == 7)),
                                     reads=[hTb, wb], writes=[bankb[pb_i]])
                            P.op("act", lambda e, HB=HB, r=r, hf=hf, pb_i=pb_i, gsv=gsv: e.activation(out=gsv[0:4, r, hf * HB:(hf + 1) * HB], in_=bank(pb_i, HB)[0:4, :], func=AF.Identity, bias=bmg[:, r:r + 1]),
                                 reads=[bankb[pb_i], bconst], writes=[gsb])
                    dd = GG[j] if own else GGc
                    P.dma("pool", dd[:, t0:t0 + tb].rearrange("(r h) t -> h r t", h=4), gsv[0:4], reads=[gsb], writes=[DB("GG%d%d" % (j, own))])

    def phaseB(j):
        Sk = Sctx[j]
        nkt = Sk // 128
        Sq = SO[j]
        QBM = min(512, Sq)
        ktr = Rot([ar.bf(Sk) for _ in range(2)], "kt")
        vr = Rot([ar.bf(Sk) for _ in range(2)], "v")
        qr = Rot([ar.bf(QBM) for _ in range(2)], "q")
        ptr = Rot([ar.bf(2 * QBM) for _ in range(3)], "pt")
        accs = [ar.f32(2 * QBM) for _ in range(2)]
        accb = [Buf("accA"), Buf("accB")]
        ocr = Rot([ar.f32(2 * QBM) for _ in range(2)], "oc")
        tfr = Rot([tuple(ar.f32(QBM) for _ in range(3)) for _ in range(2)], "tf")
        yst = Rot([ar.bf(QBM) for _ in range(2)], "yst")
        deferred = []

        def fire(force):
            keep = []
            for item in deferred:
                item[0] -= 1
                if force or item[0] <= 0:
                    item[1]()
                else:
                    keep.append(item)
            deferred[:] = keep

        for h in range(HA):
            kt_, ktb = ktr.next()
            v_, vb = vr.next()
            P.dma("sp", kt_, KT[j][h], reads=[DB("KT%d" % j)], writes=[ktb])
            vv = v_.rearrange("p (t d) -> p t d", d=128)
            P.dma("sp", vv, VV[j][h].rearrange("(t p) d -> p t d", p=128), reads=[DB("VV%d" % j)], writes=[vb])
            for qb in range((Sq + QBM - 1) // QBM):
                q0 = qb * QBM
                QB = min(QBM, Sq - q0)
                q_, qb_ = qr.next()
                q_ = q_[:, 0:QB]
                P.dma("sp", q_, QT[j][h, :, q0:q0 + QB], reads=[DB("QT%d" % j)], writes=[qb_])

                def qk(u, kt_=kt_, ktb=ktb, q_=q_, qb_=qb_, QB=QB):
                    sb = u % 2
                    for m in range(2):
                        P.op("pe", lambda e, m=m, sb=sb, u=u: e.matmul(bank(2 * sb + m, QB), lhsT=kt_[64 * m:64 * m + 64, u * 128:(u + 1) * 128], rhs=q_[64 * m:64 * m + 64, :], start=True, stop=True),
                             reads=[ktb, qb_], writes=[bankb[2 * sb + m]])
                pts = {}

                def ex(u, QB=QB):
                    sb = u % 2
                    pt, ptb = ptr.next()
                    pt = pt[:, 0:2 * QB]
                    pts[u] = (pt, ptb)
                    ptv = pt.rearrange("p (m q) -> p m q", m=2)
                    src = PS[:, 2 * sb * 512:(2 * sb + 2) * 512].rearrange("p (m q) -> p m q", m=2)[:, :, 0:QB]
                    P.op("act", lambda e, ptv=ptv, src=src: e.activation(out=ptv, in_=src, func=AF.Exp, scale=0.125), reads=[bankb[2 * sb], bankb[2 * sb + 1]], writes=[ptb])

                def pv(u, vv=vv, vb=vb, QB=QB):
                    pt, ptb = pts.pop(u)
                    for m in range(2):
                        P.op("pe", lambda e, m=m, u=u, pt=pt: e.matmul(bank(4 + m, QB), lhsT=vv[:, u, :], rhs=pt[:, m * QB:(m + 1) * QB], start=(u == 0), stop=(u == nkt - 1)),
                             reads=[vb, ptb], writes=[bankb[4 + m]])
                    ai = u % 2
                    acc = accs[ai][:, 0:2 * QB]
                    if u < 2:
                        P.op("dve", lambda e, acc=acc, pt=pt: e.tensor_copy(out=acc, in_=pt), reads=[ptb], writes=[accb[ai]])
                    else:
                        P.op("dve", lambda e, acc=acc, pt=pt: e.tensor_tensor(out=acc, in0=pt, in1=acc, op=ALU.add), reads=[ptb, accb[ai]], writes=[accb[ai]])
                qk(0)
                for u in range(nkt):
                    ex(u)
                    if u + 1 < nkt:
                        qk(u + 1)
                    pv(u)
                    fire(False)
                fire(True)
                nacc = min(2, nkt)
                for ai in range(nacc):
                    acc = accs[ai][:, 0:2 * QB]
                    for m in range(2):
                        P.op("pe", lambda e, m=m, acc=acc, QB=QB, ai=ai: e.matmul(bank(6 + m, QB), lhsT=onesf, rhs=acc[:, m * QB:(m + 1) * QB], start=(ai == 0), stop=(ai == nacc - 1)),
                             reads=[bconst, accb[ai]], writes=[bankb[6 + m]])
                oc, ocb = ocr.next()
                (r0, r1, sq), tfb = tfr.next()
                r0 = r0[:, 0:QB]
                r1 = r1[:, 0:QB]
                sq = sq[:, 0:QB]
                a0 = oc[:, 0:QB]
                a1 = oc[:, QBM:QBM + QB]
                P.op("act", lambda e, a0=a0, QB=QB: e.activation(out=a0, in_=bank(4, QB), func=AF.Copy), reads=[bankb[4]], writes=[ocb])
                P.op("dve", lambda e, a1=a1, QB=QB: e.tensor_copy(out=a1, in_=bank(5, QB)), reads=[bankb[5]], writes=[ocb])
                P.op("dve", lambda e, r0=r0, QB=QB: e.reciprocal(out=r0, in_=bank(6, QB)), reads=[bankb[6]], writes=[tfb])
                P.op("dve", lambda e, r1=r1, QB=QB: e.reciprocal(out=r1, in_=bank(7, QB)), reads=[bankb[7]], writes=[tfb])
                P.op("dve", lambda e, a0=a0, r0=r0: e.tensor_tensor(out=a0, in0=a0, in1=r0, op=ALU.mult), reads=[ocb, tfb], writes=[ocb])
                P.op("dve", lambda e, a1=a1, r1=r1: e.tensor_tensor(out=a1, in0=a1, in1=r1, op=ALU.mult), reads=[ocb, tfb], writes=[ocb])
                P.op("dve", lambda e, a0=a0, a1=a1: e.scalar_tensor_tensor(out=a0, in0=a1, scalar=nlam, in1=a0, op0=ALU.mult, op1=ALU.add), reads=[ocb, bconst], writes=[ocb])
                P.op("dve", lambda e, a0=a0, sq=sq: e.tensor_tensor(out=sq, in0=a0, in1=a0, op=ALU.mult), reads=[ocb], writes=[tfb])

                def part2(a0=a0, r0=r0, sq=sq, ocb=ocb, tfb=tfb, QB=QB, h=h, q0=q0):
                    P.op("pe", lambda e: e.matmul(bank(6, QB), lhsT=onesf, rhs=sq, start=True, stop=True), reads=[tfb, bconst], writes=[bankb[6]])
                    P.op("act", lambda e: e.activation(out=r0, in_=bank(6, QB), func=AF.Sqrt, scale=1.0 / 128.0, bias=EPS), reads=[bankb[6]], writes=[tfb])
                    P.op("dve", lambda e: e.reciprocal(out=r0, in_=r0), reads=[tfb], writes=[tfb])
                    ys, ysb = yst.next()
                    ys = ys[:, 0:QB]
                    P.op("dve", lambda e: e.scalar_tensor_tensor(out=ys, in0=a0, scalar=gsc, in1=r0, op0=ALU.mult, op1=ALU.mult), reads=[tfb, ocb, bconst], writes=[ysb])
                    P.dma("pool", yaT[j][h, :, q0:q0 + QB], ys, reads=[ysb], writes=[DB("yaT%d" % j)])
                deferred.append([6, part2])
        fire(True)

    def phaseC(j):
        is_p = (j == 2)
        So = SO[j]
        nch = So // 128
        UG = ar.f32(nch * 16)
        UGv = UG.rearrange("p (c q) -> p c q", q=16)
        DECB = ar.f32(8 * nch)
        Cf = [ar.f32(257) for _ in range(8)]
        Cb = [ar.bf(257) for _ in range(8)]
        ini = ar.f32(16)
        Minit = [ini[0:4, 0:1], ini[0:4, 1:2]]
        Binit = [ini[0:4, 2:3], ini[0:4, 3:4]]
        bC = Buf("Cstate")
        bini = Buf("ini")
        bUG = Buf("UG")
        P.op("dve", lambda e: e.memset(ini, 0.0), writes=[bini])
        m0 = ar.mark()
        def c2():
            PC = min(2048, Sc)
            npc = Sc // PC
            It = ar.f32(PC)
            Ft = ar.f32(PC)
            Pt = ar.f32(PC)
            Mt = ar.f32(PC)
            T1 = ar.f32(PC)
            T2 = ar.f32(PC)
            pm = ar.f32(2 * npc)
            bs = ar.f32(2 * npc)
            carry = ar.f32(4)
            tot = ar.f32(4)
            bg_ = Buf("c2g")
            bsm = Buf("c2s")
            P.op("dve", lambda e: e.memset(carry, 0.0), writes=[bsm])
            for d in range(2):
                msk = mb_in if d == 0 else ma_in
                for pc in range(npc):
                    sl = slice(pc * PC, (pc + 1) * PC)
                    P.dma("sp", It[0:4, :], GGc[d * 8:d * 8 + 4, sl], reads=[DB("GG20")], writes=[bg_])
                    P.dma("sp", Ft[0:4, :], GGc[d * 8 + 4:d * 8 + 8, sl], reads=[DB("GG20")], writes=[bg_])
                    P.dma("sp", Mt[0:4, :], msk[:, sl], writes=[bg_])
                    P.op("act", lambda e: e.activation(out=Ft[0:4, :], in_=Ft[0:4, :], func=AF.Exp, scale=-1.0), reads=[bg_], writes=[bg_])
                    P.op("act", lambda e: e.activation(out=Ft[0:4, :], in_=Ft[0:4, :], func=AF.Ln, bias=1.0), reads=[bg_], writes=[bg_])
                    init = 0.0 if pc == 0 else carry[0:4, d:d + 1]
                    P.op("dve", lambda e, init=init: e.tensor_tensor_scan(out=Pt[0:4, :], data0=onesf[0:4, 0:1].to_broadcast([4, PC]), data1=Ft[0:4, :], initial=init, op0=ALU.mult, op1=ALU.add),
                         reads=[bg_, bconst, bsm], writes=[bg_])
                    P.op("dve", lambda e, d=d: e.tensor_copy(out=carry[0:4, d:d + 1], in_=Pt[0:4, PC - 1:PC]), reads=[bg_], writes=[bsm])
                    P.op("dve", lambda e: e.tensor_tensor(out=T1[0:4, :], in0=Ft[0:4, :], in1=Mt[0:4, :], op=ALU.mult), reads=[bg_], writes=[bg_])
                    P.op("dve", lambda e, d=d, pc=pc: e.tensor_reduce(out=bs[0:4, d * npc + pc:d * npc + pc + 1], in_=T1[0:4, :], axis=AX.X, op=ALU.add), reads=[bg_], writes=[bsm])
                    if d == 0:
                        P.op("dve", lambda e: e.tensor_tensor(out=It[0:4, :], in0=It[0:4, :], in1=Pt[0:4, :], op=ALU.add), reads=[bg_], writes=[bg_])
                    else:
                        P.op("dve", lambda e: e.tensor_tensor(out=It[0:4, :], in0=It[0:4, :], in1=Ft[0:4, :], op=ALU.add), reads=[bg_], writes=[bg_])
                        P.op("dve", lambda e: e.tensor_tensor(out=It[0:4, :], in0=It[0:4, :], in1=Pt[0:4, :], op=ALU.subtract), reads=[bg_], writes=[bg_])
                    P.op("dve", lambda e: e.tensor_tensor(out=It[0:4, :], in0=It[0:4, :], in1=Mt[0:4, :], op=ALU.mult), reads=[bg_], writes=[bg_])
                    P.op("dve", lambda e: e.tensor_scalar(out=T2[0:4, :], in0=Mt[0:4, :], scalar1=-NEG, scalar2=NEG, op0=ALU.mult, op1=ALU.add), reads=[bg_], writes=[bg_])
                    P.op("dve", lambda e: e.tensor_tensor(out=It[0:4, :], in0=It[0:4, :], in1=T2[0:4, :], op=ALU.add), reads=[bg_], writes=[bg_])
                    P.op("dve", lambda e, d=d, pc=pc: e.tensor_reduce(out=pm[0:4, d * npc + pc:d * npc + pc + 1], in_=It[0:4, :], axis=AX.X, op=ALU.max), reads=[bg_], writes=[bsm])
                    P.dma("sp", AMs[d, :, sl], It[0:4, :], reads=[bg_], writes=[DB("AMs")])
            bias2 = ar.f32(4)
            for d in range(2):
                P.op("dve", lambda e, d=d: e.tensor_reduce(out=Binit[d], in_=bs[0:4, d * npc:(d + 1) * npc], axis=AX.X, op=ALU.add), reads=[bsm], writes=[bini])
                P.op("dve", lambda e, d=d: e.tensor_reduce(out=Minit[d], in_=pm[0:4, d * npc:(d + 1) * npc], axis=AX.X, op=ALU.max), reads=[bsm], writes=[bini])
            P.op("dve", lambda e: e.tensor_tensor(out=Minit[1], in0=Minit[1], in1=carry[0:4, 1:2], op=ALU.add), reads=[bini, bsm], writes=[bini])
            for d in range(2):
                P.op("dve", lambda e, d=d: e.tensor_scalar(out=Minit[d], in0=Minit[d], scalar1=0.0, scalar2=None, op0=ALU.max), reads=[bini], writes=[bini])
            P.op("dve", lambda e: e.tensor_scalar(out=bias2[0:4, 0:1], in0=Minit[0], scalar1=-1.0, scalar2=LNSC, op0=ALU.mult, op1=ALU.add), reads=[bini], writes=[bsm])
            P.op("dve", lambda e: e.tensor_tensor(out=bias2[0:4, 1:2], in0=carry[0:4, 1:2], in1=Minit[1], op=ALU.subtract), reads=[bini, bsm], writes=[bsm])
            P.op("dve", lambda e: e.tensor_scalar(out=bias2[0:4, 1:2], in0=bias2[0:4, 1:2], scalar1=LNSC, scalar2=None, op0=ALU.add), reads=[bsm], writes=[bsm])
            wtr = Rot([ar.f32(4) for _ in range(3)], "wt")
            kr = Rot([ar.bf(512) for _ in range(3)], "kc")
            var_ = [ar.bf(4 * 257) for _ in range(3)]
            vbufs = [Buf("vc%d" % i) for i in range(3)]
            for t, vb_ in zip(var_, vbufs):
                P.op("dve", lambda e, t=t: e.memset(t, 1.0), writes=[vb_])
            kur = Rot([ar.bf(128) for _ in range(4)], "ku")
            w0 = ar.f32(PC)
            bw = Buf("wrow")
            for d in range(2):
                for pc in range(npc):
                    sl = slice(pc * PC, (pc + 1) * PC)
                    P.dma("sp", w0[0:4, :], AMs[d, :, sl], reads=[DB("AMs")], writes=[bw])
                    P.op("act", lambda e, d=d: e.activation(out=w0[0:4, :], in_=w0[0:4, :], func=AF.Exp, bias=bias2[0:4, d:d + 1]), reads=[bw, bsm], writes=[bw])
                    for cl in range(PC // 128):
                        cc = pc * (PC // 128) + cl
                        wt_, wtb = wtr.next()
                        P.op("pe", lambda e, cl=cl: e.transpose(PS[:, 7 * 512:7 * 512 + 4], w0[0:4, cl * 128:(cl + 1) * 128], identf[0:4, 0:4]), reads=[bw, bconst], writes=[bankb[7]])
                        P.op("dve", lambda e, wt_=wt_: e.tensor_copy(out=wt_, in_=PS[:, 7 * 512:7 * 512 + 4]), reads=[bankb[7]], writes=[wtb])
                        k_, kb_ = kr.next()
                        P.dma("sp", k_, mktc[cc * 128:(cc + 1) * 128, :], reads=[DB("mkt20")], writes=[kb_])
                        vi = cc % 3
                        va = var_[vi].rearrange("p (h f) -> p h f", h=4)
                        P.dma("sp", va[:, :, 0:256], mvtc[cc * 128:(cc + 1) * 128, :].rearrange("p (h f) -> p h f", h=4), reads=[DB("mvt20")], writes=[vbufs[vi]])
                        for hd in range(4):
                            ku, kub = kur.next()
                            P.op("dve", lambda e, ku=ku, k_=k_, hd=hd, wt_=wt_: e.tensor_scalar(out=ku, in0=k_[:, hd * 128:(hd + 1) * 128], scalar1=wt_[:, hd:hd + 1], scalar2=None, op0=ALU.mult),
                                 reads=[kb_, wtb], writes=[kub])
                            P.op("pe", lambda e, ku=ku, va=va, hd=hd, cc=cc: e.matmul(bank(hd, 257), lhsT=ku, rhs=va[:, hd, :], start=(cc == 0), stop=(cc == nchc - 1)),
                                 reads=[kub, vbufs[vi]], writes=[bankb[hd]])
                for hd in range(4):
                    P.op("dve", lambda e, hd=hd, d=d: e.tensor_copy(out=Cf[d * 4 + hd], in_=bank(hd, 257)), reads=[bankb[hd]], writes=[bC])
        if is_p:
            c2()
            barrier_local(m0)
        else:
            for bi in range(8):
                P.op("dve", lambda e, bi=bi: e.memset(Cf[bi], 0.0), writes=[bC])
        It = ar.f32(So)
        Ft = ar.f32(So)
        Pt = ar.f32(So)
        cm = ar.f32(nch)
        Me = ar.f32(nch)
        Me2 = ar.f32(nch)
        Mp = ar.f32(nch)
        dec = [ar.f32(nch), ar.f32(nch)]
        tt = ar.f32(4)
        bg_ = Buf("c1g")
        bsm = Buf("c1s")
        bdec = Buf("dec")
        for d in range(2):
            P.dma("sp", It[0:4, :], GG[j][d * 8:d * 8 + 4, :], reads=[DB("GG%d1" % j)], writes=[bg_])
            P.dma("sp", Ft[0:4, :], GG[j][d * 8 + 4:d * 8 + 8, :], reads=[DB("GG%d1" % j)], writes=[bg_])
            P.op("act", lambda e: e.activation(out=Ft[0:4, :], in_=Ft[0:4, :], func=AF.Exp, scale=-1.0), reads=[bg_], writes=[bg_])
            P.op("act", lambda e: e.activation(out=Ft[0:4, :], in_=Ft[0:4, :], func=AF.Ln, bias=1.0), reads=[bg_], writes=[bg_])
            P.op("dve", lambda e: e.tensor_tensor_scan(out=Pt[0:4, :], data0=onesf[0:4, 0:1].to_broadcast([4, So]), data1=Ft[0:4, :], initial=0.0, op0=ALU.mult, op1=ALU.add),
                 reads=[bg_, bconst], writes=[bg_])
            if d == 0:
                P.op("dve", lambda e: e.tensor_scalar(out=Pt[0:4, :], in0=Pt[0:4, :], scalar1=Binit[0], scalar2=None, op0=ALU.add), reads=[bg_, bini], writes=[bg_])
            else:
                P.op("dve", lambda e: e.tensor_tensor(out=tt[0:4, 0:1], in0=Pt[0:4, So - 1:So], in1=Binit[1], op=ALU.add), reads=[bg_, bini], writes=[bsm])
                P.op("dve", lambda e: e.tensor_tensor(out=Pt[0:4, :], in0=Ft[0:4, :], in1=Pt[0:4, :], op=ALU.subtract), reads=[bg_, bsm], writes=[bg_])
                P.op("dve", lambda e: e.tensor_scalar(out=Pt[0:4, :], in0=Pt[0:4, :], scalar1=tt[0:4, 0:1], scalar2=None, op0=ALU.add), reads=[bg_, bsm], writes=[bg_])
            P.op("dve", lambda e: e.tensor_tensor(out=It[0:4, :], in0=It[0:4, :], in1=Pt[0:4, :], op=ALU.add), reads=[bg_], writes=[bg_])
            P.op("dve", lambda e: e.tensor_reduce(out=cm[0:4, :], in_=It[0:4, :].rearrange("p (c t) -> p c t", t=128), axis=AX.X, op=ALU.max), reads=[bg_], writes=[bsm])
            if d == 0:
                P.op("dve", lambda e: e.tensor_tensor_scan(out=Me[0:4, :], data0=cm[0:4, :], data1=cm[0:4, :], initial=Minit[0], op0=ALU.max, op1=ALU.max), reads=[bsm, bini], writes=[bsm])
                if nch > 1:
                    P.op("dve", lambda e: e.tensor_copy(out=Mp[0:4, 1:nch], in_=Me[0:4, 0:nch - 1]), reads=[bsm], writes=[bsm])
                P.op("dve", lambda e: e.tensor_copy(out=Mp[0:4, 0:1], in_=Minit[0]), reads=[bsm, bini], writes=[bsm])
            else:
                src, dst_ = cm, Me2
                sh = 1
                while sh < nch:
                    P.op("dve", lambda e, src=src, dst_=dst_, sh=sh: e.tensor_tensor(out=dst_[0:4, 0:nch - sh], in0=src[0:4, 0:nch - sh], in1=src[0:4, sh:nch], op=ALU.max), reads=[bsm], writes=[bsm])
                    P.op("dve", lambda e, src=src, dst_=dst_, sh=sh: e.tensor_copy(out=dst_[0:4, nch - sh:nch], in_=src[0:4, nch - sh:nch]), reads=[bsm], writes=[bsm])
                    src, dst_ = dst_, src
                    sh *= 2
                P.op("dve", lambda e, src=src: e.tensor_scalar(out=Me[0:4, :], in0=src[0:4, :], scalar1=Minit[1], scalar2=None, op0=ALU.max), reads=[bsm, bini], writes=[bsm])
                if nch > 1:
                    P.op("dve", lambda e: e.tensor_copy(out=Mp[0:4, 0:nch - 1], in_=Me[0:4, 1:nch]), reads=[bsm], writes=[bsm])
                P.op("dve", lambda e: e.tensor_copy(out=Mp[0:4, nch - 1:nch], in_=Minit[1]), reads=[bsm, bini], writes=[bsm])
            P.op("dve", lambda e: e.tensor_tensor(out=Mp[0:4, :], in0=Mp[0:4, :], in1=Me[0:4, :], op=ALU.subtract), reads=[bsm], writes=[bsm])
            P.op("act", lambda e, d=d: e.activation(out=dec[d][0:4, :], in_=Mp[0:4, :], func=AF.Exp), reads=[bsm], writes=[bdec])
            Meb = Me[0:4, :].unsqueeze(2).to_broadcast([4, nch, 128])
            P.op("dve", lambda e, Meb=Meb: e.tensor_tensor(out=Pt[0:4, :].rearrange("p (c t) -> p c t", t=128), in0=Pt[0:4, :].rearrange("p (c t) -> p c t", t=128), in1=Meb, op=ALU.subtract), reads=[bg_, bsm], writes=[bg_])
            P.op("dve", lambda e, Meb=Meb: e.tensor_tensor(out=It[0:4, :].rearrange("p (c t) -> p c t", t=128), in0=It[0:4, :].rearrange("p (c t) -> p c t", t=128), in1=Meb, op=ALU.subtract), reads=[bg_, bsm], writes=[bg_])
            P.op("act", lambda e: e.activation(out=Ft[0:4, :], in_=Pt[0:4, :], func=AF.Exp), reads=[bg_], writes=[bg_])
            P.op("act", lambda e: e.activation(out=It[0:4, :], in_=It[0:4, :], func=AF.Exp, bias=LNSC), reads=[bg_], writes=[bg_])
            for c in range(nch):
                P.op("pe", lambda e, c=c, d=d: e.transpose(PS[:, d * 512 + c * 8:d * 512 + c * 8 + 4], It[0:4, c * 128:(c + 1) * 128], identf[0:4, 0:4]), reads=[bg_, bconst], writes=[bankb[d]])
                P.op("pe", lambda e, c=c, d=d: e.transpose(PS[:, d * 512 + c * 8 + 4:d * 512 + c * 8 + 8], Ft[0:4, c * 128:(c + 1) * 128], identf[0:4, 0:4]), reads=[bg_, bconst], writes=[bankb[d]])
            for hd in range(4):
                P.op("pe", lambda e, hd=hd, d=d: e.matmul(PS[:, 1024 + (d * 4 + hd) * nch:1024 + (d * 4 + hd + 1) * nch], lhsT=self_[0:4, hd * 128:(hd + 1) * 128], rhs=dec[d][0:4, :], start=True, stop=True),
                     reads=[bdec, bconst], writes=[bankb[2]])
        assert nch * 8 <= 512
        for d in range(2):
            P.op("dve", lambda e, d=d: e.tensor_copy(out=UGv[:, :, d * 8:(d + 1) * 8], in_=PS[:, d * 512:d * 512 + nch * 8].rearrange("p (c q) -> p c q", q=8)), reads=[bankb[d]], writes=[bUG])
        P.op("dve", lambda e: e.tensor_copy(out=DECB, in_=PS[:, 1024:1024 + 8 * nch]), reads=[bankb[2]], writes=[bUG])
        barrier_local(m0)
        qTt = ar.bf(So)
        kTt = ar.bf(So)
        ktk = ar.bf(So)
        vau = ar.bf(nch * 257)
        hsum = ar.f32(nch * 256)
        smt = ar.bf(nch * 256)
        yms = ar.bf(2 * So)
        mnb = ar.f32(1024)
        bh = Buf("headdata")
        bhs = Buf("hsum")
        bym = Buf("yms")
        bmn = Buf("mnb")
        P.dma("sp", mnb, mnorm.partition_broadcast(128), writes=[bmn])
        vav = vau.rearrange("p (c f) -> p c f", f=257)
        ktkv = ktk.rearrange("p (c f) -> p c f", f=128)
        hsv = hsum.rearrange("p (c f) -> p c f", f=256)
        smv = smt.rearrange("p (c f) -> p c f", f=256)
        ymv = yms.rearrange("p (i t) -> p i t", i=2)
        ptr = Rot([ar.bf(128) for _ in range(4)], "ptm")
        kur = Rot([ar.bf(128) for _ in range(4)], "kum")
        dr = Rot([ar.f32(2) for _ in range(4)], "den")
        fin = Rot([(ar.f32(6), ar.f32(2), ar.f32(1), ar.f32(256), ar.bf(256)) for _ in range(2)], "fin")
        for hd in range(HM):
            P.dma("sp", qTt, mqT[j][hd], reads=[DB("mqT%d" % j)], writes=[bh])
            P.dma("sp", kTt, mkT[j][hd], reads=[DB("mkT%d" % j)], writes=[bh])
            P.dma("sp", ktkv, mkt[j][:, hd * 128:(hd + 1) * 128].rearrange("(c p) f -> p c f", p=128), reads=[DB("mkt%d1" % j)], writes=[bh])
            P.op("dve", lambda e: e.memset(vau, 1.0), reads=[], writes=[bh])
            P.dma("sp", vav[:, :, 0:256], mvt[j][:, hd * 256:(hd + 1) * 256].rearrange("(c p) f -> p c f", p=128), reads=[DB("mvt%d1" % j)], writes=[bh])
            P.dma("sp", smv, smo[j][:, hd * 256:(hd + 1) * 256].rearrange("(c p) f -> p c f", p=128), reads=[DB("smo%d" % j)], writes=[bh])
            P.op("pool", lambda e: e.memset(hsum, 0.0), writes=[bhs])
            for i in range(nch):
                for d in range(2):
                    c = i if d == 0 else nch - 1 - i
                    bi = d * 4 + hd
                    ci = bi * nch + c
                    msk = maskF if d == 0 else maskB
                    ucol = UGv[:, c, d * 8 + hd:d * 8 + hd + 1]
                    gcol = UGv[:, c, d * 8 + 4 + hd:d * 8 + 4 + hd + 1]
                    cs = slice(c * 128, (c + 1) * 128)
                    bS, bX, bU = 2, 3 + d, 5 + d
                    P.op("pool", lambda e, bi=bi, ci=ci: e.tensor_scalar(out=Cb[bi], in0=Cf[bi], scalar1=DECB[:, ci:ci + 1], scalar2=None, op0=ALU.mult), reads=[bC, bUG], writes=[bC])
                    P.op("pool", lambda e, bi=bi, ci=ci: e.tensor_scalar(out=Cf[bi], in0=Cf[bi], scalar1=DECB[:, ci:ci + 1], scalar2=None, op0=ALU.mult), reads=[bC, bUG], writes=[bC])
                    P.op("pe", lambda e, cs=cs, d=d: e.matmul(PS[:, bS * 512 + d * 128:bS * 512 + (d + 1) * 128], lhsT=kTt[:, cs], rhs=qTt[:, cs], start=True, stop=True), reads=[bh], writes=[bankb[bS]])
                    pt, ptb = ptr.next()
                    P.op("dve", lambda e, pt=pt, d=d, ucol=ucol, msk=msk: e.scalar_tensor_tensor(out=pt, in0=PS[:, bS * 512 + d * 128:bS * 512 + (d + 1) * 128], scalar=ucol, in1=msk, op0=ALU.mult, op1=ALU.mult),
                         reads=[bankb[bS], bUG, bconst], writes=[ptb])
                    ku, kub = kur.next()
                    P.op("pool", lambda e, ku=ku, c=c, ucol=ucol: e.tensor_scalar(out=ku, in0=ktkv[:, c, :], scalar1=ucol, scalar2=None, op0=ALU.mult), reads=[bh, bUG], writes=[kub])
                    P.op("pe", lambda e, pt=pt, c=c, bX=bX: e.matmul(bank(bX, 257), lhsT=pt, rhs=vav[:, c, :], start=True, stop=False), reads=[ptb, bh], writes=[bankb[bX]])
                    P.op("pe", lambda e, cs=cs, bi=bi, bX=bX: e.matmul(bank(bX, 257), lhsT=qTt[:, cs], rhs=Cb[bi], start=False, stop=True), reads=[bh, bC], writes=[bankb[bX]])
                    P.op("pe", lambda e, ku=ku, c=c, bU=bU: e.matmul(bank(bU, 257), lhsT=ku, rhs=vav[:, c, :], start=True, stop=True), reads=[kub, bh], writes=[bankb[bU]])
                    dn, dnb = dr.next()
                    P.op("dve", lambda e, dn=dn, bX=bX: e.tensor_scalar(out=dn[:, 0:1], in0=bank(bX, 257)[:, 256:257], scalar1=-1.0, scalar2=None, op0=ALU.mult), reads=[bankb[bX]], writes=[dnb])
                    P.op("dve", lambda e, dn=dn, bX=bX: e.tensor_tensor(out=dn[:, 0:1], in0=dn[:, 0:1], in1=bank(bX, 257)[:, 256:257], op=ALU.max), reads=[bankb[bX], dnb], writes=[dnb])
                    P.op("dve", lambda e, dn=dn, gcol=gcol: e.tensor_tensor(out=dn[:, 0:1], in0=dn[:, 0:1], in1=gcol, op=ALU.max), reads=[dnb, bUG], writes=[dnb])
                    P.op("dve", lambda e, dn=dn: e.reciprocal(out=dn[:, 1:2], in_=dn[:, 0:1]), reads=[dnb], writes=[dnb])
                    P.op("dve", lambda e, dn=dn, bX=bX, c=c: e.scalar_tensor_tensor(out=hsv[:, c, :], in0=bank(bX, 256), scalar=dn[:, 1:2], in1=hsv[:, c, :], op0=ALU.mult, op1=ALU.add),
                         reads=[bankb[bX], dnb, bhs], writes=[bhs])
                    P.op("dve", lambda e, bi=bi, bU=bU: e.tensor_tensor(out=Cf[bi], in0=Cf[bi], in1=bank(bU, 257), op=ALU.add), reads=[bC, bankb[bU]], writes=[bC])
            for c in range(nch):
                (stats, mv, rs, hn, yb), fb = fin.next()
                P.op("dve", lambda e, stats=stats, c=c: e.bn_stats(out=stats, in_=hsv[:, c, :]), reads=[bhs], writes=[fb])
                P.op("dve", lambda e, stats=stats, mv=mv: e.bn_aggr(out=mv, in_=stats), reads=[fb], writes=[fb])
                P.op("act", lambda e, rs=rs, mv=mv: e.activation(out=rs, in_=mv[:, 1:2], func=AF.Sqrt, bias=EPS), reads=[fb], writes=[fb])
                P.op("dve", lambda e, rs=rs: e.reciprocal(out=rs, in_=rs), reads=[fb], writes=[fb])
                P.op("dve", lambda e, hn=hn, mv=mv, rs=rs, c=c: e.tensor_scalar(out=hn, in0=hsv[:, c, :], scalar1=mv[:, 0:1], scalar2=rs, op0=ALU.subtract, op1=ALU.mult), reads=[bhs, fb], writes=[fb])
                P.op("dve", lambda e, hn=hn, hd=hd: e.tensor_tensor(out=hn, in0=hn, in1=mnb[:, hd * 256:(hd + 1) * 256], op=ALU.mult), reads=[fb, bmn], writes=[fb])
                P.op("dve", lambda e, hn=hn, yb=yb, c=c: e.tensor_tensor(out=yb, in0=hn, in1=smv[:, c, :], op=ALU.mult), reads=[fb, bh], writes=[fb])
                tb_i = 7
                tpb = bankbf(tb_i)
                for i2 in range(2):
                    P.op("pe", lambda e, i2=i2, yb=yb, tpb=tpb: e.transpose(tpb[:, i2 * 128:(i2 + 1) * 128], yb[:, i2 * 128:(i2 + 1) * 128], identb), reads=[fb, bconst], writes=[bankb[tb_i]])
                P.op("act", lambda e, c=c, tpb=tpb: e.activation(out=ymv[:, :, c * 128:(c + 1) * 128], in_=tpb[:, 0:256].rearrange("p (i t) -> p i t", i=2), func=AF.Copy), reads=[bankb[tb_i]], writes=[bym])
            P.dma("pool", ymT[j][2 * hd:2 * hd + 2].rearrange("i p t -> p i t"), ymv, reads=[bym], writes=[DB("ymT%d" % j)])

    def barrier_local(m):
        barrier()
        ar.reset(m)

    def phaseD1(j, wpa, wpm, wo, bw, g1b, l1g, l1b, bD):
        So = SO[j]
        nch = So // 128
        yar = Rot([ar.bf(8 * 128) for _ in range(3)], "ya")
        ymr = Rot([ar.bf(8 * 128) for _ in range(3)], "ym")
        sgr = Rot([ar.bf(2048) for _ in range(3)], "sg")
        xr = Rot([ar.f32(1024) for _ in range(3)], "xd")
        tr_ = Rot([ar.f32(1024) for _ in range(3)], "td")
        mir = Rot([ar.bf(1024) for _ in range(3)], "mi")
        mTr = Rot([ar.bf(1024) for _ in range(3)], "mT")
        rr = Rot([ar.f32(1024) for _ in range(3)], "rd")
        x1r = Rot([ar.f32(1024) for _ in range(3)], "x1")
        xnr = Rot([ar.bf(1024) for _ in range(3)], "xn2")
        str_ = Rot([(ar.f32(12), ar.f32(2), ar.f32(1)) for _ in range(6)], "st2")
        h2r = Rot([ar.bf(1024) for _ in range(3)], "h2")
        P.dma("sp", g1b, adab[j, 0:1, :].partition_broadcast(128), reads=[DB("adab")], writes=[bD])
        wpav = wpa.rearrange("p (k n) -> p k n", k=8)
        wpmv = wpm.rearrange("p (k n) -> p k n", k=8)
        wov = wo.rearrange("p (k n) -> p k n", k=8)
        for t in range(nch):
            tok = t * 128
            ya, yab = yar.next()
            ym, ymb = ymr.next()
            sg, sgb = sgr.next()
            xt, xb_ = xr.next()
            yav = ya.rearrange("p (h t) -> p h t", h=8)
            ymv = ym.rearrange("p (h t) -> p h t", h=8)
            P.dma("sp", yav, yaT[j][:, :, tok:tok + 128].rearrange("h p t -> p h t"), reads=[DB("yaT%d" % j)], writes=[yab])
            P.dma("sp", ymv, ymT[j][:, :, tok:tok + 128].rearrange("h p t -> p h t"), reads=[DB("ymT%d" % j)], writes=[ymb])
            P.dma("sp", sg, sbg[j][tok:tok + 128, :], reads=[DB("sbg%d" % j)], writes=[sgb])
            P.dma("sp", xt, xs[j][tok:tok + 128, :], writes=[xb_])
            for n in range(2):
                for k in range(8):
                    P.op("pe", lambda e, n=n, k=k, yav=yav: e.matmul(bank(n), lhsT=yav[:, k, :], rhs=wpav[:, k, n * 512:(n + 1) * 512], start=(k == 0), stop=(k == 7)), reads=[yab, bw], writes=[bankb[n]])
            for n in range(2):
                for k in range(8):
                    P.op("pe", lambda e, n=n, k=k, ymv=ymv: e.matmul(bank(2 + n), lhsT=ymv[:, k, :], rhs=wpmv[:, k, n * 512:(n + 1) * 512], start=(k == 0), stop=(k == 7)), reads=[ymb, bw], writes=[bankb[2 + n]])
            tm, tmb = tr_.next()
            mi, mib = mir.next()
            P.op("dve", lambda e, tm=tm, sg=sg: e.tensor_tensor(out=tm, in0=PS[:, 0:1024], in1=sg[:, 0:1024], op=ALU.mult), reads=[bankb[0], bankb[1], sgb], writes=[tmb])
            P.op("dve", lambda e, mi=mi, sg=sg: e.tensor_tensor(out=mi, in0=PS[:, 1024:2048], in1=sg[:, 1024:2048], op=ALU.mult), reads=[bankb[2], bankb[3], sgb], writes=[mib])
            P.op("dve", lambda e, mi=mi, tm=tm: e.tensor_tensor(out=mi, in0=mi, in1=tm, op=ALU.add), reads=[tmb, mib], writes=[mib])
            mT, mTb = mTr.next()
            tpb = bankbf(6)
            for k in range(8):
                P.op("pe", lambda e, k=k, mi=mi: e.transpose(tpb[:, k * 128:(k + 1) * 128], mi[:, k * 128:(k + 1) * 128], identb), reads=[mib, bconst], writes=[bankb[6]])
            P.op("act", lambda e, mT=mT: e.activation(out=mT, in_=tpb, func=AF.Copy), reads=[bankb[6]], writes=[mTb])
            mTv = mT.rearrange("p (k t) -> p k t", k=8)
            for n in range(2):
                for k in range(8):
                    P.op("pe", lambda e, n=n, k=k, mTv=mTv: e.matmul(bank(4 + n), lhsT=mTv[:, k, :], rhs=wov[:, k, n * 512:(n + 1) * 512], start=(k == 0), stop=(k == 7)), reads=[mTb, bw], writes=[bankb[4 + n]])
            r_, rb_ = rr.next()
            P.op("dve", lambda e, r_=r_: e.tensor_tensor(out=r_, in0=PS[:, 2048:3072], in1=g1b, op=ALU.mult), reads=[bankb[4], bankb[5], bD], writes=[rb_])
            P.op("dve", lambda e, r_=r_, xt=xt: e.scalar_tensor_tensor(out=r_, in0=xt, scalar=ALPHA_C, in1=r_, op0=ALU.mult, op1=ALU.add), reads=[xb_, rb_], writes=[rb_])
            x1, x1b = x1r.next()
            st, _ = str_.next()
            stats, mv, rs = st
            P.op("dve", lambda e, stats=stats, r_=r_: e.bn_stats(out=stats[:, 0:6], in_=r_[:, 0:512]), reads=[rb_], writes=[x1b])
            P.op("dve", lambda e, stats=stats, r_=r_: e.bn_stats(out=stats[:, 6:12], in_=r_[:, 512:1024]), reads=[rb_], writes=[x1b])
            P.op("dve", lambda e, stats=stats, mv=mv: e.bn_aggr(out=mv, in_=stats), reads=[x1b], writes=[x1b])
            P.op("act", lambda e, rs=rs, mv=mv: e.activation(out=rs, in_=mv[:, 1:2], func=AF.Sqrt, bias=EPS), reads=[x1b], writes=[x1b])
            P.op("dve", lambda e, rs=rs: e.reciprocal(out=rs, in_=rs), reads=[x1b], writes=[x1b])
            P.op("dve", lambda e, x1=x1, r_=r_, mv=mv, rs=rs: e.tensor_scalar(out=x1, in0=r_, scalar1=mv[:, 0:1], scalar2=rs, op0=ALU.subtract, op1=ALU.mult), reads=[rb_, x1b], writes=[x1b])
            P.op("dve", lambda e, x1=x1: e.tensor_tensor(out=x1, in0=x1, in1=l1g, op=ALU.mult), reads=[x1b, bD], writes=[x1b])
            P.op("dve", lambda e, x1=x1: e.tensor_tensor(out=x1, in0=x1, in1=l1b, op=ALU.add), reads=[x1b, bD], writes=[x1b])
            P.dma("pool", x1s[j][tok:tok + 128, :], x1, reads=[x1b], writes=[DB("x1s%d" % j)])
            xn, xnb = xnr.next()
            st2, _ = str_.next()
            h2, h2b = h2r.next()
            h2v = h2.rearrange("p (k t) -> p k t", k=8)
            ln_to_hT(x1, x1b, (h2b, h2v), sc2p1, sh2, j, 7, [], st2, xn, xnb)
            P.dma("pool", h2T[j][:, :, 1 + tok:1 + tok + 128].rearrange("k p t -> p k t"), h2v, reads=[h2b], writes=[DB("h2T%d" % j)])

    def phaseD2(j, wup, wdn, bw, g2b, l2g, l2b, bD):
        So = SO[j]
        TB2 = min(256, So)
        ntb = TB2 // 128
        hr = Rot([ar.bf(8 * (TB2 + 2)) for _ in range(1)], "h2l")
        zr = Rot([ar.bf(NF * TB2) for _ in range(1)], "zT")
        ar_ = Rot([ar.f32(TB2) for _ in range(3)], "ca")
        br_ = Rot([ar.f32(TB2) for _ in range(3)], "cb")
        uvr = Rot([ar.f32(TB2) for _ in range(3)], "uvs")
        x1r = Rot([ar.f32(1024) for _ in range(1)], "x1l")
        rr = Rot([ar.f32(1024) for _ in range(1)], "r2")
        str_ = Rot([(ar.f32(12), ar.f32(2), ar.f32(1)) for _ in range(2)], "st3")
        P.dma("sp", g2b, adab[j, 1:2, :].partition_broadcast(128), reads=[DB("adab")], writes=[bD])
        wupv = wup.rearrange("p (k n) -> p k n", k=8)
        wdnv = wdn.rearrange("p (f n) -> p f n", f=NF)
        GC = 2.0 * math.sqrt(2.0 / math.pi)
        pbi = 0
        for blk in range(So // TB2):
            t0 = blk * TB2
            hh, hb = hr.next()
            hv = hh.rearrange("p (k t) -> p k t", k=8)
            P.dma("sp", hv, h2T[j][:, :, t0:t0 + TB2 + 2].rearrange("k p t -> p k t"), reads=[DB("h2T%d" % j)], writes=[hb])
            z, zb = zr.next()
            zv = z.rearrange("p (f t) -> p f t", f=NF)
            for f in range(NF):
                bg_i = (pbi % 3) * 2
                pbi += 1
                for k in range(8):
                    P.op("pe", lambda e, f=f, k=k, bg_i=bg_i, hv=hv: e.matmul(bank(bg_i, TB2 + 2), lhsT=wupv[:, k, f * 128:(f + 1) * 128], rhs=hv[:, k, :], start=(k == 0), stop=(k == 7)), reads=[bw, hb], writes=[bankb[bg_i]])
                for k in range(8):
                    P.op("pe", lambda e, f=f, k=k, bg_i=bg_i, hv=hv: e.matmul(bank(bg_i + 1, TB2), lhsT=wupv[:, k, DFF + f * 128:DFF + (f + 1) * 128], rhs=hv[:, k, 1:TB2 + 1], start=(k == 0), stop=(k == 7)), reads=[bw, hb], writes=[bankb[bg_i + 1]])
                a, ab = ar_.next()
                b2, bb = br_.next()
                uvs, uvb = uvr.next()
                P.op("act", lambda e, uvs=uvs, bg_i=bg_i: e.activation(out=uvs, in_=bank(bg_i + 1, TB2), func=AF.Copy), reads=[bankb[bg_i + 1]], writes=[uvb])
                ug = bank(bg_i, TB2 + 2)
                P.op("dve", lambda e, a=a, ug=ug, f=f: e.tensor_scalar(out=a, in0=ug[:, 0:TB2], scalar1=cwT[:, 3 * f:3 * f + 1], scalar2=cbT[:, f:f + 1], op0=ALU.mult, op1=ALU.add), reads=[bankb[bg_i], bconst], writes=[ab])
                P.op("dve", lambda e, a=a, ug=ug, f=f: e.scalar_tensor_tensor(out=a, in0=ug[:, 1:TB2 + 1], scalar=cwT[:, 3 * f + 1:3 * f + 2], in1=a, op0=ALU.mult, op1=ALU.add), reads=[bankb[bg_i], bconst, ab], writes=[ab])
                P.op("dve", lambda e, a=a, ug=ug, f=f: e.scalar_tensor_tensor(out=a, in0=ug[:, 2:TB2 + 2], scalar=cwT[:, 3 * f + 2:3 * f + 3], in1=a, op0=ALU.mult, op1=ALU.add), reads=[bankb[bg_i], bconst, ab], writes=[ab])
                P.op("act", lambda e, a=a, b2=b2: e.activation(out=b2, in_=a, func=AF.Square, scale=math.sqrt(0.044715)), reads=[ab], writes=[bb])
                P.op("dve", lambda e, a=a, b2=b2: e.scalar_tensor_tensor(out=b2, in0=b2, scalar=1.0, in1=a, op0=ALU.add, op1=ALU.mult), reads=[ab, bb], writes=[bb])
                P.op("act", lambda e, b2=b2: e.activation(out=b2, in_=b2, func=AF.Sigmoid, scale=GC), reads=[bb], writes=[bb])
                P.op("dve", lambda e, a=a, b2=b2: e.tensor_tensor(out=b2, in0=b2, in1=a, op=ALU.mult), reads=[ab, bb], writes=[bb])
                P.op("dve", lambda e, b2=b2, f=f, uvs=uvs, zv=zv: e.tensor_tensor(out=zv[:, f, :], in0=uvs, in1=b2, op=ALU.mult), reads=[uvb, bb], writes=[zb])
            for ti in range(ntb):
                tok = t0 + ti * 128
                x1, x1b = x1r.next()
                P.dma("sp", x1, x1s[j][tok:tok + 128, :], reads=[DB("x1s%d" % j)], writes=[x1b])
                for n in range(2):
                    for f in range(NF):
                        P.op("pe", lambda e, n=n, f=f, ti=ti, zv=zv: e.matmul(bank(6 + n), lhsT=zv[:, f, ti * 128:(ti + 1) * 128], rhs=wdnv[:, f, n * 512:(n + 1) * 512], start=(f == 0), stop=(f == NF - 1)), reads=[zb, bw], writes=[bankb[6 + n]])
                r_, rb_ = rr.next()
                P.op("dve", lambda e, r_=r_: e.tensor_tensor(out=r_, in0=PS[:, 3072:4096], in1=g2b, op=ALU.mult), reads=[bankb[6], bankb[7], bD], writes=[rb_])
                P.op("dve", lambda e, r_=r_, x1=x1: e.scalar_tensor_tensor(out=r_, in0=x1, scalar=ALPHA_C, in1=r_, op0=ALU.mult, op1=ALU.add), reads=[x1b, rb_], writes=[rb_])
                yo, yob = r_, rb_
                (stats, mv, rs), _ = str_.next()
                P.op("dve", lambda e, stats=stats, r_=r_: e.bn_stats(out=stats[:, 0:6], in_=r_[:, 0:512]), reads=[rb_], writes=[yob])
                P.op("dve", lambda e, stats=stats, r_=r_: e.bn_stats(out=stats[:, 6:12], in_=r_[:, 512:1024]), reads=[rb_], writes=[yob])
                P.op("dve", lambda e, stats=stats, mv=mv: e.bn_aggr(out=mv, in_=stats), reads=[yob], writes=[yob])
                P.op("act", lambda e, rs=rs, mv=mv: e.activation(out=rs, in_=mv[:, 1:2], func=AF.Sqrt, bias=EPS), reads=[yob], writes=[yob])
                P.op("dve", lambda e, rs=rs: e.reciprocal(out=rs, in_=rs), reads=[yob], writes=[yob])
                P.op("dve", lambda e, yo=yo, r_=r_, mv=mv, rs=rs: e.tensor_scalar(out=yo, in0=r_, scalar1=mv[:, 0:1], scalar2=rs, op0=ALU.subtract, op1=ALU.mult), reads=[rb_, yob], writes=[yob])
                P.op("dve", lambda e, yo=yo: e.tensor_tensor(out=yo, in0=yo, in1=l2g, op=ALU.mult), reads=[yob, bD], writes=[yob])
                P.op("dve", lambda e, yo=yo: e.tensor_tensor(out=yo, in0=yo, in1=l2b, op=ALU.add), reads=[yob, bD], writes=[yob])
                P.dma("pool", yout[j][tok:tok + 128, :], yo, reads=[yob], writes=[DB("yout")], is_output=True)

    phase0()
    convert_w_in()
    barrier()
    for j in range(NJ):
        phaseA(j, xs[j], SO[j], True, ropeo[j])
        barrier()
    phaseA(2, xctx, Sc, False, ropec)
    barrier()
    for j in range(NJ):
        phaseB(j)
        barrier()
    for j in range(NJ):
        phaseC(j)
        barrier()
    wpa = ar.bf(8 * 1024)
    wpm = ar.bf(8 * 1024)
    wo = ar.bf(8 * 1024)
    g1b = ar.f32(1024)
    l1g = ar.f32(1024)
    l1b = ar.f32(1024)
    bw = Buf("wD1")
    bD = Buf("bD1")
    for wt_, src in ((wpa, w_pa), (wpm, w_pm), (wo, w_out)):
        P.dma("pool", wt_.rearrange("p (k n) -> p k n", k=8), src.rearrange("(k p) n -> p k n", p=128), writes=[bw])
    P.dma("sp", l1g, ln1g.partition_broadcast(128), writes=[bD])
    P.dma("sp", l1b, ln1b.partition_broadcast(128), writes=[bD])
    zt = ar.bf(16)
    bz = Buf("zt")
    P.op("dve", lambda e: e.memset(zt, 0.0), writes=[bz])
    for j in range(NJ):
        P.dma("sp", h2T[j][:, :, 0:1].rearrange("k p t -> p k t"), zt[:, 0:8].rearrange("p (k t) -> p k t", t=1), reads=[bz], writes=[DB("h2T%d" % j)], slow=True)
        P.dma("sp", h2T[j][:, :, SO[j] + 1:SO[j] + 2].rearrange("k p t -> p k t"), zt[:, 8:16].rearrange("p (k t) -> p k t", t=1), reads=[bz], writes=[DB("h2T%d" % j)], slow=True)
    mD = ar.mark()
    for j in range(NJ):
        phaseD1(j, wpa, wpm, wo, bw, g1b, l1g, l1b, bD)
        barrier()
        ar.reset(mD)
    ar.reset()
    wup = ar.bf(8 * 2 * DFF)
    wdn = ar.bf(NF * 1024)
    g2b = ar.f32(1024)
    l2g = ar.f32(1024)
    l2b = ar.f32(1024)
    bw2 = Buf("wD2")
    bD2 = Buf("bD2")
    wupv = wup.rearrange("p (k n) -> p k n", k=8)
    for k in range(8):
        P.dma("pool", wupv[:, k, :], w_up[k * 128:(k + 1) * 128, :], writes=[bw2])
    P.dma("pool", wdn.rearrange("p (f n) -> p f n", f=NF), w_down.rearrange("(f p) n -> p f n", p=128), writes=[bw2])
    P.dma("sp", l2g, ln2g.partition_broadcast(128), writes=[bD2])
    P.dma("sp", l2b, ln2b.partition_broadcast(128), writes=[bD2])
    mD = ar.mark()
    for j in range(NJ):
        phaseD2(j, wup, wdn, bw2, g2b, l2g, l2b, bD2)
        barrier()
        ar.reset(mD)
    P.emit()
    return nc


def rope_tab(pos):
    inv = (500000.0 ** (-np.arange(0, 16, 2, dtype=np.float32) / np.float32(16))).astype(np.float32)
    ang = pos.astype(np.float32)[:, None] * inv[None, :]
    return np.concatenate([np.cos(ang), np.sin(ang)], axis=1).astype(np.float32)


_NC_CACHE = {}


def kernel(x_prompt, x_sample, c_prompt, c_sample, w_ada, b_ada, w_in, lambda_qk, subln_g,
           b_mgate, mnorm_g, w_pa, w_pm, w_out, ln1_g, ln1_b, w_up, conv_w, conv_b, w_down,
           ln2_g, ln2_b):
    f = np.float32
    x_prompt = np.asarray(x_prompt, f)
    x_sample = np.asarray(x_sample, f)
    Bp, Sc, _ = x_prompt.shape
    Bs, So, _ = x_sample.shape
    ncores = 8
    assert Bs == 2 * ncores and Bp * Sc == ncores * So
    nq = Sc // So
    key = (So, Sc)
    if key not in _NC_CACHE:
        _NC_CACHE[key] = build(So, Sc)
    nc = _NC_CACHE[key]
    c_prompt = np.asarray(c_prompt, f)
    c_sample = np.asarray(c_sample, f)
    tri = np.triu(np.ones((128, 128), f))
    sel = np.zeros((4, 4, 128), f)
    for h in range(4):
        sel[h, h, :] = 1.0
    shared = {
        "w_ada": np.ascontiguousarray(np.asarray(w_ada, f)[0]),
        "b_adaT": np.ascontiguousarray(np.asarray(b_ada, f)[0].reshape(48, 128).T),
        "b_ada": np.ascontiguousarray(np.asarray(b_ada, f)[0].reshape(1, -1)),
        "w_in": np.ascontiguousarray(np.asarray(w_in, f)[0]),
        "lambda_qk": np.ascontiguousarray(np.asarray(lambda_qk, f)[0].reshape(1, 256)),
        "subln": np.ascontiguousarray(np.asarray(subln_g, f)[0].reshape(128, 1)),
        "bmgT": np.ascontiguousarray(np.asarray(b_mgate, f)[0].reshape(4, 4).T),
        "mnorm": np.ascontiguousarray(np.asarray(mnorm_g, f)[0].reshape(1, 1024)),
        "w_pa": np.ascontiguousarray(np.asarray(w_pa, f)[0]),
        "w_pm": np.ascontiguousarray(np.asarray(w_pm, f)[0]),
        "w_out": np.ascontiguousarray(np.asarray(w_out, f)[0]),
        "ln1g": np.ascontiguousarray(np.asarray(ln1_g, f)[0].reshape(1, -1)),
        "ln1b": np.ascontiguousarray(np.asarray(ln1_b, f)[0].reshape(1, -1)),
        "w_up": np.ascontiguousarray(np.asarray(w_up, f)[0]),
        "convwT": np.ascontiguousarray(np.asarray(conv_w, f)[0].reshape(3, NF, 128).transpose(2, 1, 0)),
        "convbT": np.ascontiguousarray(np.asarray(conv_b, f)[0].reshape(NF, 128).T),
        "w_down": np.ascontiguousarray(np.asarray(w_down, f)[0]),
        "ln2g": np.ascontiguousarray(np.asarray(ln2_g, f)[0].reshape(1, -1)),
        "ln2b": np.ascontiguousarray(np.asarray(ln2_b, f)[0].reshape(1, -1)),
        "ident": np.eye(128, dtype=f),
        "sel": sel.reshape(4, 512),
        "maskF": tri,
        "maskB": np.ascontiguousarray(tri.T),
        "ropec": rope_tab(np.arange(Sc)),
    }
    in_maps = []
    offs = []
    So2 = So + 256 if Sc > So + 256 else So
    for c in range(ncores):
        pb, pq = c // nq, c % nq
        m = dict(shared)
        m["xs"] = np.ascontiguousarray(np.stack([x_sample[2 * c], x_sample[2 * c + 1]]))
        st2 = min(max(pq * So - 128, 0), Sc - So2)
        offs.append(pq * So - st2)
        m["xs2"] = np.ascontiguousarray(x_prompt[pb, st2:st2 + So2])
        m["xctx"] = np.ascontiguousarray(x_prompt[pb])
        cj = np.stack([c_sample[2 * c], c_sample[2 * c + 1], c_prompt[pb]])
        m["cT"] = np.ascontiguousarray(cj.reshape(3, 8, 128).transpose(2, 1, 0))
        ro = rope_tab(np.arange(So))
        m["ropeo"] = np.ascontiguousarray(np.stack([ro, ro]))
        m["ropeo2"] = rope_tab(np.arange(st2, st2 + So2))
        pos = np.arange(Sc)
        m["mb"] = np.ascontiguousarray(np.tile((pos < st2).astype(f)[None, :], (4, 1)))
        m["ma"] = np.ascontiguousarray(np.tile((pos >= st2 + So2).astype(f)[None, :], (4, 1)))
        in_maps.append(m)
    res = run_bass_kernel_spmd(nc, in_maps, core_ids=list(range(ncores)))
    y_prompt = np.zeros((Bp, Sc, D), f)
    y_sample = np.zeros((Bs, So, D), f)
    for c in range(ncores):
        y = res.results[c]["y"]
        y2 = res.results[c]["y2"]
        pb, pq = c // nq, c % nq
        y_sample[2 * c] = y[0]
        y_sample[2 * c + 1] = y[1]
        y_prompt[pb, pq * So:(pq + 1) * So] = y2[offs[c]:offs[c] + So]
    return (y_prompt, y_sample)
```

```python
import concourse.bass as bass
import concourse.mybir as mybir

F32 = mybir.dt.float32
BF16 = mybir.dt.bfloat16
AF = mybir.ActivationFunctionType
ALU = mybir.AluOpType
AX = mybir.AxisListType

COMPUTE = ("pe", "act", "dve", "pool")
QUEUES = ("sp", "act", "pool")
N_DMA_SEMS = 24


class Buf:
    __slots__ = ("name", "writers", "readers")

    def __init__(self, name=""):
        self.name = name
        self.writers = {}
        self.readers = {}


class Op:
    __slots__ = ("eng", "fn", "deps", "sig", "sem", "val", "idx", "is_dma", "key", "pre")

    def __init__(self, eng, fn):
        self.eng = eng
        self.fn = fn
        self.deps = []
        self.sig = False
        self.sem = None
        self.val = None
        self.idx = None
        self.is_dma = False
        self.key = eng
        self.pre = None


class Prog:
    def __init__(self, nc):
        self.nc = nc
        self.streams = {e: [] for e in ("pe", "act", "dve", "pool", "sp")}
        self.dma_count = [0] * N_DMA_SEMS
        self.dma_rr = 0
        self.out_dmas = []
        self.fence = []
        self.last_dma = [None] * N_DMA_SEMS

    def _add(self, op, reads, writes):
        st = self.streams[op.eng]
        op.idx = len(st)
        deps = list(self.fence)
        for b in reads:
            for k, w in b.writers.items():
                deps.append(w)
        for b in writes:
            for k, r in b.readers.items():
                deps.append(r)
            for k, w in b.writers.items():
                deps.append(w)
        seen = set()
        for d in deps:
            if d is op or id(d) in seen:
                continue
            seen.add(id(d))
            if (not d.is_dma) and d.eng == op.eng:
                if not op.is_dma:
                    if op.eng == "pe":
                        continue
                    if op.idx - d.idx > 3:
                        continue
            op.deps.append(d)
        for b in reads:
            b.readers[op.key] = op
        for b in writes:
            b.writers[op.key] = op
        st.append(op)
        return op

    def op(self, eng, fn, reads=(), writes=()):
        return self._add(Op(eng, fn), reads, writes)

    def dma(self, q, out, in_, reads=(), writes=(), is_output=False, slow=False):
        k = self.dma_rr
        self.dma_rr = (self.dma_rr + 1) % N_DMA_SEMS
        self.dma_count[k] += 1
        n = self.dma_count[k]
        o = Op(q, (lambda e, out=out, in_=in_: e.dma_start(out=out, in_=in_, allow_slow_non_contiguous=True)) if slow else (lambda e, out=out, in_=in_: e.dma_start(out=out, in_=in_)))
        o.is_dma = True
        o.key = ("dma", k)
        o.sem = k
        o.val = 16 * n
        o.pre = (k, 16 * (n - 1))
        self._add(o, reads, writes)
        self.last_dma[k] = o
        if is_output:
            self.out_dmas.append(o)
        return o

    def emit(self):
        nc = self.nc
        import contextlib
        with contextlib.ExitStack() as es:
            esem = {e: es.enter_context(nc.semaphore("sem_" + e)) for e in COMPUTE}
            dsem = [es.enter_context(nc.semaphore("dsem%d" % i)) for i in range(N_DMA_SEMS)]
            for e, st in self.streams.items():
                for o in st:
                    for d in o.deps:
                        if not d.is_dma:
                            d.sig = True
            for e, st in self.streams.items():
                c = 0
                for o in st:
                    if o.is_dma:
                        o.sem = dsem[o.sem] if isinstance(o.sem, int) else o.sem
                        continue
                    o.sem = esem[e]
                    if o.sig:
                        c += 1
                        o.val = c
            block = es.enter_context(nc.Block())
            engmap = {"pe": "tensor", "act": "scalar", "dve": "vector", "pool": "gpsimd", "sp": "sync"}
            out_dmas = self.out_dmas

            def make(ename, st, final):
                def body(eng):
                    waited = {}
                    for o in st:
                        if o.pre is not None:
                            s = dsem[o.pre[0]]
                            v = o.pre[1]
                            if v > 0 and waited.get(id(s), 0) < v:
                                eng.wait_ge(s, v)
                                waited[id(s)] = v
                        for d in o.deps:
                            if waited.get(id(d.sem), 0) < d.val:
                                eng.wait_ge(d.sem, d.val)
                                waited[id(d.sem)] = d.val
                        ins = o.fn(eng)
                        if o.is_dma:
                            ins.then_inc(o.sem, 16)
                        elif o.sig:
                            ins.then_inc(o.sem, 1)
                    if final:
                        for o in out_dmas:
                            if waited.get(id(o.sem), 0) < o.val:
                                eng.wait_ge(o.sem, o.val)
                                waited[id(o.sem)] = o.val
                return body

            for e, st in self.streams.items():
                deco = getattr(block, engmap[e])
                deco(make(e, st, e == "sp"))

import math
import numpy as np
import ml_dtypes
from concourse.bass_utils import run_bass_kernel_spmd

D = 1024
HA = 8
HM = 4
DFF = 2816
NF = 22
DIN = 8208
EPS = 1e-5
ALPHA_C = 2.0 ** 0.25
LAM_INIT = 0.8 - 0.6 * math.exp(-0.3 * 0)
LNSC = -0.5 * math.log(128.0)
NEG = -1.0e4
COLS = dict(aq=(0, 1024), ak=(1024, 2048), av=(2048, 3072), mq=(3072, 3584), mk=(3584, 4096),
            mv=(4096, 5120), mo=(5120, 6144), mg=(6144, 6160), bg=(6160, 8208))
ARENA_ELEMS = 94208


class Arena:
    def __init__(self, A, total):
        self.A = A
        self.lo = 0
        self.hi = total
        self.base = 0

    def bf(self, n, persist=False):
        n2 = (n + 15) // 16 * 16
        if persist:
            self.hi -= n2
            off = self.hi
        else:
            off = self.lo
            self.lo += n2
        assert self.lo <= self.hi, ("arena overflow", self.lo, self.hi)
        return self.A[:, off:off + n]

    def f32(self, n, persist=False):
        return self.bf(2 * n, persist).bitcast(F32)

    def mark(self):
        return self.lo

    def reset(self, m=0):
        self.lo = m


class Rot:
    def __init__(self, aps, name="r"):
        self.aps = aps
        self.bufs = [Buf(name + str(i)) for i in range(len(aps))]
        self.i = 0

    def next(self):
        k = self.i % len(self.aps)
        self.i += 1
        return self.aps[k], self.bufs[k]


def build(So, Sc):
    nc = bass.Bass("TRN2", target_bir_lowering=False)
    NJ = 3
    So2 = So + 256 if Sc > So + 256 else So
    SO = [So, So, So2]
    nchc = Sc // 128

    def din(name, shape, dt=F32):
        return nc.dram_tensor(name, list(shape), dt, kind="ExternalInput").ap()

    def dscr(name, shape, dt=BF16):
        return nc.dram_tensor(name, list(shape), dt).ap()

    xs01 = din("xs", [2, So, D])
    xs2 = din("xs2", [So2, D])
    xs = [xs01[0], xs01[1], xs2]
    xctx = din("xctx", [Sc, D])
    cT = din("cT", [128, 8, NJ])
    ropeo01 = din("ropeo", [2, So, 16])
    ropeo2 = din("ropeo2", [So2, 16])
    ropeo = [ropeo01[0], ropeo01[1], ropeo2]
    ropec = din("ropec", [Sc, 16])
    mb_in = din("mb", [4, Sc])
    ma_in = din("ma", [4, Sc])
    w_ada = din("w_ada", [D, 6 * D])
    b_adaT = din("b_adaT", [128, 48])
    b_ada = din("b_ada", [1, 6 * D])
    w_in = din("w_in", [D, DIN])
    lambda_qk = din("lambda_qk", [1, 256])
    subln = din("subln", [128, 1])
    bmgT = din("bmgT", [4, 4])
    mnorm = din("mnorm", [1, 1024])
    w_pa = din("w_pa", [D, D])
    w_pm = din("w_pm", [D, D])
    w_out = din("w_out", [D, D])
    ln1g = din("ln1g", [1, D])
    ln1b = din("ln1b", [1, D])
    w_up = din("w_up", [D, 2 * DFF])
    convwT = din("convwT", [128, NF, 3])
    convbT = din("convbT", [128, NF])
    w_down = din("w_down", [DFF, D])
    ln2g = din("ln2g", [1, D])
    ln2b = din("ln2b", [1, D])
    ident_in = din("ident", [128, 128])
    sel_in = din("sel", [4, 512])
    maskF_in = din("maskF", [128, 128])
    maskB_in = din("maskB", [128, 128])
    yout01 = nc.dram_tensor("y", [2, So, D], F32, kind="ExternalOutput").ap()
    yout2 = nc.dram_tensor("y2", [So2, D], F32, kind="ExternalOutput").ap()
    yout = [yout01[0], yout01[1], yout2]

    Sctx = [So, So, Sc]
    QT = [dscr("QT%d" % j, [HA, 128, SO[j]]) for j in range(NJ)]
    KT = [dscr("KT%d" % j, [HA, 128, Sctx[j]]) for j in range(NJ)]
    VV = [dscr("VV%d" % j, [HA, Sctx[j], 128]) for j in range(NJ)]
    mqT = [dscr("mqT%d" % j, [HM, 128, SO[j]]) for j in range(NJ)]
    mkT = [dscr("mkT%d" % j, [HM, 128, SO[j]]) for j in range(NJ)]
    mkt = [dscr("mkt%d" % j, [SO[j], 512]) for j in range(NJ)]
    mvt = [dscr("mvt%d" % j, [SO[j], 1024]) for j in range(NJ)]
    mktc = dscr("mktc", [Sc, 512])
    mvtc = dscr("mvtc", [Sc, 1024])
    GG = [dscr("GG%d" % j, [16, SO[j]], F32) for j in range(NJ)]
    GGc = dscr("GGc", [16, Sc], F32)
    AMs = dscr("AMs", [2, 4, Sc], F32)
    smo = [dscr("smo%d" % j, [SO[j], 1024]) for j in range(NJ)]
    sbg = [dscr("sbg%d" % j, [SO[j], 2048]) for j in range(NJ)]
    yaT = [dscr("yaT%d" % j, [HA, 128, SO[j]]) for j in range(NJ)]
    ymT = [dscr("ymT%d" % j, [8, 128, SO[j]]) for j in range(NJ)]
    x1s = [dscr("x1s%d" % j, [SO[j], D], F32) for j in range(NJ)]
    h2T = [dscr("h2T%d" % j, [8, 128, SO[j] + 2]) for j in range(NJ)]
    adab = dscr("adab", [NJ, 2, D], F32)
    wb_in = dscr("wb_in", [128, 8, DIN])

    P = Prog(nc)
    A_ = nc.alloc_sbuf_tensor("arena", [128, ARENA_ELEMS], BF16).ap()
    PS = nc.alloc_psum_tensor("psum", [128, 4096], F32).ap()
    ar = Arena(A_, ARENA_ELEMS)
    bankb = [Buf("bank%d" % i) for i in range(8)]

    def bank(i, n=512, off=0):
        return PS[:, i * 512 + off:i * 512 + off + n]

    def bankbf(i):
        return PS[:, i * 512:(i + 1) * 512].bitcast(BF16)

    dbuf = {}

    def DB(name):
        if name not in dbuf:
            dbuf[name] = Buf(name)
        return dbuf[name]

    identf = ar.f32(128, True)
    identb = ar.bf(128, True)
    maskF = ar.bf(128, True)
    maskB = ar.bf(128, True)
    onesb = ar.bf(128, True)
    onesf = ar.f32(128, True)
    self_ = ar.f32(512, True)
    sc1p1 = ar.f32(NJ * 8, True)
    sh1 = ar.f32(NJ * 8, True)
    sc2p1 = ar.f32(NJ * 8, True)
    sh2 = ar.f32(NJ * 8, True)
    small = ar.f32(64, True)
    lamc = small[:, 0:1]
    nlam = small[:, 1:2]
    gsc = small[:, 2:3]
    sublc = small[:, 3:4]
    bmg = small[0:4, 8:12]
    zero4 = small[0:4, 12:13]
    cwT = ar.f32(NF * 3, True)
    cbT = ar.f32(NF, True)
    bconst = Buf("const")
    tmpf = ar.f32(128)
    tmpm = ar.f32(256)
    P.dma("sp", identf, ident_in, writes=[bconst])
    P.dma("sp", tmpm[:, 0:128], maskF_in, writes=[bconst])
    P.dma("sp", tmpm[:, 128:256], maskB_in, writes=[bconst])
    P.dma("sp", self_[0:4, :], sel_in, writes=[bconst])
    P.dma("sp", sublc, subln, writes=[bconst])
    P.dma("sp", bmg, bmgT, writes=[bconst])
    P.dma("sp", cwT, convwT.rearrange("p f k -> p (f k)"), writes=[bconst])
    P.dma("sp", cbT, convbT, writes=[bconst])
    P.op("dve", lambda e: e.tensor_copy(out=identb, in_=identf), reads=[bconst], writes=[bconst])
    P.op("dve", lambda e: e.tensor_copy(out=maskF, in_=tmpm[:, 0:128]), reads=[bconst], writes=[bconst])
    P.op("dve", lambda e: e.tensor_copy(out=maskB, in_=tmpm[:, 128:256]), reads=[bconst], writes=[bconst])
    P.op("dve", lambda e: e.memset(onesb, 1.0), writes=[bconst])
    P.op("dve", lambda e: e.memset(onesf, 1.0), writes=[bconst])
    P.op("dve", lambda e: e.memset(zero4, 0.0), writes=[bconst])
    lq = ar.f32(256)
    lt = ar.f32(128)
    l2 = ar.f32(2)
    P.dma("sp", lq, lambda_qk.partition_broadcast(128), writes=[bconst])
    lqv = lq.rearrange("p (a b d) -> p a b d", a=2, b=2)
    P.op("dve", lambda e: e.tensor_tensor(out=lt.rearrange("p (a d) -> p a d", a=2), in0=lqv[:, :, 0, :], in1=lqv[:, :, 1, :], op=ALU.mult), reads=[bconst], writes=[bconst])
    P.op("dve", lambda e: e.tensor_reduce(out=l2, in_=lt.rearrange("p (a d) -> p a d", a=2), axis=AX.X, op=ALU.add), reads=[bconst], writes=[bconst])
    P.op("act", lambda e: e.activation(out=l2, in_=l2, func=AF.Exp), reads=[bconst], writes=[bconst])
    P.op("dve", lambda e: e.tensor_tensor(out=lamc, in0=l2[:, 0:1], in1=l2[:, 1:2], op=ALU.subtract), reads=[bconst], writes=[bconst])
    P.op("dve", lambda e: e.tensor_scalar(out=nlam, in0=lamc, scalar1=LAM_INIT, scalar2=-1.0, op0=ALU.add, op1=ALU.mult), reads=[bconst], writes=[bconst])
    P.op("dve", lambda e: e.tensor_scalar(out=gsc, in0=sublc, scalar1=(1.0 - LAM_INIT), scalar2=None, op0=ALU.mult), reads=[bconst], writes=[bconst])

    bar_bufs = {e: Buf("bar_" + e) for e in COMPUTE}
    bar_t = ar.f32(8, True)
    bar_tb = ar.bf(16, True)

    def barrier():
        s1 = []
        s1.append(P.op("dve", lambda e: e.memset(bar_t[0:1, 0:1], 0.0), writes=[bar_bufs["dve"]]))
        s1.append(P.op("pool", lambda e: e.memset(bar_t[0:1, 2:3], 0.0), writes=[bar_bufs["pool"]]))
        s1.append(P.op("act", lambda e: e.activation(out=bar_t[0:1, 4:5], in_=identf[0:1, 0:1], func=AF.Copy), writes=[bar_bufs["act"]]))
        s1.append(P.op("pe", lambda e: e.matmul(PS[0:1, 7 * 512:7 * 512 + 1], lhsT=identb[0:1, 0:1], rhs=identb[0:1, 0:1], start=True, stop=True),
                       reads=[bconst], writes=[bar_bufs["pe"], bankb[7]]))
        P.fence = list(s1) + [o for o in P.last_dma if o is not None]
        ar.reset()

    def convert_w_in():
        for k in range(8):
            P.dma("pool", wb_in[:, k, :], w_in[k * 128:(k + 1) * 128, :], writes=[DB("wb_in")])

    def phase0():
        cTt = ar.f32(8 * NJ)
        scb = ar.bf(8 * NJ)
        b0 = Buf("p0")
        P.dma("sp", cTt, cT.rearrange("p k j -> p (k j)"), writes=[b0])
        P.op("act", lambda e: e.activation(out=scb, in_=cTt, func=AF.Silu), reads=[b0], writes=[b0])
        scbv = scb.rearrange("p (k j) -> p k j", j=NJ)
        badT = ar.f32(48)
        P.dma("sp", badT, b_adaT, writes=[b0])
        wr = Rot([ar.bf(8 * 512) for _ in range(2)], "wada")
        rowt = Rot([ar.f32(512) for _ in range(2)], "rowt")
        brow = Rot([ar.f32(512) for _ in range(2)], "brow")
        dst = {0: (sh1, 0.0), 1: (sc1p1, 1.0), 3: (sh2, 0.0), 4: (sc2p1, 1.0)}
        for gi in range(12):
            wt, wb = wr.next()
            wv = wt.rearrange("p (k n) -> p k n", k=8)
            P.dma("pool", wv, w_ada[:, gi * 512:(gi + 1) * 512].rearrange("(k p) n -> p k n", p=128), writes=[wb])
            v = gi // 2
            if v in dst:
                tgt, addc = dst[v]
                for q in range(4):
                    kc = (gi % 2) * 4 + q
                    for k in range(8):
                        P.op("pe", lambda e, k=k, q=q, wv=wv: e.matmul(bank(0, NJ, 0), lhsT=wv[:, k, q * 128:(q + 1) * 128], rhs=scbv[:, k, :], start=(k == 0), stop=(k == 7)),
                             reads=[wb, b0], writes=[bankb[0]])
                    tv = tgt.rearrange("p (j k) -> p j k", j=NJ)[:, :, kc]
                    col = v * 8 + kc
                    P.op("dve", lambda e, tv=tv, col=col, addc=addc: e.tensor_scalar(out=tv, in0=bank(0, NJ, 0), scalar1=badT[:, col:col + 1], scalar2=addc, op0=ALU.add, op1=ALU.add),
                         reads=[bankb[0], b0], writes=[bconst])
            else:
                gidx = 0 if v == 2 else 1
                half = gi % 2
                bt, bb = brow.next()
                P.dma("sp", bt[0:1, :], b_ada[:, gi * 512:(gi + 1) * 512], writes=[bb])
                for j in range(NJ):
                    for k in range(8):
                        P.op("pe", lambda e, k=k, j=j, wv=wv: e.matmul(bank(1)[0:1, :], lhsT=scbv[:, k, j:j + 1], rhs=wv[:, k, :], start=(k == 0), stop=(k == 7)),
                             reads=[wb, b0], writes=[bankb[1]])
                    rt, rb = rowt.next()
                    P.op("dve", lambda e, rt=rt, bt=bt: e.tensor_tensor(out=rt[0:1, :], in0=bank(1)[0:1, :], in1=bt[0:1, :], op=ALU.add), reads=[bankb[1], bb], writes=[rb])
                    P.dma("sp", adab[j, gidx:gidx + 1, half * 512:(half + 1) * 512], rt[0:1, :], reads=[rb], writes=[DB("adab")])

    def ln_to_hT(xt, xb_, hT_dst, scp, shp, j, tr_bank, extra_reads, st, xn, xnb):
        stats, mv, rs = st
        P.op("dve", lambda e: e.bn_stats(out=stats[:, 0:6], in_=xt[:, 0:512]), reads=[xb_] + extra_reads, writes=[xnb])
        P.op("dve", lambda e: e.bn_stats(out=stats[:, 6:12], in_=xt[:, 512:1024]), reads=[xb_], writes=[xnb])
        P.op("dve", lambda e: e.bn_aggr(out=mv, in_=stats), reads=[xnb], writes=[xnb])
        P.op("act", lambda e: e.activation(out=rs, in_=mv[:, 1:2], func=AF.Sqrt, bias=EPS), reads=[xnb], writes=[xnb])
        P.op("dve", lambda e: e.reciprocal(out=rs, in_=rs), reads=[xnb], writes=[xnb])
        P.op("dve", lambda e: e.tensor_scalar(out=xn, in0=xt, scalar1=mv[:, 0:1], scalar2=rs, op0=ALU.subtract, op1=ALU.mult), reads=[xb_, xnb], writes=[xnb])
        pb = bankbf(tr_bank)
        for k in range(8):
            P.op("pe", lambda e, k=k: e.transpose(pb[:, k * 128:(k + 1) * 128], xn[:, k * 128:(k + 1) * 128], identb), reads=[xnb, bconst], writes=[bankb[tr_bank]])
        for k in range(8):
            P.op("act", lambda e, k=k: e.activation(out=hT_dst[1][:, k, :], in_=pb[:, k * 128:(k + 1) * 128], func=AF.Identity,
                                                    scale=scp[:, j * 8 + k:j * 8 + k + 1], bias=shp[:, j * 8 + k:j * 8 + k + 1]),
                 reads=[bankb[tr_bank], bconst], writes=[hT_dst[0]])

    def phaseA(j, xsrc, S, own, ropetab):
        TB = min(1024, S)
        nt = TB // 128
        xr = Rot([ar.f32(1024) for _ in range(2)], "x")
        xnr = Rot([ar.bf(1024) for _ in range(2)], "xn")
        str_ = Rot([(ar.f32(12), ar.f32(2), ar.f32(1)) for _ in range(2)], "st")
        hTr = Rot([ar.bf(8 * TB) for _ in range(2)], "hT")
        wr = Rot([ar.bf(8 * 512) for _ in range(4)], "w")
        stq = Rot([ar.bf(4 * TB) for _ in range(2)], "stq")
        stt = Rot([ar.bf(nt * 512) for _ in range(3)], "stt")
        rpb = Rot([ar.bf(512) for _ in range(2)], "rpb")
        rtm = Rot([ar.f32(4 * 64) for _ in range(2)], "rtm")
        ropr = Rot([ar.f32(nt * 16) for _ in range(2)], "rop")
        gst = Rot([ar.f32(4 * TB) for _ in range(2)], "gst")
        pbank = [2, 3, 4, 5]
        pbi = [0]
        tbk = [6, 7]
        tbi = [0]
        groups = []
        if own and j == 2:
            names = ["aq", "mq", "mk", "mv", "mo", "mg", "bg"]
        elif own:
            names = ["aq", "ak", "av", "mq", "mk", "mv", "mo", "mg", "bg"]
        else:
            names = ["ak", "av", "mk", "mv", "mg"]
        for nm in names:
            lo, hi = COLS[nm]
            ng = max(1, (hi - lo) // 512)
            for g in range(ng):
                groups.append((nm, g, lo + g * 512, min(512, hi - lo)))
        Kd = KT[j]
        Vd = VV[j]
        for blk in range((S + TB - 1) // TB):
            t0 = blk * TB
            tb = min(TB, S - t0)
            nt = tb // 128
            HB = min(512, tb)
            assert tb % HB == 0
            hT, hTb = hTr.next()
            hTv = hT[:, 0:8 * tb].rearrange("p (k t) -> p k t", k=8)
            rop, ropb = ropr.next()
            ropv = rop[:, 0:nt * 16].rearrange("p (t c) -> p t c", c=16)
            P.dma("sp", ropv, ropetab[t0:t0 + tb, :].rearrange("(t p) c -> p t c", p=128), writes=[ropb])
            for ti in range(nt):
                xt, xb_ = xr.next()
                xn, xnb = xnr.next()
                st, _ = str_.next()
                P.dma("sp", xt, xsrc[t0 + ti * 128:t0 + (ti + 1) * 128, :], writes=[xb_])
                ln_to_hT(xt, xb_, (hTb, hTv[:, :, ti * 128:(ti + 1) * 128]), sc1p1, sh1, j, ti % 2, [], st, xn, xnb)
            for (nm, g, c0, ncol) in groups:
                wt, wb = wr.next()
                wv = wt.rearrange("p (k n) -> p k n", k=8)
                P.dma("sp", wv[:, :, 0:ncol], wb_in[:, :, c0:c0 + ncol], reads=[DB("wb_in")], writes=[wb])
                if nm in ("aq", "ak", "av", "mk", "mv", "mo", "bg"):
                    if nm in ("aq", "ak"):
                        sq, sqb = stq.next()
                        sqv = sq[:, 0:4 * tb].rearrange("p (h t) -> p h t", h=4)
                    else:
                        sblk, sb_ = stt.next()
                        sv = sblk[:, 0:nt * 512].rearrange("p (t c) -> p t c", c=512)
                    for ti in range(nt):
                        pb_i = pbank[pbi[0] % 4]
                        pbi[0] += 1
                        for k in range(8):
                            P.op("pe", lambda e, k=k, ti=ti, pb_i=pb_i, wv=wv, hTv=hTv: e.matmul(bank(pb_i), lhsT=hTv[:, k, ti * 128:(ti + 1) * 128], rhs=wv[:, k, :], start=(k == 0), stop=(k == 7)),
                                 reads=[hTb, wb], writes=[bankb[pb_i]])
                        tok = t0 + ti * 128
                        if nm in ("aq", "ak"):
                            rb_, rbb = rpb.next()
                            tm, tmb = rtm.next()
                            psv = bank(pb_i).rearrange("p (h d) -> p h d", h=8)
                            rv = rb_.rearrange("p (h d) -> p h d", h=8)
                            tmv = tm.rearrange("p (a h d) -> p a h d", a=4, h=8)
                            cosb = ropv[:, ti, 0:8].unsqueeze(1).to_broadcast([128, 8, 8])
                            sinb = ropv[:, ti, 8:16].unsqueeze(1).to_broadcast([128, 8, 8])
                            rd = [bankb[pb_i], ropb]
                            P.op("dve", lambda e, psv=psv, tmv=tmv, cosb=cosb: e.tensor_tensor(out=tmv[:, 0], in0=psv[:, :, 0:8], in1=cosb, op=ALU.mult), reads=rd, writes=[tmb])
                            P.op("dve", lambda e, psv=psv, tmv=tmv, sinb=sinb: e.tensor_tensor(out=tmv[:, 1], in0=psv[:, :, 8:16], in1=sinb, op=ALU.mult), reads=rd, writes=[tmb])
                            P.op("dve", lambda e, psv=psv, tmv=tmv, cosb=cosb: e.tensor_tensor(out=tmv[:, 2], in0=psv[:, :, 8:16], in1=cosb, op=ALU.mult), reads=rd, writes=[tmb])
                            P.op("dve", lambda e, psv=psv, tmv=tmv, sinb=sinb: e.tensor_tensor(out=tmv[:, 3], in0=psv[:, :, 0:8], in1=sinb, op=ALU.mult), reads=rd, writes=[tmb])
                            P.op("dve", lambda e, rv=rv, psv=psv: e.tensor_copy(out=rv[:, :, 16:64], in_=psv[:, :, 16:64]), reads=rd, writes=[rbb])
                            P.op("dve", lambda e, rv=rv, tmv=tmv: e.tensor_tensor(out=rv[:, :, 0:8], in0=tmv[:, 0], in1=tmv[:, 1], op=ALU.subtract), reads=[tmb], writes=[rbb])
                            P.op("dve", lambda e, rv=rv, tmv=tmv: e.tensor_tensor(out=rv[:, :, 8:16], in0=tmv[:, 2], in1=tmv[:, 3], op=ALU.add), reads=[tmb], writes=[rbb])
                            tb_i = tbk[tbi[0] % 2]
                            tbi[0] += 1
                            tpb = bankbf(tb_i)
                            for hh in range(4):
                                P.op("pe", lambda e, hh=hh, tpb=tpb, rb_=rb_: e.transpose(tpb[:, hh * 128:(hh + 1) * 128], rb_[:, hh * 128:(hh + 1) * 128], identb),
                                     reads=[rbb, bconst], writes=[bankb[tb_i]])
                            P.op("act", lambda e, sqv=sqv, tpb=tpb, ti=ti: e.activation(out=sqv[:, :, ti * 128:(ti + 1) * 128], in_=tpb[:, 0:512].rearrange("p (h t) -> p h t", h=4), func=AF.Copy),
                                 reads=[bankb[tb_i]], writes=[sqb])
                        else:
                            s_ = sv[:, ti, :]
                            if nm in ("mo", "bg"):
                                P.op("act", lambda e, s_=s_, pb_i=pb_i: e.activation(out=s_, in_=bank(pb_i), func=AF.Sigmoid), reads=[bankb[pb_i]], writes=[sb_])
                            else:
                                P.op("act", lambda e, s_=s_, pb_i=pb_i: e.activation(out=s_, in_=bank(pb_i), func=AF.Copy), reads=[bankb[pb_i]], writes=[sb_])
                    if nm == "av":
                        for hh in range(4):
                            P.dma("pool", Vd[4 * g + hh, t0:t0 + tb, :].rearrange("(t p) d -> p t d", p=128), sv[:, :, hh * 128:(hh + 1) * 128], reads=[sb_], writes=[DB("VV%d" % j)])
                    elif nm == "mv":
                        dd = mvt[j] if own else mvtc
                        P.dma("pool", dd[t0:t0 + tb, g * 512:(g + 1) * 512].rearrange("(t p) c -> p t c", p=128), sv, reads=[sb_], writes=[DB("mvt%d%d" % (j, own))])
                    elif nm == "mk":
                        dd = mkt[j] if own else mktc
                        P.dma("pool", dd[t0:t0 + tb, :].rearrange("(t p) c -> p t c", p=128), sv, reads=[sb_], writes=[DB("mkt%d%d" % (j, own))])
                    elif nm == "mo":
                        P.dma("pool", smo[j][t0:t0 + tb, g * 512:(g + 1) * 512].rearrange("(t p) c -> p t c", p=128), sv, reads=[sb_], writes=[DB("smo%d" % j)])
                    elif nm == "bg":
                        P.dma("pool", sbg[j][t0:t0 + tb, g * 512:(g + 1) * 512].rearrange("(t p) c -> p t c", p=128), sv, reads=[sb_], writes=[DB("sbg%d" % j)])
                    if nm == "aq":
                        P.dma("pool", QT[j][4 * g:4 * g + 4, :, t0:t0 + tb].rearrange("h p t -> p h t"), sqv, reads=[sqb], writes=[DB("QT%d" % j)])
                    elif nm == "ak":
                        P.dma("pool", Kd[4 * g:4 * g + 4, :, t0:t0 + tb].rearrange("h p t -> p h t"), sqv, reads=[sqb], writes=[DB("KT%d" % j)])
                if (nm == "mq") or (nm == "mk" and own):
                    sq, sqb = stq.next()
                    sqv = sq[:, 0:4 * tb].rearrange("p (h t) -> p h t", h=4)
                    for hd in range(4):
                        for hf in range(tb // HB):
                            pb_i = pbank[pbi[0] % 4]
                            pbi[0] += 1
                            for k in range(8):
                                P.op("pe", lambda e, HB=HB, k=k, hd=hd, hf=hf, pb_i=pb_i, wv=wv, hTv=hTv: e.matmul(bank(pb_i, HB), lhsT=wv[:, k, hd * 128:(hd + 1) * 128], rhs=hTv[:, k, hf * HB:(hf + 1) * HB], start=(k == 0), stop=(k == 7)),
                                     reads=[hTb, wb], writes=[bankb[pb_i]])
                            P.op("dve", lambda e, HB=HB, hd=hd, hf=hf, pb_i=pb_i, sqv=sqv: e.tensor_copy(out=sqv[:, hd, hf * HB:(hf + 1) * HB], in_=bank(pb_i, HB)), reads=[bankb[pb_i]], writes=[sqb])
                    dd = mqT[j] if nm == "mq" else mkT[j]
                    P.dma("pool", dd[:, :, t0:t0 + tb].rearrange("h p t -> p h t"), sqv, reads=[sqb], writes=[DB(("mqT%d" if nm == "mq" else "mkT%d") % j)])
                if nm == "mg":
                    gs, gsb = gst.next()
                    gsv = gs[:, 0:4 * tb].rearrange("p (r t) -> p r t", r=4)
                    for r in range(4):
                        for hf in range(tb // HB):
                            pb_i = pbank[pbi[0] % 4]
                            pbi[0] += 1
                            for k in range(8):
                                P.op("pe", lambda e, HB=HB, k=k, r=r, hf=hf, pb_i=pb_i, wv=wv, hTv=hTv: e.matmul(bank(pb_i, HB)[0:4, :], lhsT=wv[:, k, 4 * r:4 * r + 4], rhs=hTv[:, k, hf * HB:(hf + 1) * HB], start=(k == 0), stop=(k == 7)),
                                     reads=[hTb, wb], writes=[bankb[pb_i]])
                            P.op("act", lambda e, HB=HB, r=r, hf=hf, pb_i=pb_i, gsv=gsv: e.activation(out=gsv[0:4, r, hf * HB:(hf + 1) * HB], in_=bank(pb_i, HB)[0:4, :], func=AF.Identity, bias=bmg[:, r:r + 1]),
                                 reads=[bankb[pb_i], bconst], writes=[gsb])
                    dd = GG[j] if own else GGc
                    P.dma("pool", dd[:, t0:t0 + tb].rearrange("(r h) t -> h r t", h=4), gsv[0:4], reads=[gsb], writes=[DB("GG%d%d" % (j, own))])

    def phaseB(j):
        Sk = Sctx[j]
        nkt = Sk // 128
        Sq = SO[j]
        QBM = min(512, Sq)
        ktr = Rot([ar.bf(Sk) for _ in range(2)], "kt")
        vr = Rot([ar.bf(Sk) for _ in range(2)], "v")
        qr = Rot([ar.bf(QBM) for _ in range(2)], "q")
        ptr = Rot([ar.bf(2 * QBM) for _ in range(3)], "pt")
        accs = [ar.f32(2 * QBM) for _ in range(2)]
        accb = [Buf("accA"), Buf("accB")]
        ocr = Rot([ar.f32(2 * QBM) for _ in range(2)], "oc")
        tfr = Rot([tuple(ar.f32(QBM) for _ in range(3)) for _ in range(2)], "tf")
        yst = Rot([ar.bf(QBM) for _ in range(2)], "yst")
        deferred = []

        def fire(force):
            keep = []
            for item in deferred:
                item[0] -= 1
                if force or item[0] <= 0:
                    item[1]()
                else:
                    keep.append(item)
            deferred[:] = keep

        for h in range(HA):
            kt_, ktb = ktr.next()
            v_, vb = vr.next()
            P.dma("sp", kt_, KT[j][h], reads=[DB("KT%d" % j)], writes=[ktb])
            vv = v_.rearrange("p (t d) -> p t d", d=128)
            P.dma("sp", vv, VV[j][h].rearrange("(t p) d -> p t d", p=128), reads=[DB("VV%d" % j)], writes=[vb])
            for qb in range((Sq + QBM - 1) // QBM):
                q0 = qb * QBM
                QB = min(QBM, Sq - q0)
                q_, qb_ = qr.next()
                q_ = q_[:, 0:QB]
                P.dma("sp", q_, QT[j][h, :, q0:q0 + QB], reads=[DB("QT%d" % j)], writes=[qb_])

                def qk(u, kt_=kt_, ktb=ktb, q_=q_, qb_=qb_, QB=QB):
                    sb = u % 2
                    for m in range(2):
                        P.op("pe", lambda e, m=m, sb=sb, u=u: e.matmul(bank(2 * sb + m, QB), lhsT=kt_[64 * m:64 * m + 64, u * 128:(u + 1) * 128], rhs=q_[64 * m:64 * m + 64, :], start=True, stop=True),
                             reads=[ktb, qb_], writes=[bankb[2 * sb + m]])
                pts = {}

                def ex(u, QB=QB):
                    sb = u % 2
                    pt, ptb = ptr.next()
                    pt = pt[:, 0:2 * QB]
                    pts[u] = (pt, ptb)
                    ptv = pt.rearrange("p (m q) -> p m q", m=2)
                    src = PS[:, 2 * sb * 512:(2 * sb + 2) * 512].rearrange("p (m q) -> p m q", m=2)[:, :, 0:QB]
                    P.op("act", lambda e, ptv=ptv, src=src: e.activation(out=ptv, in_=src, func=AF.Exp, scale=0.125), reads=[bankb[2 * sb], bankb[2 * sb + 1]], writes=[ptb])

                def pv(u, vv=vv, vb=vb, QB=QB):
                    pt, ptb = pts.pop(u)
                    for m in range(2):
                        P.op("pe", lambda e, m=m, u=u, pt=pt: e.matmul(bank(4 + m, QB), lhsT=vv[:, u, :], rhs=pt[:, m * QB:(m + 1) * QB], start=(u == 0), stop=(u == nkt - 1)),
                             reads=[vb, ptb], writes=[bankb[4 + m]])
                    if u in pe_units:
                        for m in range(2):
                            P.op("pe", lambda e, m=m, u=u, pt=pt: e.matmul(bank(6 + m, QB), lhsT=onesb, rhs=pt[:, m * QB:(m + 1) * QB], start=(u == pe_units[0]), stop=False),
                                 reads=[bconst, ptb], writes=[bankb[6 + m]])
                        return
                    ai = dct[0] % 2
                    acc = accs[ai][:, 0:2 * QB]
                    if dct[0] < 2:
                        P.op("dve", lambda e, acc=acc, pt=pt: e.tensor_copy(out=acc, in_=pt), reads=[ptb], writes=[accb[ai]])
                    else:
                        P.op("dve", lambda e, acc=acc, pt=pt: e.tensor_tensor(out=acc, in0=pt, in1=acc, op=ALU.add), reads=[ptb, accb[ai]], writes=[accb[ai]])
                    dct[0] += 1
                dct = [0]
                pe_units = [u for u in range(nkt) if u >= 8 and u % 4 == 3]
                qk(0)
                for u in range(nkt):
                    ex(u)
                    if u + 1 < nkt:
                        qk(u + 1)
                    pv(u)
                    fire(False)
                fire(True)
                nacc = min(2, nkt)
                for ai in range(nacc):
                    acc = accs[ai][:, 0:2 * QB]
                    for m in range(2):
                        P.op("pe", lambda e, m=m, acc=acc, QB=QB, ai=ai: e.matmul(bank(6 + m, QB), lhsT=onesf, rhs=acc[:, m * QB:(m + 1) * QB], start=(ai == 0 and not pe_units), stop=(ai == nacc - 1)),
                             reads=[bconst, accb[ai]], writes=[bankb[6 + m]])
                oc, ocb = ocr.next()
                (r0, r1, sq), tfb = tfr.next()
                r0 = r0[:, 0:QB]
                r1 = r1[:, 0:QB]
                sq = sq[:, 0:QB]
                a0 = oc[:, 0:QB]
                a1 = oc[:, QBM:QBM + QB]
                P.op("act", lambda e, a0=a0, QB=QB: e.activation(out=a0, in_=bank(4, QB), func=AF.Copy), reads=[bankb[4]], writes=[ocb])
                P.op("dve", lambda e, a1=a1, QB=QB: e.tensor_copy(out=a1, in_=bank(5, QB)), reads=[bankb[5]], writes=[ocb])
                P.op("dve", lambda e, r0=r0, QB=QB: e.reciprocal(out=r0, in_=bank(6, QB)), reads=[bankb[6]], writes=[tfb])
                P.op("dve", lambda e, r1=r1, QB=QB: e.reciprocal(out=r1, in_=bank(7, QB)), reads=[bankb[7]], writes=[tfb])
                P.op("dve", lambda e, a0=a0, r0=r0: e.tensor_tensor(out=a0, in0=a0, in1=r0, op=ALU.mult), reads=[ocb, tfb], writes=[ocb])
                P.op("dve", lambda e, a1=a1, r1=r1: e.tensor_tensor(out=a1, in0=a1, in1=r1, op=ALU.mult), reads=[ocb, tfb], writes=[ocb])
                P.op("dve", lambda e, a0=a0, a1=a1: e.scalar_tensor_tensor(out=a0, in0=a1, scalar=nlam, in1=a0, op0=ALU.mult, op1=ALU.add), reads=[ocb, bconst], writes=[ocb])
                P.op("dve", lambda e, a0=a0, sq=sq: e.tensor_tensor(out=sq, in0=a0, in1=a0, op=ALU.mult), reads=[ocb], writes=[tfb])

                def part2(a0=a0, r0=r0, sq=sq, ocb=ocb, tfb=tfb, QB=QB, h=h, q0=q0):
                    P.op("pe", lambda e: e.matmul(bank(6, QB), lhsT=onesf, rhs=sq, start=True, stop=True), reads=[tfb, bconst], writes=[bankb[6]])
                    P.op("act", lambda e: e.activation(out=r0, in_=bank(6, QB), func=AF.Sqrt, scale=1.0 / 128.0, bias=EPS), reads=[bankb[6]], writes=[tfb])
                    P.op("dve", lambda e: e.reciprocal(out=r0, in_=r0), reads=[tfb], writes=[tfb])
                    ys, ysb = yst.next()
                    ys = ys[:, 0:QB]
                    P.op("dve", lambda e: e.scalar_tensor_tensor(out=ys, in0=a0, scalar=gsc, in1=r0, op0=ALU.mult, op1=ALU.mult), reads=[tfb, ocb, bconst], writes=[ysb])
                    P.dma("pool", yaT[j][h, :, q0:q0 + QB], ys, reads=[ysb], writes=[DB("yaT%d" % j)])
                deferred.append([6, part2])
        fire(True)

    def phaseC(j):
        is_p = (j == 2)
        So = SO[j]
        nch = So // 128
        UG = ar.f32(nch * 16)
        UGv = UG.rearrange("p (c q) -> p c q", q=16)
        DECB = ar.f32(8 * nch)
        Cf = [ar.f32(257) for _ in range(8)]
        Cb = [ar.bf(257) for _ in range(8)]
        ini = ar.f32(16)
        Minit = [ini[0:4, 0:1], ini[0:4, 1:2]]
        Binit = [ini[0:4, 2:3], ini[0:4, 3:4]]
        bC = Buf("Cstate")
        bini = Buf("ini")
        bUG = Buf("UG")
        P.op("dve", lambda e: e.memset(ini, 0.0), writes=[bini])
        m0 = ar.mark()
        def c2():
            PC = min(2048, Sc)
            npc = Sc // PC
            It = ar.f32(PC)
            Ft = ar.f32(PC)
            Pt = ar.f32(PC)
            Mt = ar.f32(PC)
            T1 = ar.f32(PC)
            T2 = ar.f32(PC)
            pm = ar.f32(2 * npc)
            bs = ar.f32(2 * npc)
            carry = ar.f32(4)
            tot = ar.f32(4)
            bg_ = Buf("c2g")
            bsm = Buf("c2s")
            P.op("dve", lambda e: e.memset(carry, 0.0), writes=[bsm])
            for d in range(2):
                msk = mb_in if d == 0 else ma_in
                for pc in range(npc):
                    sl = slice(pc * PC, (pc + 1) * PC)
                    P.dma("sp", It[0:4, :], GGc[d * 8:d * 8 + 4, sl], reads=[DB("GG20")], writes=[bg_])
                    P.dma("sp", Ft[0:4, :], GGc[d * 8 + 4:d * 8 + 8, sl], reads=[DB("GG20")], writes=[bg_])
                    P.dma("sp", Mt[0:4, :], msk[:, sl], writes=[bg_])
                    P.op("act", lambda e: e.activation(out=Ft[0:4, :], in_=Ft[0:4, :], func=AF.Exp, scale=-1.0), reads=[bg_], writes=[bg_])
                    P.op("act", lambda e: e.activation(out=Ft[0:4, :], in_=Ft[0:4, :], func=AF.Ln, bias=1.0), reads=[bg_], writes=[bg_])
                    init = 0.0 if pc == 0 else carry[0:4, d:d + 1]
                    P.op("dve", lambda e, init=init: e.tensor_tensor_scan(out=Pt[0:4, :], data0=onesf[0:4, 0:1].to_broadcast([4, PC]), data1=Ft[0:4, :], initial=init, op0=ALU.mult, op1=ALU.add),
                         reads=[bg_, bconst, bsm], writes=[bg_])
                    P.op("dve", lambda e, d=d: e.tensor_copy(out=carry[0:4, d:d + 1], in_=Pt[0:4, PC - 1:PC]), reads=[bg_], writes=[bsm])
                    P.op("dve", lambda e: e.tensor_tensor(out=T1[0:4, :], in0=Ft[0:4, :], in1=Mt[0:4, :], op=ALU.mult), reads=[bg_], writes=[bg_])
                    P.op("dve", lambda e, d=d, pc=pc: e.tensor_reduce(out=bs[0:4, d * npc + pc:d * npc + pc + 1], in_=T1[0:4, :], axis=AX.X, op=ALU.add), reads=[bg_], writes=[bsm])
                    if d == 0:
                        P.op("dve", lambda e: e.tensor_tensor(out=It[0:4, :], in0=It[0:4, :], in1=Pt[0:4, :], op=ALU.add), reads=[bg_], writes=[bg_])
                    else:
                        P.op("dve", lambda e: e.tensor_tensor(out=It[0:4, :], in0=It[0:4, :], in1=Ft[0:4, :], op=ALU.add), reads=[bg_], writes=[bg_])
                        P.op("dve", lambda e: e.tensor_tensor(out=It[0:4, :], in0=It[0:4, :], in1=Pt[0:4, :], op=ALU.subtract), reads=[bg_], writes=[bg_])
                    P.op("dve", lambda e: e.tensor_tensor(out=It[0:4, :], in0=It[0:4, :], in1=Mt[0:4, :], op=ALU.mult), reads=[bg_], writes=[bg_])
                    P.op("dve", lambda e: e.tensor_scalar(out=T2[0:4, :], in0=Mt[0:4, :], scalar1=-NEG, scalar2=NEG, op0=ALU.mult, op1=ALU.add), reads=[bg_], writes=[bg_])
                    P.op("dve", lambda e: e.tensor_tensor(out=It[0:4, :], in0=It[0:4, :], in1=T2[0:4, :], op=ALU.add), reads=[bg_], writes=[bg_])
                    P.op("dve", lambda e, d=d, pc=pc: e.tensor_reduce(out=pm[0:4, d * npc + pc:d * npc + pc + 1], in_=It[0:4, :], axis=AX.X, op=ALU.max), reads=[bg_], writes=[bsm])
                    P.dma("sp", AMs[d, :, sl], It[0:4, :], reads=[bg_], writes=[DB("AMs")])
            bias2 = ar.f32(4)
            for d in range(2):
                P.op("dve", lambda e, d=d: e.tensor_reduce(out=Binit[d], in_=bs[0:4, d * npc:(d + 1) * npc], axis=AX.X, op=ALU.add), reads=[bsm], writes=[bini])
                P.op("dve", lambda e, d=d: e.tensor_reduce(out=Minit[d], in_=pm[0:4, d * npc:(d + 1) * npc], axis=AX.X, op=ALU.max), reads=[bsm], writes=[bini])
            P.op("dve", lambda e: e.tensor_tensor(out=Minit[1], in0=Minit[1], in1=carry[0:4, 1:2], op=ALU.add), reads=[bini, bsm], writes=[bini])
            for d in range(2):
                P.op("dve", lambda e, d=d: e.tensor_scalar(out=Minit[d], in0=Minit[d], scalar1=0.0, scalar2=None, op0=ALU.max), reads=[bini], writes=[bini])
            P.op("dve", lambda e: e.tensor_scalar(out=bias2[0:4, 0:1], in0=Minit[0], scalar1=-1.0, scalar2=LNSC, op0=ALU.mult, op1=ALU.add), reads=[bini], writes=[bsm])
            P.op("dve", lambda e: e.tensor_tensor(out=bias2[0:4, 1:2], in0=carry[0:4, 1:2], in1=Minit[1], op=ALU.subtract), reads=[bini, bsm], writes=[bsm])
            P.op("dve", lambda e: e.tensor_scalar(out=bias2[0:4, 1:2], in0=bias2[0:4, 1:2], scalar1=LNSC, scalar2=None, op0=ALU.add), reads=[bsm], writes=[bsm])
            wtr = Rot([ar.f32(4) for _ in range(3)], "wt")
            kr = Rot([ar.bf(512) for _ in range(3)], "kc")
            var_ = [ar.bf(4 * 257) for _ in range(3)]
            vbufs = [Buf("vc%d" % i) for i in range(3)]
            for t, vb_ in zip(var_, vbufs):
                P.op("dve", lambda e, t=t: e.memset(t, 1.0), writes=[vb_])
            kur = Rot([ar.bf(128) for _ in range(4)], "ku")
            w0 = ar.f32(PC)
            bw = Buf("wrow")
            for d in range(2):
                for pc in range(npc):
                    sl = slice(pc * PC, (pc + 1) * PC)
                    P.dma("sp", w0[0:4, :], AMs[d, :, sl], reads=[DB("AMs")], writes=[bw])
                    P.op("act", lambda e, d=d: e.activation(out=w0[0:4, :], in_=w0[0:4, :], func=AF.Exp, bias=bias2[0:4, d:d + 1]), reads=[bw, bsm], writes=[bw])
                    for cl in range(PC // 128):
                        cc = pc * (PC // 128) + cl
                        wt_, wtb = wtr.next()
                        P.op("pe", lambda e, cl=cl: e.transpose(PS[:, 7 * 512:7 * 512 + 4], w0[0:4, cl * 128:(cl + 1) * 128], identf[0:4, 0:4]), reads=[bw, bconst], writes=[bankb[7]])
                        P.op("dve", lambda e, wt_=wt_: e.tensor_copy(out=wt_, in_=PS[:, 7 * 512:7 * 512 + 4]), reads=[bankb[7]], writes=[wtb])
                        k_, kb_ = kr.next()
                        P.dma("sp", k_, mktc[cc * 128:(cc + 1) * 128, :], reads=[DB("mkt20")], writes=[kb_])
                        vi = cc % 3
                        va = var_[vi].rearrange("p (h f) -> p h f", h=4)
                        P.dma("sp", va[:, :, 0:256], mvtc[cc * 128:(cc + 1) * 128, :].rearrange("p (h f) -> p h f", h=4), reads=[DB("mvt20")], writes=[vbufs[vi]])
                        for hd in range(4):
                            ku, kub = kur.next()
                            P.op("dve", lambda e, ku=ku, k_=k_, hd=hd, wt_=wt_: e.tensor_scalar(out=ku, in0=k_[:, hd * 128:(hd + 1) * 128], scalar1=wt_[:, hd:hd + 1], scalar2=None, op0=ALU.mult),
                                 reads=[kb_, wtb], writes=[kub])
                            P.op("pe", lambda e, ku=ku, va=va, hd=hd, cc=cc: e.matmul(bank(hd, 257), lhsT=ku, rhs=va[:, hd, :], start=(cc == 0), stop=(cc == nchc - 1)),
                                 reads=[kub, vbufs[vi]], writes=[bankb[hd]])
                for hd in range(4):
                    P.op("dve", lambda e, hd=hd, d=d: e.tensor_copy(out=Cf[d * 4 + hd], in_=bank(hd, 257)), reads=[bankb[hd]], writes=[bC])
        if is_p:
            c2()
            barrier_local(m0)
        else:
            for bi in range(8):
                P.op("dve", lambda e, bi=bi: e.memset(Cf[bi], 0.0), writes=[bC])
        It = ar.f32(So)
        Ft = ar.f32(So)
        Pt = ar.f32(So)
        cm = ar.f32(nch)
        Me = ar.f32(nch)
        Me2 = ar.f32(nch)
        Mp = ar.f32(nch)
        dec = [ar.f32(nch), ar.f32(nch)]
        tt = ar.f32(4)
        bg_ = Buf("c1g")
        bsm = Buf("c1s")
        bdec = Buf("dec")
        for d in range(2):
            P.dma("sp", It[0:4, :], GG[j][d * 8:d * 8 + 4, :], reads=[DB("GG%d1" % j)], writes=[bg_])
            P.dma("sp", Ft[0:4, :], GG[j][d * 8 + 4:d * 8 + 8, :], reads=[DB("GG%d1" % j)], writes=[bg_])
            P.op("act", lambda e: e.activation(out=Ft[0:4, :], in_=Ft[0:4, :], func=AF.Exp, scale=-1.0), reads=[bg_], writes=[bg_])
            P.op("act", lambda e: e.activation(out=Ft[0:4, :], in_=Ft[0:4, :], func=AF.Ln, bias=1.0), reads=[bg_], writes=[bg_])
            P.op("dve", lambda e: e.tensor_tensor_scan(out=Pt[0:4, :], data0=onesf[0:4, 0:1].to_broadcast([4, So]), data1=Ft[0:4, :], initial=0.0, op0=ALU.mult, op1=ALU.add),
                 reads=[bg_, bconst], writes=[bg_])
            if d == 0:
                P.op("dve", lambda e: e.tensor_scalar(out=Pt[0:4, :], in0=Pt[0:4, :], scalar1=Binit[0], scalar2=None, op0=ALU.add), reads=[bg_, bini], writes=[bg_])
            else:
                P.op("dve", lambda e: e.tensor_tensor(out=tt[0:4, 0:1], in0=Pt[0:4, So - 1:So], in1=Binit[1], op=ALU.add), reads=[bg_, bini], writes=[bsm])
                P.op("dve", lambda e: e.tensor_tensor(out=Pt[0:4, :], in0=Ft[0:4, :], in1=Pt[0:4, :], op=ALU.subtract), reads=[bg_, bsm], writes=[bg_])
                P.op("dve", lambda e: e.tensor_scalar(out=Pt[0:4, :], in0=Pt[0:4, :], scalar1=tt[0:4, 0:1], scalar2=None, op0=ALU.add), reads=[bg_, bsm], writes=[bg_])
            P.op("dve", lambda e: e.tensor_tensor(out=It[0:4, :], in0=It[0:4, :], in1=Pt[0:4, :], op=ALU.add), reads=[bg_], writes=[bg_])
            P.op("dve", lambda e: e.tensor_reduce(out=cm[0:4, :], in_=It[0:4, :].rearrange("p (c t) -> p c t", t=128), axis=AX.X, op=ALU.max), reads=[bg_], writes=[bsm])
            if d == 0:
                P.op("dve", lambda e: e.tensor_tensor_scan(out=Me[0:4, :], data0=cm[0:4, :], data1=cm[0:4, :], initial=Minit[0], op0=ALU.max, op1=ALU.max), reads=[bsm, bini], writes=[bsm])
                if nch > 1:
                    P.op("dve", lambda e: e.tensor_copy(out=Mp[0:4, 1:nch], in_=Me[0:4, 0:nch - 1]), reads=[bsm], writes=[bsm])
                P.op("dve", lambda e: e.tensor_copy(out=Mp[0:4, 0:1], in_=Minit[0]), reads=[bsm, bini], writes=[bsm])
            else:
                src, dst_ = cm, Me2
                sh = 1
                while sh < nch:
                    P.op("dve", lambda e, src=src, dst_=dst_, sh=sh: e.tensor_tensor(out=dst_[0:4, 0:nch - sh], in0=src[0:4, 0:nch - sh], in1=src[0:4, sh:nch], op=ALU.max), reads=[bsm], writes=[bsm])
                    P.op("dve", lambda e, src=src, dst_=dst_, sh=sh: e.tensor_copy(out=dst_[0:4, nch - sh:nch], in_=src[0:4, nch - sh:nch]), reads=[bsm], writes=[bsm])
                    src, dst_ = dst_, src
                    sh *= 2
                P.op("dve", lambda e, src=src: e.tensor_scalar(out=Me[0:4, :], in0=src[0:4, :], scalar1=Minit[1], scalar2=None, op0=ALU.max), reads=[bsm, bini], writes=[bsm])
                if nch > 1:
                    P.op("dve", lambda e: e.tensor_copy(out=Mp[0:4, 0:nch - 1], in_=Me[0:4, 1:nch]), reads=[bsm], writes=[bsm])
                P.op("dve", lambda e: e.tensor_copy(out=Mp[0:4, nch - 1:nch], in_=Minit[1]), reads=[bsm, bini], writes=[bsm])
            P.op("dve", lambda e: e.tensor_tensor(out=Mp[0:4, :], in0=Mp[0:4, :], in1=Me[0:4, :], op=ALU.subtract), reads=[bsm], writes=[bsm])
            P.op("act", lambda e, d=d: e.activation(out=dec[d][0:4, :], in_=Mp[0:4, :], func=AF.Exp), reads=[bsm], writes=[bdec])
            Meb = Me[0:4, :].unsqueeze(2).to_broadcast([4, nch, 128])
            P.op("dve", lambda e, Meb=Meb: e.tensor_tensor(out=Pt[0:4, :].rearrange("p (c t) -> p c t", t=128), in0=Pt[0:4, :].rearrange("p (c t) -> p c t", t=128), in1=Meb, op=ALU.subtract), reads=[bg_, bsm], writes=[bg_])
            P.op("dve", lambda e, Meb=Meb: e.tensor_tensor(out=It[0:4, :].rearrange("p (c t) -> p c t", t=128), in0=It[0:4, :].rearrange("p (c t) -> p c t", t=128), in1=Meb, op=ALU.subtract), reads=[bg_, bsm], writes=[bg_])
            P.op("act", lambda e: e.activation(out=Ft[0:4, :], in_=Pt[0:4, :], func=AF.Exp), reads=[bg_], writes=[bg_])
            P.op("act", lambda e: e.activation(out=It[0:4, :], in_=It[0:4, :], func=AF.Exp, bias=LNSC), reads=[bg_], writes=[bg_])
            for c in range(nch):
                P.op("pe", lambda e, c=c, d=d: e.transpose(PS[:, d * 512 + c * 8:d * 512 + c * 8 + 4], It[0:4, c * 128:(c + 1) * 128], identf[0:4, 0:4]), reads=[bg_, bconst], writes=[bankb[d]])
                P.op("pe", lambda e, c=c, d=d: e.transpose(PS[:, d * 512 + c * 8 + 4:d * 512 + c * 8 + 8], Ft[0:4, c * 128:(c + 1) * 128], identf[0:4, 0:4]), reads=[bg_, bconst], writes=[bankb[d]])
            for hd in range(4):
                P.op("pe", lambda e, hd=hd, d=d: e.matmul(PS[:, 1024 + (d * 4 + hd) * nch:1024 + (d * 4 + hd + 1) * nch], lhsT=self_[0:4, hd * 128:(hd + 1) * 128], rhs=dec[d][0:4, :], start=True, stop=True),
                     reads=[bdec, bconst], writes=[bankb[2]])
        assert nch * 8 <= 512
        for d in range(2):
            P.op("dve", lambda e, d=d: e.tensor_copy(out=UGv[:, :, d * 8:(d + 1) * 8], in_=PS[:, d * 512:d * 512 + nch * 8].rearrange("p (c q) -> p c q", q=8)), reads=[bankb[d]], writes=[bUG])
        P.op("dve", lambda e: e.tensor_copy(out=DECB, in_=PS[:, 1024:1024 + 8 * nch]), reads=[bankb[2]], writes=[bUG])
        barrier_local(m0)
        qTt = ar.bf(So)
        kTt = ar.bf(So)
        ktk = ar.bf(So)
        vau = ar.bf(nch * 257)
        hsum = ar.f32(nch * 256)
        smt = ar.bf(nch * 256)
        yms = ar.bf(2 * So)
        mnb = ar.f32(1024)
        bh = Buf("headdata")
        bhs = Buf("hsum")
        bym = Buf("yms")
        bmn = Buf("mnb")
        P.dma("sp", mnb, mnorm.partition_broadcast(128), writes=[bmn])
        vav = vau.rearrange("p (c f) -> p c f", f=257)
        ktkv = ktk.rearrange("p (c f) -> p c f", f=128)
        hsv = hsum.rearrange("p (c f) -> p c f", f=256)
        smv = smt.rearrange("p (c f) -> p c f", f=256)
        ymv = yms.rearrange("p (i t) -> p i t", i=2)
        ptr = Rot([ar.bf(128) for _ in range(4)], "ptm")
        kur = Rot([ar.bf(128) for _ in range(4)], "kum")
        dr = Rot([ar.f32(2) for _ in range(4)], "den")
        fin = Rot([(ar.f32(6), ar.f32(2), ar.f32(1), ar.f32(256), ar.bf(256)) for _ in range(2)], "fin")
        for hd in range(HM):
            P.dma("sp", qTt, mqT[j][hd], reads=[DB("mqT%d" % j)], writes=[bh])
            P.dma("sp", kTt, mkT[j][hd], reads=[DB("mkT%d" % j)], writes=[bh])
            P.dma("sp", ktkv, mkt[j][:, hd * 128:(hd + 1) * 128].rearrange("(c p) f -> p c f", p=128), reads=[DB("mkt%d1" % j)], writes=[bh])
            P.op("dve", lambda e: e.memset(vau, 1.0), reads=[], writes=[bh])
            P.dma("sp", vav[:, :, 0:256], mvt[j][:, hd * 256:(hd + 1) * 256].rearrange("(c p) f -> p c f", p=128), reads=[DB("mvt%d1" % j)], writes=[bh])
            P.dma("sp", smv, smo[j][:, hd * 256:(hd + 1) * 256].rearrange("(c p) f -> p c f", p=128), reads=[DB("smo%d" % j)], writes=[bh])
            P.op("dve", lambda e: e.memset(hsum, 0.0), writes=[bhs])
            for i in range(nch):
                for d in range(2):
                    c = i if d == 0 else nch - 1 - i
                    bi = d * 4 + hd
                    ci = bi * nch + c
                    msk = maskF if d == 0 else maskB
                    ucol = UGv[:, c, d * 8 + hd:d * 8 + hd + 1]
                    gcol = UGv[:, c, d * 8 + 4 + hd:d * 8 + 4 + hd + 1]
                    cs = slice(c * 128, (c + 1) * 128)
                    bS, bX, bU = 2, 3 + d, 5 + d
                    P.op("dve", lambda e, bi=bi, ci=ci: e.tensor_scalar(out=Cb[bi], in0=Cf[bi], scalar1=DECB[:, ci:ci + 1], scalar2=None, op0=ALU.mult), reads=[bC, bUG], writes=[bC])
                    P.op("dve", lambda e, bi=bi, ci=ci: e.tensor_scalar(out=Cf[bi], in0=Cf[bi], scalar1=DECB[:, ci:ci + 1], scalar2=None, op0=ALU.mult), reads=[bC, bUG], writes=[bC])
                    P.op("pe", lambda e, cs=cs, d=d: e.matmul(PS[:, bS * 512 + d * 128:bS * 512 + (d + 1) * 128], lhsT=kTt[:, cs], rhs=qTt[:, cs], start=True, stop=True), reads=[bh], writes=[bankb[bS]])
                    pt, ptb = ptr.next()
                    P.op("dve", lambda e, pt=pt, d=d, ucol=ucol, msk=msk: e.scalar_tensor_tensor(out=pt, in0=PS[:, bS * 512 + d * 128:bS * 512 + (d + 1) * 128], scalar=ucol, in1=msk, op0=ALU.mult, op1=ALU.mult),
                         reads=[bankb[bS], bUG, bconst], writes=[ptb])
                    ku, kub = kur.next()
                    P.op("dve", lambda e, ku=ku, c=c, ucol=ucol: e.tensor_scalar(out=ku, in0=ktkv[:, c, :], scalar1=ucol, scalar2=None, op0=ALU.mult), reads=[bh, bUG], writes=[kub])
                    P.op("pe", lambda e, pt=pt, c=c, bX=bX: e.matmul(bank(bX, 257), lhsT=pt, rhs=vav[:, c, :], start=True, stop=False), reads=[ptb, bh], writes=[bankb[bX]])
                    P.op("pe", lambda e, cs=cs, bi=bi, bX=bX: e.matmul(bank(bX, 257), lhsT=qTt[:, cs], rhs=Cb[bi], start=False, stop=True), reads=[bh, bC], writes=[bankb[bX]])
                    P.op("pe", lambda e, ku=ku, c=c, bU=bU: e.matmul(bank(bU, 257), lhsT=ku, rhs=vav[:, c, :], start=True, stop=True), reads=[kub, bh], writes=[bankb[bU]])
                    dn, dnb = dr.next()
                    P.op("dve", lambda e, dn=dn, bX=bX: e.tensor_scalar(out=dn[:, 0:1], in0=bank(bX, 257)[:, 256:257], scalar1=-1.0, scalar2=None, op0=ALU.mult), reads=[bankb[bX]], writes=[dnb])
                    P.op("dve", lambda e, dn=dn, bX=bX: e.tensor_tensor(out=dn[:, 0:1], in0=dn[:, 0:1], in1=bank(bX, 257)[:, 256:257], op=ALU.max), reads=[bankb[bX], dnb], writes=[dnb])
                    P.op("dve", lambda e, dn=dn, gcol=gcol: e.tensor_tensor(out=dn[:, 0:1], in0=dn[:, 0:1], in1=gcol, op=ALU.max), reads=[dnb, bUG], writes=[dnb])
                    P.op("dve", lambda e, dn=dn: e.reciprocal(out=dn[:, 1:2], in_=dn[:, 0:1]), reads=[dnb], writes=[dnb])
                    P.op("dve", lambda e, dn=dn, bX=bX, c=c: e.scalar_tensor_tensor(out=hsv[:, c, :], in0=bank(bX, 256), scalar=dn[:, 1:2], in1=hsv[:, c, :], op0=ALU.mult, op1=ALU.add),
                         reads=[bankb[bX], dnb, bhs], writes=[bhs])
                    P.op("dve", lambda e, bi=bi, bU=bU: e.tensor_tensor(out=Cf[bi], in0=Cf[bi], in1=bank(bU, 257), op=ALU.add), reads=[bC, bankb[bU]], writes=[bC])
            for c in range(nch):
                (stats, mv, rs, hn, yb), fb = fin.next()
                P.op("dve", lambda e, stats=stats, c=c: e.bn_stats(out=stats, in_=hsv[:, c, :]), reads=[bhs], writes=[fb])
                P.op("dve", lambda e, stats=stats, mv=mv: e.bn_aggr(out=mv, in_=stats), reads=[fb], writes=[fb])
                P.op("act", lambda e, rs=rs, mv=mv: e.activation(out=rs, in_=mv[:, 1:2], func=AF.Sqrt, bias=EPS), reads=[fb], writes=[fb])
                P.op("dve", lambda e, rs=rs: e.reciprocal(out=rs, in_=rs), reads=[fb], writes=[fb])
                P.op("dve", lambda e, hn=hn, mv=mv, rs=rs, c=c: e.tensor_scalar(out=hn, in0=hsv[:, c, :], scalar1=mv[:, 0:1], scalar2=rs, op0=ALU.subtract, op1=ALU.mult), reads=[bhs, fb], writes=[fb])
                P.op("dve", lambda e, hn=hn, hd=hd: e.tensor_tensor(out=hn, in0=hn, in1=mnb[:, hd * 256:(hd + 1) * 256], op=ALU.mult), reads=[fb, bmn], writes=[fb])
                P.op("dve", lambda e, hn=hn, yb=yb, c=c: e.tensor_tensor(out=yb, in0=hn, in1=smv[:, c, :], op=ALU.mult), reads=[fb, bh], writes=[fb])
                tb_i = 7
                tpb = bankbf(tb_i)
                for i2 in range(2):
                    P.op("pe", lambda e, i2=i2, yb=yb, tpb=tpb: e.transpose(tpb[:, i2 * 128:(i2 + 1) * 128], yb[:, i2 * 128:(i2 + 1) * 128], identb), reads=[fb, bconst], writes=[bankb[tb_i]])
                P.op("act", lambda e, c=c, tpb=tpb: e.activation(out=ymv[:, :, c * 128:(c + 1) * 128], in_=tpb[:, 0:256].rearrange("p (i t) -> p i t", i=2), func=AF.Copy), reads=[bankb[tb_i]], writes=[bym])
            P.dma("pool", ymT[j][2 * hd:2 * hd + 2].rearrange("i p t -> p i t"), ymv, reads=[bym], writes=[DB("ymT%d" % j)])

    def barrier_local(m):
        barrier()
        ar.reset(m)

    def phaseD1(j, wpa, wpm, wo, bw, g1b, l1g, l1b, bD):
        So = SO[j]
        nch = So // 128
        yar = Rot([ar.bf(8 * 128) for _ in range(3)], "ya")
        ymr = Rot([ar.bf(8 * 128) for _ in range(3)], "ym")
        sgr = Rot([ar.bf(2048) for _ in range(3)], "sg")
        xr = Rot([ar.f32(1024) for _ in range(3)], "xd")
        tr_ = Rot([ar.f32(1024) for _ in range(3)], "td")
        mir = Rot([ar.bf(1024) for _ in range(3)], "mi")
        mTr = Rot([ar.bf(1024) for _ in range(3)], "mT")
        rr = Rot([ar.f32(1024) for _ in range(3)], "rd")
        x1r = Rot([ar.f32(1024) for _ in range(3)], "x1")
        xnr = Rot([ar.bf(1024) for _ in range(3)], "xn2")
        str_ = Rot([(ar.f32(12), ar.f32(2), ar.f32(1)) for _ in range(6)], "st2")
        h2r = Rot([ar.bf(1024) for _ in range(3)], "h2")
        P.dma("sp", g1b, adab[j, 0:1, :].partition_broadcast(128), reads=[DB("adab")], writes=[bD])
        wpav = wpa.rearrange("p (k n) -> p k n", k=8)
        wpmv = wpm.rearrange("p (k n) -> p k n", k=8)
        wov = wo.rearrange("p (k n) -> p k n", k=8)
        for t in range(nch):
            tok = t * 128
            ya, yab = yar.next()
            ym, ymb = ymr.next()
            sg, sgb = sgr.next()
            xt, xb_ = xr.next()
            yav = ya.rearrange("p (h t) -> p h t", h=8)
            ymv = ym.rearrange("p (h t) -> p h t", h=8)
            P.dma("sp", yav, yaT[j][:, :, tok:tok + 128].rearrange("h p t -> p h t"), reads=[DB("yaT%d" % j)], writes=[yab])
            P.dma("sp", ymv, ymT[j][:, :, tok:tok + 128].rearrange("h p t -> p h t"), reads=[DB("ymT%d" % j)], writes=[ymb])
            P.dma("sp", sg, sbg[j][tok:tok + 128, :], reads=[DB("sbg%d" % j)], writes=[sgb])
            P.dma("sp", xt, xs[j][tok:tok + 128, :], writes=[xb_])
            for n in range(2):
                for k in range(8):
                    P.op("pe", lambda e, n=n, k=k, yav=yav: e.matmul(bank(n), lhsT=yav[:, k, :], rhs=wpav[:, k, n * 512:(n + 1) * 512], start=(k == 0), stop=(k == 7)), reads=[yab, bw], writes=[bankb[n]])
            for n in range(2):
                for k in range(8):
                    P.op("pe", lambda e, n=n, k=k, ymv=ymv: e.matmul(bank(2 + n), lhsT=ymv[:, k, :], rhs=wpmv[:, k, n * 512:(n + 1) * 512], start=(k == 0), stop=(k == 7)), reads=[ymb, bw], writes=[bankb[2 + n]])
            tm, tmb = tr_.next()
            mi, mib = mir.next()
            P.op("dve", lambda e, tm=tm, sg=sg: e.tensor_tensor(out=tm, in0=PS[:, 0:1024], in1=sg[:, 0:1024], op=ALU.mult), reads=[bankb[0], bankb[1], sgb], writes=[tmb])
            P.op("dve", lambda e, mi=mi, sg=sg: e.tensor_tensor(out=mi, in0=PS[:, 1024:2048], in1=sg[:, 1024:2048], op=ALU.mult), reads=[bankb[2], bankb[3], sgb], writes=[mib])
            P.op("dve", lambda e, mi=mi, tm=tm: e.tensor_tensor(out=mi, in0=mi, in1=tm, op=ALU.add), reads=[tmb, mib], writes=[mib])
            mT, mTb = mTr.next()
            tpb = bankbf(6)
            for k in range(8):
                P.op("pe", lambda e, k=k, mi=mi: e.transpose(tpb[:, k * 128:(k + 1) * 128], mi[:, k * 128:(k + 1) * 128], identb), reads=[mib, bconst], writes=[bankb[6]])
            P.op("act", lambda e, mT=mT: e.activation(out=mT, in_=tpb, func=AF.Copy), reads=[bankb[6]], writes=[mTb])
            mTv = mT.rearrange("p (k t) -> p k t", k=8)
            for n in range(2):
                for k in range(8):
                    P.op("pe", lambda e, n=n, k=k, mTv=mTv: e.matmul(bank(4 + n), lhsT=mTv[:, k, :], rhs=wov[:, k, n * 512:(n + 1) * 512], start=(k == 0), stop=(k == 7)), reads=[mTb, bw], writes=[bankb[4 + n]])
            r_, rb_ = rr.next()
            P.op("dve", lambda e, r_=r_: e.tensor_tensor(out=r_, in0=PS[:, 2048:3072], in1=g1b, op=ALU.mult), reads=[bankb[4], bankb[5], bD], writes=[rb_])
            P.op("dve", lambda e, r_=r_, xt=xt: e.scalar_tensor_tensor(out=r_, in0=xt, scalar=ALPHA_C, in1=r_, op0=ALU.mult, op1=ALU.add), reads=[xb_, rb_], writes=[rb_])
            x1, x1b = x1r.next()
            st, _ = str_.next()
            stats, mv, rs = st
            P.op("dve", lambda e, stats=stats, r_=r_: e.bn_stats(out=stats[:, 0:6], in_=r_[:, 0:512]), reads=[rb_], writes=[x1b])
            P.op("dve", lambda e, stats=stats, r_=r_: e.bn_stats(out=stats[:, 6:12], in_=r_[:, 512:1024]), reads=[rb_], writes=[x1b])
            P.op("dve", lambda e, stats=stats, mv=mv: e.bn_aggr(out=mv, in_=stats), reads=[x1b], writes=[x1b])
            P.op("act", lambda e, rs=rs, mv=mv: e.activation(out=rs, in_=mv[:, 1:2], func=AF.Sqrt, bias=EPS), reads=[x1b], writes=[x1b])
            P.op("dve", lambda e, rs=rs: e.reciprocal(out=rs, in_=rs), reads=[x1b], writes=[x1b])
            P.op("dve", lambda e, x1=x1, r_=r_, mv=mv, rs=rs: e.tensor_scalar(out=x1, in0=r_, scalar1=mv[:, 0:1], scalar2=rs, op0=ALU.subtract, op1=ALU.mult), reads=[rb_, x1b], writes=[x1b])
            P.op("dve", lambda e, x1=x1: e.tensor_tensor(out=x1, in0=x1, in1=l1g, op=ALU.mult), reads=[x1b, bD], writes=[x1b])
            P.op("dve", lambda e, x1=x1: e.tensor_tensor(out=x1, in0=x1, in1=l1b, op=ALU.add), reads=[x1b, bD], writes=[x1b])
            P.dma("pool", x1s[j][tok:tok + 128, :], x1, reads=[x1b], writes=[DB("x1s%d" % j)])
            xn, xnb = xnr.next()
            st2, _ = str_.next()
            h2, h2b = h2r.next()
            h2v = h2.rearrange("p (k t) -> p k t", k=8)
            ln_to_hT(x1, x1b, (h2b, h2v), sc2p1, sh2, j, 7, [], st2, xn, xnb)
            P.dma("pool", h2T[j][:, :, 1 + tok:1 + tok + 128].rearrange("k p t -> p k t"), h2v, reads=[h2b], writes=[DB("h2T%d" % j)])

    def phaseD2(j, wup, wdn, bw, g2b, l2g, l2b, bD):
        So = SO[j]
        TB2 = min(256, So)
        ntb = TB2 // 128
        hr = Rot([ar.bf(8 * (TB2 + 2)) for _ in range(1)], "h2l")
        zr = Rot([ar.bf(NF * TB2) for _ in range(1)], "zT")
        ar_ = Rot([ar.f32(TB2) for _ in range(4)], "ca")
        br_ = Rot([ar.f32(TB2) for _ in range(4)], "cb")
        uvr = Rot([ar.bf(TB2) for _ in range(4)], "uvs")
        x1r = Rot([ar.f32(1024) for _ in range(1)], "x1l")
        rr = Rot([ar.f32(1024) for _ in range(1)], "r2")
        str_ = Rot([(ar.f32(12), ar.f32(2), ar.f32(1)) for _ in range(2)], "st3")
        P.dma("sp", g2b, adab[j, 1:2, :].partition_broadcast(128), reads=[DB("adab")], writes=[bD])
        wupv = wup.rearrange("p (k n) -> p k n", k=8)
        wdnv = wdn.rearrange("p (f n) -> p f n", f=NF)
        GC = 2.0 * math.sqrt(2.0 / math.pi)
        pbi = 0
        for blk in range(So // TB2):
            t0 = blk * TB2
            hh, hb = hr.next()
            hv = hh.rearrange("p (k t) -> p k t", k=8)
            P.dma("sp", hv, h2T[j][:, :, t0:t0 + TB2 + 2].rearrange("k p t -> p k t"), reads=[DB("h2T%d" % j)], writes=[hb])
            z, zb = zr.next()
            zv = z.rearrange("p (f t) -> p f t", f=NF)
            stash = {}

            def s12(f):
                nonlocal pbi
                bg_i = (pbi % 3) * 2
                pbi += 1
                for k in range(8):
                    P.op("pe", lambda e, f=f, k=k, bg_i=bg_i, hv=hv: e.matmul(bank(bg_i, TB2 + 2), lhsT=wupv[:, k, f * 128:(f + 1) * 128], rhs=hv[:, k, :], start=(k == 0), stop=(k == 7)), reads=[bw, hb], writes=[bankb[bg_i]])
                for k in range(8):
                    P.op("pe", lambda e, f=f, k=k, bg_i=bg_i, hv=hv: e.matmul(bank(bg_i + 1, TB2), lhsT=wupv[:, k, DFF + f * 128:DFF + (f + 1) * 128], rhs=hv[:, k, 1:TB2 + 1], start=(k == 0), stop=(k == 7)), reads=[bw, hb], writes=[bankb[bg_i + 1]])
                a, ab = ar_.next()
                b2, bb = br_.next()
                uvs, uvb = uvr.next()
                stash[f] = (a, ab, b2, bb, uvs, uvb)
                P.op("act", lambda e, uvs=uvs, bg_i=bg_i: e.activation(out=uvs, in_=bank(bg_i + 1, TB2), func=AF.Copy), reads=[bankb[bg_i + 1]], writes=[uvb])
                ug = bank(bg_i, TB2 + 2)
                P.op("dve", lambda e, a=a, ug=ug, f=f: e.tensor_scalar(out=a, in0=ug[:, 0:TB2], scalar1=cwT[:, 3 * f:3 * f + 1], scalar2=cbT[:, f:f + 1], op0=ALU.mult, op1=ALU.add), reads=[bankb[bg_i], bconst], writes=[ab])
                P.op("dve", lambda e, a=a, ug=ug, f=f: e.scalar_tensor_tensor(out=a, in0=ug[:, 1:TB2 + 1], scalar=cwT[:, 3 * f + 1:3 * f + 2], in1=a, op0=ALU.mult, op1=ALU.add), reads=[bankb[bg_i], bconst, ab], writes=[ab])
                P.op("dve", lambda e, a=a, ug=ug, f=f: e.scalar_tensor_tensor(out=a, in0=ug[:, 2:TB2 + 2], scalar=cwT[:, 3 * f + 2:3 * f + 3], in1=a, op0=ALU.mult, op1=ALU.add), reads=[bankb[bg_i], bconst, ab], writes=[ab])
                P.op("act", lambda e, a=a, b2=b2: e.activation(out=b2, in_=a, func=AF.Square, scale=math.sqrt(0.044715)), reads=[ab], writes=[bb])

            def s3(f):
                a, ab, b2, bb, uvs, uvb = stash[f]
                P.op("dve", lambda e, a=a, b2=b2: e.scalar_tensor_tensor(out=b2, in0=b2, scalar=1.0, in1=a, op0=ALU.add, op1=ALU.mult), reads=[ab, bb], writes=[bb])
                P.op("act", lambda e, b2=b2: e.activation(out=b2, in_=b2, func=AF.Sigmoid, scale=GC), reads=[bb], writes=[bb])

            def s4(f):
                a, ab, b2, bb, uvs, uvb = stash.pop(f)
                P.op("dve", lambda e, a=a, b2=b2: e.tensor_tensor(out=b2, in0=b2, in1=a, op=ALU.mult), reads=[ab, bb], writes=[bb])
                P.op("dve", lambda e, b2=b2, f=f, uvs=uvs, zv=zv: e.tensor_tensor(out=zv[:, f, :], in0=uvs, in1=b2, op=ALU.mult), reads=[uvb, bb], writes=[zb])

            for t in range(NF + 2):
                if t < NF:
                    s12(t)
                if 0 <= t - 1 < NF:
                    s3(t - 1)
                if 0 <= t - 2 < NF:
                    s4(t - 2)
            for ti in range(ntb):
                tok = t0 + ti * 128
                x1, x1b = x1r.next()
                P.dma("sp", x1, x1s[j][tok:tok + 128, :], reads=[DB("x1s%d" % j)], writes=[x1b])
                for n in range(2):
                    for f in range(NF):
                        P.op("pe", lambda e, n=n, f=f, ti=ti, zv=zv: e.matmul(bank(6 + n), lhsT=zv[:, f, ti * 128:(ti + 1) * 128], rhs=wdnv[:, f, n * 512:(n + 1) * 512], start=(f == 0), stop=(f == NF - 1)), reads=[zb, bw], writes=[bankb[6 + n]])
                r_, rb_ = rr.next()
                P.op("dve", lambda e, r_=r_: e.tensor_tensor(out=r_, in0=PS[:, 3072:4096], in1=g2b, op=ALU.mult), reads=[bankb[6], bankb[7], bD], writes=[rb_])
                P.op("dve", lambda e, r_=r_, x1=x1: e.scalar_tensor_tensor(out=r_, in0=x1, scalar=ALPHA_C, in1=r_, op0=ALU.mult, op1=ALU.add), reads=[x1b, rb_], writes=[rb_])
                yo, yob = r_, rb_
                (stats, mv, rs), _ = str_.next()
                P.op("dve", lambda e, stats=stats, r_=r_: e.bn_stats(out=stats[:, 0:6], in_=r_[:, 0:512]), reads=[rb_], writes=[yob])
                P.op("dve", lambda e, stats=stats, r_=r_: e.bn_stats(out=stats[:, 6:12], in_=r_[:, 512:1024]), reads=[rb_], writes=[yob])
                P.op("dve", lambda e, stats=stats, mv=mv: e.bn_aggr(out=mv, in_=stats), reads=[yob], writes=[yob])
                P.op("act", lambda e, rs=rs, mv=mv: e.activation(out=rs, in_=mv[:, 1:2], func=AF.Sqrt, bias=EPS), reads=[yob], writes=[yob])
                P.op("dve", lambda e, rs=rs: e.reciprocal(out=rs, in_=rs), reads=[yob], writes=[yob])
                P.op("dve", lambda e, yo=yo, r_=r_, mv=mv, rs=rs: e.tensor_scalar(out=yo, in0=r_, scalar1=mv[:, 0:1], scalar2=rs, op0=ALU.subtract, op1=ALU.mult), reads=[rb_, yob], writes=[yob])
                P.op("dve", lambda e, yo=yo: e.tensor_tensor(out=yo, in0=yo, in1=l2g, op=ALU.mult), reads=[yob, bD], writes=[yob])
                P.op("dve", lambda e, yo=yo: e.tensor_tensor(out=yo, in0=yo, in1=l2b, op=ALU.add), reads=[yob, bD], writes=[yob])
                P.dma("pool", yout[j][tok:tok + 128, :], yo, reads=[yob], writes=[DB("yout")], is_output=True)

    phase0()
    convert_w_in()
    barrier()
    for j in range(NJ):
        phaseA(j, xs[j], SO[j], True, ropeo[j])
        barrier()
    phaseA(2, xctx, Sc, False, ropec)
    barrier()
    for j in range(NJ):
        phaseB(j)
        barrier()
    for j in range(NJ):
        phaseC(j)
        barrier()
    wpa = ar.bf(8 * 1024)
    wpm = ar.bf(8 * 1024)
    wo = ar.bf(8 * 1024)
    g1b = ar.f32(1024)
    l1g = ar.f32(1024)
    l1b = ar.f32(1024)
    bw = Buf("wD1")
    bD = Buf("bD1")
    for wt_, src in ((wpa, w_pa), (wpm, w_pm), (wo, w_out)):
        P.dma("pool", wt_.rearrange("p (k n) -> p k n", k=8), src.rearrange("(k p) n -> p k n", p=128), writes=[bw])
    P.dma("sp", l1g, ln1g.partition_broadcast(128), writes=[bD])
    P.dma("sp", l1b, ln1b.partition_broadcast(128), writes=[bD])
    zt = ar.bf(16)
    bz = Buf("zt")
    P.op("dve", lambda e: e.memset(zt, 0.0), writes=[bz])
    for j in range(NJ):
        P.dma("sp", h2T[j][:, :, 0:1].rearrange("k p t -> p k t"), zt[:, 0:8].rearrange("p (k t) -> p k t", t=1), reads=[bz], writes=[DB("h2T%d" % j)], slow=True)
        P.dma("sp", h2T[j][:, :, SO[j] + 1:SO[j] + 2].rearrange("k p t -> p k t"), zt[:, 8:16].rearrange("p (k t) -> p k t", t=1), reads=[bz], writes=[DB("h2T%d" % j)], slow=True)
    mD = ar.mark()
    for j in range(NJ):
        phaseD1(j, wpa, wpm, wo, bw, g1b, l1g, l1b, bD)
        barrier()
        ar.reset(mD)
    ar.reset()
    wup = ar.bf(8 * 2 * DFF)
    wdn = ar.bf(NF * 1024)
    g2b = ar.f32(1024)
    l2g = ar.f32(1024)
    l2b = ar.f32(1024)
    bw2 = Buf("wD2")
    bD2 = Buf("bD2")
    wupv = wup.rearrange("p (k n) -> p k n", k=8)
    for k in range(8):
        P.dma("pool", wupv[:, k, :], w_up[k * 128:(k + 1) * 128, :], writes=[bw2])
    P.dma("pool", wdn.rearrange("p (f n) -> p f n", f=NF), w_down.rearrange("(f p) n -> p f n", p=128), writes=[bw2])
    P.dma("sp", l2g, ln2g.partition_broadcast(128), writes=[bD2])
    P.dma("sp", l2b, ln2b.partition_broadcast(128), writes=[bD2])
    mD = ar.mark()
    for j in range(NJ):
        phaseD2(j, wup, wdn, bw2, g2b, l2g, l2b, bD2)
        barrier()
        ar.reset(mD)
    P.emit()
    return nc


def rope_tab(pos):
    inv = (500000.0 ** (-np.arange(0, 16, 2, dtype=np.float32) / np.float32(16))).astype(np.float32)
    ang = pos.astype(np.float32)[:, None] * inv[None, :]
    return np.concatenate([np.cos(ang), np.sin(ang)], axis=1).astype(np.float32)


_NC_CACHE = {}


def kernel(x_prompt, x_sample, c_prompt, c_sample, w_ada, b_ada, w_in, lambda_qk, subln_g,
           b_mgate, mnorm_g, w_pa, w_pm, w_out, ln1_g, ln1_b, w_up, conv_w, conv_b, w_down,
           ln2_g, ln2_b):
    f = np.float32
    x_prompt = np.asarray(x_prompt, f)
    x_sample = np.asarray(x_sample, f)
    Bp, Sc, _ = x_prompt.shape
    Bs, So, _ = x_sample.shape
    ncores = 8
    assert Bs == 2 * ncores and Bp * Sc == ncores * So
    nq = Sc // So
    key = (So, Sc)
    if key not in _NC_CACHE:
        _NC_CACHE[key] = build(So, Sc)
    nc = _NC_CACHE[key]
    c_prompt = np.asarray(c_prompt, f)
    c_sample = np.asarray(c_sample, f)
    tri = np.triu(np.ones((128, 128), f))
    sel = np.zeros((4, 4, 128), f)
    for h in range(4):
        sel[h, h, :] = 1.0
    shared = {
        "w_ada": np.ascontiguousarray(np.asarray(w_ada, f)[0]),
        "b_adaT": np.ascontiguousarray(np.asarray(b_ada, f)[0].reshape(48, 128).T),
        "b_ada": np.ascontiguousarray(np.asarray(b_ada, f)[0].reshape(1, -1)),
        "w_in": np.ascontiguousarray(np.asarray(w_in, f)[0]),
        "lambda_qk": np.ascontiguousarray(np.asarray(lambda_qk, f)[0].reshape(1, 256)),
        "subln": np.ascontiguousarray(np.asarray(subln_g, f)[0].reshape(128, 1)),
        "bmgT": np.ascontiguousarray(np.asarray(b_mgate, f)[0].reshape(4, 4).T),
        "mnorm": np.ascontiguousarray(np.asarray(mnorm_g, f)[0].reshape(1, 1024)),
        "w_pa": np.ascontiguousarray(np.asarray(w_pa, f)[0]),
        "w_pm": np.ascontiguousarray(np.asarray(w_pm, f)[0]),
        "w_out": np.ascontiguousarray(np.asarray(w_out, f)[0]),
        "ln1g": np.ascontiguousarray(np.asarray(ln1_g, f)[0].reshape(1, -1)),
        "ln1b": np.ascontiguousarray(np.asarray(ln1_b, f)[0].reshape(1, -1)),
        "w_up": np.ascontiguousarray(np.asarray(w_up, f)[0]),
        "convwT": np.ascontiguousarray(np.asarray(conv_w, f)[0].reshape(3, NF, 128).transpose(2, 1, 0)),
        "convbT": np.ascontiguousarray(np.asarray(conv_b, f)[0].reshape(NF, 128).T),
        "w_down": np.ascontiguousarray(np.asarray(w_down, f)[0]),
        "ln2g": np.ascontiguousarray(np.asarray(ln2_g, f)[0].reshape(1, -1)),
        "ln2b": np.ascontiguousarray(np.asarray(ln2_b, f)[0].reshape(1, -1)),
        "ident": np.eye(128, dtype=f),
        "sel": sel.reshape(4, 512),
        "maskF": tri,
        "maskB": np.ascontiguousarray(tri.T),
        "ropec": rope_tab(np.arange(Sc)),
    }
    in_maps = []
    offs = []
    So2 = So + 256 if Sc > So + 256 else So
    for c in range(ncores):
        pb, pq = c // nq, c % nq
        m = dict(shared)
        m["xs"] = np.ascontiguousarray(np.stack([x_sample[2 * c], x_sample[2 * c + 1]]))
        st2 = min(max(pq * So - 128, 0), Sc - So2)
        offs.append(pq * So - st2)
        m["xs2"] = np.ascontiguousarray(x_prompt[pb, st2:st2 + So2])
        m["xctx"] = np.ascontiguousarray(x_prompt[pb])
        cj = np.stack([c_sample[2 * c], c_sample[2 * c + 1], c_prompt[pb]])
        m["cT"] = np.ascontiguousarray(cj.reshape(3, 8, 128).transpose(2, 1, 0))
        ro = rope_tab(np.arange(So))
        m["ropeo"] = np.ascontiguousarray(np.stack([ro, ro]))
        m["ropeo2"] = rope_tab(np.arange(st2, st2 + So2))
        pos = np.arange(Sc)
        m["mb"] = np.ascontiguousarray(np.tile((pos < st2).astype(f)[None, :], (4, 1)))
        m["ma"] = np.ascontiguousarray(np.tile((pos >= st2 + So2).astype(f)[None, :], (4, 1)))
        in_maps.append(m)
    res = run_bass_kernel_spmd(nc, in_maps, core_ids=list(range(ncores)))
    y_prompt = np.zeros((Bp, Sc, D), f)
    y_sample = np.zeros((Bs, So, D), f)
    for c in range(ncores):
        y = res.results[c]["y"]
        y2 = res.results[c]["y2"]
        pb, pq = c // nq, c % nq
        y_sample[2 * c] = y[0]
        y_sample[2 * c + 1] = y[1]
        y_prompt[pb, pq * So:(pq + 1) * So] = y2[offs[c]:offs[c] + So]
    return (y_prompt, y_sample)
```

```python
import concourse.bass as bass
import concourse.mybir as mybir

F32 = mybir.dt.float32
BF16 = mybir.dt.bfloat16
AF = mybir.ActivationFunctionType
ALU = mybir.AluOpType
AX = mybir.AxisListType

COMPUTE = ("pe", "act", "dve", "pool")
QUEUES = ("sp", "act", "pool")
N_DMA_SEMS = 24


class Buf:
    __slots__ = ("name", "writers", "readers")

    def __init__(self, name=""):
        self.name = name
        self.writers = {}
        self.readers = {}


class Op:
    __slots__ = ("eng", "fn", "deps", "sig", "sem", "val", "idx", "is_dma", "key", "pre")

    def __init__(self, eng, fn):
        self.eng = eng
        self.fn = fn
        self.deps = []
        self.sig = False
        self.sem = None
        self.val = None
        self.idx = None
        self.is_dma = False
        self.key = eng
        self.pre = None


class Prog:
    def __init__(self, nc):
        self.nc = nc
        self.streams = {e: [] for e in ("pe", "act", "dve", "pool", "sp")}
        self.dma_count = [0] * N_DMA_SEMS
        self.dma_rr = 0
        self.out_dmas = []
        self.fence = []
        self.last_dma = [None] * N_DMA_SEMS

    def _add(self, op, reads, writes):
        st = self.streams[op.eng]
        op.idx = len(st)
        deps = list(self.fence)
        for b in reads:
            for k, w in b.writers.items():
                deps.append(w)
        for b in writes:
            for k, r in b.readers.items():
                deps.append(r)
            for k, w in b.writers.items():
                deps.append(w)
        seen = set()
        for d in deps:
            if d is op or id(d) in seen:
                continue
            seen.add(id(d))
            if (not d.is_dma) and d.eng == op.eng:
                if not op.is_dma:
                    if op.eng == "pe":
                        continue
                    if op.idx - d.idx > 3:
                        continue
            op.deps.append(d)
        for b in reads:
            b.readers[op.key] = op
        for b in writes:
            b.writers[op.key] = op
        st.append(op)
        return op

    def op(self, eng, fn, reads=(), writes=()):
        return self._add(Op(eng, fn), reads, writes)

    def dma(self, q, out, in_, reads=(), writes=(), is_output=False, slow=False):
        k = self.dma_rr
        self.dma_rr = (self.dma_rr + 1) % N_DMA_SEMS
        self.dma_count[k] += 1
        n = self.dma_count[k]
        o = Op(q, (lambda e, out=out, in_=in_: e.dma_start(out=out, in_=in_, allow_slow_non_contiguous=True)) if slow else (lambda e, out=out, in_=in_: e.dma_start(out=out, in_=in_)))
        o.is_dma = True
        o.key = ("dma", k)
        o.sem = k
        o.val = 16 * n
        o.pre = (k, 16 * (n - 1))
        self._add(o, reads, writes)
        self.last_dma[k] = o
        if is_output:
            self.out_dmas.append(o)
        return o

    def emit(self):
        nc = self.nc
        import contextlib
        with contextlib.ExitStack() as es:
            esem = {e: es.enter_context(nc.semaphore("sem_" + e)) for e in COMPUTE}
            dsem = [es.enter_context(nc.semaphore("dsem%d" % i)) for i in range(N_DMA_SEMS)]
            for e, st in self.streams.items():
                for o in st:
                    for d in o.deps:
                        if not d.is_dma:
                            d.sig = True
            for e, st in self.streams.items():
                c = 0
                for o in st:
                    if o.is_dma:
                        o.sem = dsem[o.sem] if isinstance(o.sem, int) else o.sem
                        continue
                    o.sem = esem[e]
                    if o.sig:
                        c += 1
                        o.val = c
            block = es.enter_context(nc.Block())
            engmap = {"pe": "tensor", "act": "scalar", "dve": "vector", "pool": "gpsimd", "sp": "sync"}
            out_dmas = self.out_dmas

            def make(ename, st, final):
                def body(eng):
                    waited = {}
                    for o in st:
                        if o.pre is not None:
                            s = dsem[o.pre[0]]
                            v = o.pre[1]
                            if v > 0 and waited.get(id(s), 0) < v:
                                eng.wait_ge(s, v)
                                waited[id(s)] = v
                        for d in o.deps:
                            if waited.get(id(d.sem), 0) < d.val:
                                eng.wait_ge(d.sem, d.val)
                                waited[id(d.sem)] = d.val
                        ins = o.fn(eng)
                        if o.is_dma:
                            ins.then_inc(o.sem, 16)
                        elif o.sig:
                            ins.then_inc(o.sem, 1)
                    if final:
                        for o in out_dmas:
                            if waited.get(id(o.sem), 0) < o.val:
                                eng.wait_ge(o.sem, o.val)
                                waited[id(o.sem)] = o.val
                return body

            for e, st in self.streams.items():
                deco = getattr(block, engmap[e])
                deco(make(e, st, e == "sp"))

import math
import numpy as np
import ml_dtypes
from concourse.bass_utils import run_bass_kernel_spmd

D = 1024
HA = 8
HM = 4
DFF = 2816
NF = 22
DIN = 8208
EPS = 1e-5
ALPHA_C = 2.0 ** 0.25
LAM_INIT = 0.8 - 0.6 * math.exp(-0.3 * 0)
LNSC = -0.5 * math.log(128.0)
NEG = -1.0e4
COLS = dict(aq=(0, 1024), ak=(1024, 2048), av=(2048, 3072), mq=(3072, 3584), mk=(3584, 4096),
            mv=(4096, 5120), mo=(5120, 6144), mg=(6144, 6160), bg=(6160, 8208))
ARENA_ELEMS = 94208


class Arena:
    def __init__(self, A, total):
        self.A = A
        self.lo = 0
        self.hi = total
        self.base = 0

    def bf(self, n, persist=False):
        n2 = (n + 15) // 16 * 16
        if persist:
            self.hi -= n2
            off = self.hi
        else:
            off = self.lo
            self.lo += n2
        assert self.lo <= self.hi, ("arena overflow", self.lo, self.hi)
        return self.A[:, off:off + n]

    def f32(self, n, persist=False):
        return self.bf(2 * n, persist).bitcast(F32)

    def mark(self):
        return self.lo

    def reset(self, m=0):
        self.lo = m


class Rot:
    def __init__(self, aps, name="r"):
        self.aps = aps
        self.bufs = [Buf(name + str(i)) for i in range(len(aps))]
        self.i = 0

    def next(self):
        k = self.i % len(self.aps)
        self.i += 1
        return self.aps[k], self.bufs[k]


def build(So, Sc):
    nc = bass.Bass("TRN2", target_bir_lowering=False)
    NJ = 3
    So2 = So + 256 if Sc > So + 256 else So
    SO = [So, So, So2]
    nchc = Sc // 128

    def din(name, shape, dt=F32):
        return nc.dram_tensor(name, list(shape), dt, kind="ExternalInput").ap()

    def dscr(name, shape, dt=BF16):
        return nc.dram_tensor(name, list(shape), dt).ap()

    xs01 = din("xs", [2, So, D])
    xs2 = din("xs2", [So2, D])
    xs = [xs01[0], xs01[1], xs2]
    xctx = din("xctx", [Sc, D])
    cT = din("cT", [128, 8, NJ])
    ropeo01 = din("ropeo", [2, So, 16])
    ropeo2 = din("ropeo2", [So2, 16])
    ropeo = [ropeo01[0], ropeo01[1], ropeo2]
    ropec = din("ropec", [Sc, 16])
    mb_in = din("mb", [4, Sc])
    ma_in = din("ma", [4, Sc])
    w_ada = din("w_ada", [D, 6 * D])
    b_adaT = din("b_adaT", [128, 48])
    b_ada = din("b_ada", [1, 6 * D])
    w_in = din("w_in", [D, DIN])
    lambda_qk = din("lambda_qk", [1, 256])
    subln = din("subln", [128, 1])
    bmgT = din("bmgT", [4, 4])
    mnorm = din("mnorm", [1, 1024])
    w_pa = din("w_pa", [D, D])
    w_pm = din("w_pm", [D, D])
    w_out = din("w_out", [D, D])
    ln1g = din("ln1g", [1, D])
    ln1b = din("ln1b", [1, D])
    w_up = din("w_up", [D, 2 * DFF])
    convwT = din("convwT", [128, NF, 3])
    convbT = din("convbT", [128, NF])
    w_down = din("w_down", [DFF, D])
    ln2g = din("ln2g", [1, D])
    ln2b = din("ln2b", [1, D])
    ident_in = din("ident", [128, 128])
    sel_in = din("sel", [4, 512])
    maskF_in = din("maskF", [128, 128])
    maskB_in = din("maskB", [128, 128])
    yout01 = nc.dram_tensor("y", [2, So, D], F32, kind="ExternalOutput").ap()
    yout2 = nc.dram_tensor("y2", [So2, D], F32, kind="ExternalOutput").ap()
    yout = [yout01[0], yout01[1], yout2]

    Sctx = [So, So, Sc]
    QT = [dscr("QT%d" % j, [HA, 128, SO[j]]) for j in range(NJ)]
    KT = [dscr("KT%d" % j, [HA, 128, Sctx[j]]) for j in range(NJ)]
    VV = [dscr("VV%d" % j, [HA, Sctx[j], 128]) for j in range(NJ)]
    mqT = [dscr("mqT%d" % j, [HM, 128, SO[j]]) for j in range(NJ)]
    mkT = [dscr("mkT%d" % j, [HM, 128, SO[j]]) for j in range(NJ)]
    mkt = [dscr("mkt%d" % j, [SO[j], 512]) for j in range(NJ)]
    mvt = [dscr("mvt%d" % j, [SO[j], 1024]) for j in range(NJ)]
    mktc = dscr("mktc", [Sc, 512])
    mvtc = dscr("mvtc", [Sc, 1024])
    GG = [dscr("GG%d" % j, [16, SO[j]], F32) for j in range(NJ)]
    GGc = dscr("GGc", [16, Sc], F32)
    AMs = dscr("AMs", [2, 4, Sc], F32)
    smo = [dscr("smo%d" % j, [SO[j], 1024]) for j in range(NJ)]
    sbg = [dscr("sbg%d" % j, [SO[j], 2048]) for j in range(NJ)]
    yaT = [dscr("yaT%d" % j, [HA, 128, SO[j]]) for j in range(NJ)]
    ymT = [dscr("ymT%d" % j, [8, 128, SO[j]]) for j in range(NJ)]
    x1s = [dscr("x1s%d" % j, [SO[j], D], F32) for j in range(NJ)]
    h2T = [dscr("h2T%d" % j, [8, 128, SO[j] + 2]) for j in range(NJ)]
    adab = dscr("adab", [NJ, 2, D], F32)
    wb_in = dscr("wb_in", [128, 8, DIN])

    P = Prog(nc)
    A_ = nc.alloc_sbuf_tensor("arena", [128, ARENA_ELEMS], BF16).ap()
    PS = nc.alloc_psum_tensor("psum", [128, 4096], F32).ap()
    ar = Arena(A_, ARENA_ELEMS)
    bankb = [Buf("bank%d" % i) for i in range(8)]

    def bank(i, n=512, off=0):
        return PS[:, i * 512 + off:i * 512 + off + n]

    def bankbf(i):
        return PS[:, i * 512:(i + 1) * 512].bitcast(BF16)

    dbuf = {}

    def DB(name):
        if name not in dbuf:
            dbuf[name] = Buf(name)
        return dbuf[name]

    identf = ar.f32(128, True)
    identb = ar.bf(128, True)
    maskF = ar.bf(128, True)
    maskB = ar.bf(128, True)
    onesb = ar.bf(128, True)
    onesf = ar.f32(128, True)
    self_ = ar.f32(512, True)
    sc1p1 = ar.f32(NJ * 8, True)
    sh1 = ar.f32(NJ * 8, True)
    sc2p1 = ar.f32(NJ * 8, True)
    sh2 = ar.f32(NJ * 8, True)
    small = ar.f32(64, True)
    lamc = small[:, 0:1]
    nlam = small[:, 1:2]
    gsc = small[:, 2:3]
    sublc = small[:, 3:4]
    bmg = small[0:4, 8:12]
    zero4 = small[0:4, 12:13]
    cwT = ar.f32(NF * 3, True)
    cbT = ar.f32(NF, True)
    bconst = Buf("const")
    tmpf = ar.f32(128)
    tmpm = ar.f32(256)
    P.dma("sp", identf, ident_in, writes=[bconst])
    P.dma("sp", tmpm[:, 0:128], maskF_in, writes=[bconst])
    P.dma("sp", tmpm[:, 128:256], maskB_in, writes=[bconst])
    P.dma("sp", self_[0:4, :], sel_in, writes=[bconst])
    P.dma("sp", sublc, subln, writes=[bconst])
    P.dma("sp", bmg, bmgT, writes=[bconst])
    P.dma("sp", cwT, convwT.rearrange("p f k -> p (f k)"), writes=[bconst])
    P.dma("sp", cbT, convbT, writes=[bconst])
    P.op("dve", lambda e: e.tensor_copy(out=identb, in_=identf), reads=[bconst], writes=[bconst])
    P.op("dve", lambda e: e.tensor_copy(out=maskF, in_=tmpm[:, 0:128]), reads=[bconst], writes=[bconst])
    P.op("dve", lambda e: e.tensor_copy(out=maskB, in_=tmpm[:, 128:256]), reads=[bconst], writes=[bconst])
    P.op("dve", lambda e: e.memset(onesb, 1.0), writes=[bconst])
    P.op("dve", lambda e: e.memset(onesf, 1.0), writes=[bconst])
    P.op("dve", lambda e: e.memset(zero4, 0.0), writes=[bconst])
    lq = ar.f32(256)
    lt = ar.f32(128)
    l2 = ar.f32(2)
    P.dma("sp", lq, lambda_qk.partition_broadcast(128), writes=[bconst])
    lqv = lq.rearrange("p (a b d) -> p a b d", a=2, b=2)
    P.op("dve", lambda e: e.tensor_tensor(out=lt.rearrange("p (a d) -> p a d", a=2), in0=lqv[:, :, 0, :], in1=lqv[:, :, 1, :], op=ALU.mult), reads=[bconst], writes=[bconst])
    P.op("dve", lambda e: e.tensor_reduce(out=l2, in_=lt.rearrange("p (a d) -> p a d", a=2), axis=AX.X, op=ALU.add), reads=[bconst], writes=[bconst])
    P.op("act", lambda e: e.activation(out=l2, in_=l2, func=AF.Exp), reads=[bconst], writes=[bconst])
    P.op("dve", lambda e: e.tensor_tensor(out=lamc, in0=l2[:, 0:1], in1=l2[:, 1:2], op=ALU.subtract), reads=[bconst], writes=[bconst])
    P.op("dve", lambda e: e.tensor_scalar(out=nlam, in0=lamc, scalar1=LAM_INIT, scalar2=-1.0, op0=ALU.add, op1=ALU.mult), reads=[bconst], writes=[bconst])
    P.op("dve", lambda e: e.tensor_scalar(out=gsc, in0=sublc, scalar1=(1.0 - LAM_INIT), scalar2=None, op0=ALU.mult), reads=[bconst], writes=[bconst])

    bar_bufs = {e: Buf("bar_" + e) for e in COMPUTE}
    bar_t = ar.f32(8, True)
    bar_tb = ar.bf(16, True)

    def barrier():
        s1 = []
        s1.append(P.op("dve", lambda e: e.memset(bar_t[0:1, 0:1], 0.0), writes=[bar_bufs["dve"]]))
        s1.append(P.op("pool", lambda e: e.memset(bar_t[0:1, 2:3], 0.0), writes=[bar_bufs["pool"]]))
        s1.append(P.op("act", lambda e: e.activation(out=bar_t[0:1, 4:5], in_=identf[0:1, 0:1], func=AF.Copy), writes=[bar_bufs["act"]]))
        s1.append(P.op("pe", lambda e: e.matmul(PS[0:1, 7 * 512:7 * 512 + 1], lhsT=identb[0:1, 0:1], rhs=identb[0:1, 0:1], start=True, stop=True),
                       reads=[bconst], writes=[bar_bufs["pe"], bankb[7]]))
        P.fence = list(s1) + [o for o in P.last_dma if o is not None]
        ar.reset()

    def convert_w_in():
        for k in range(8):
            P.dma("pool", wb_in[:, k, :], w_in[k * 128:(k + 1) * 128, :], writes=[DB("wb_in")])

    def phase0():
        cTt = ar.f32(8 * NJ)
        scb = ar.bf(8 * NJ)
        b0 = Buf("p0")
        P.dma("sp", cTt, cT.rearrange("p k j -> p (k j)"), writes=[b0])
        P.op("act", lambda e: e.activation(out=scb, in_=cTt, func=AF.Silu), reads=[b0], writes=[b0])
        scbv = scb.rearrange("p (k j) -> p k j", j=NJ)
        badT = ar.f32(48)
        P.dma("sp", badT, b_adaT, writes=[b0])
        wr = Rot([ar.bf(8 * 512) for _ in range(2)], "wada")
        rowt = Rot([ar.f32(512) for _ in range(2)], "rowt")
        brow = Rot([ar.f32(512) for _ in range(2)], "brow")
        dst = {0: (sh1, 0.0), 1: (sc1p1, 1.0), 3: (sh2, 0.0), 4: (sc2p1, 1.0)}
        for gi in range(12):
            wt, wb = wr.next()
            wv = wt.rearrange("p (k n) -> p k n", k=8)
            P.dma("pool", wv, w_ada[:, gi * 512:(gi + 1) * 512].rearrange("(k p) n -> p k n", p=128), writes=[wb])
            v = gi // 2
            if v in dst:
                tgt, addc = dst[v]
                for q in range(4):
                    kc = (gi % 2) * 4 + q
                    for k in range(8):
                        P.op("pe", lambda e, k=k, q=q, wv=wv: e.matmul(bank(0, NJ, 0), lhsT=wv[:, k, q * 128:(q + 1) * 128], rhs=scbv[:, k, :], start=(k == 0), stop=(k == 7)),
                             reads=[wb, b0], writes=[bankb[0]])
                    tv = tgt.rearrange("p (j k) -> p j k", j=NJ)[:, :, kc]
                    col = v * 8 + kc
                    P.op("dve", lambda e, tv=tv, col=col, addc=addc: e.tensor_scalar(out=tv, in0=bank(0, NJ, 0), scalar1=badT[:, col:col + 1], scalar2=addc, op0=ALU.add, op1=ALU.add),
                         reads=[bankb[0], b0], writes=[bconst])
            else:
                gidx = 0 if v == 2 else 1
                half = gi % 2
                bt, bb = brow.next()
                P.dma("sp", bt[0:1, :], b_ada[:, gi * 512:(gi + 1) * 512], writes=[bb])
                for j in range(NJ):
                    for k in range(8):
                        P.op("pe", lambda e, k=k, j=j, wv=wv: e.matmul(bank(1)[0:1, :], lhsT=scbv[:, k, j:j + 1], rhs=wv[:, k, :], start=(k == 0), stop=(k == 7)),
                             reads=[wb, b0], writes=[bankb[1]])
                    rt, rb = rowt.next()
                    P.op("dve", lambda e, rt=rt, bt=bt: e.tensor_tensor(out=rt[0:1, :], in0=bank(1)[0:1, :], in1=bt[0:1, :], op=ALU.add), reads=[bankb[1], bb], writes=[rb])
                    P.dma("sp", adab[j, gidx:gidx + 1, half * 512:(half + 1) * 512], rt[0:1, :], reads=[rb], writes=[DB("adab")])

    def ln_to_hT(xt, xb_, hT_dst, scp, shp, j, tr_bank, extra_reads, st, xn, xnb):
        stats, mv, rs = st
        P.op("dve", lambda e: e.bn_stats(out=stats[:, 0:6], in_=xt[:, 0:512]), reads=[xb_] + extra_reads, writes=[xnb])
        P.op("dve", lambda e: e.bn_stats(out=stats[:, 6:12], in_=xt[:, 512:1024]), reads=[xb_], writes=[xnb])
        P.op("dve", lambda e: e.bn_aggr(out=mv, in_=stats), reads=[xnb], writes=[xnb])
        P.op("act", lambda e: e.activation(out=rs, in_=mv[:, 1:2], func=AF.Sqrt, bias=EPS), reads=[xnb], writes=[xnb])
        P.op("dve", lambda e: e.reciprocal(out=rs, in_=rs), reads=[xnb], writes=[xnb])
        P.op("dve", lambda e: e.tensor_scalar(out=xn, in0=xt, scalar1=mv[:, 0:1], scalar2=rs, op0=ALU.subtract, op1=ALU.mult), reads=[xb_, xnb], writes=[xnb])
        pb = bankbf(tr_bank)
        for k in range(8):
            P.op("pe", lambda e, k=k: e.transpose(pb[:, k * 128:(k + 1) * 128], xn[:, k * 128:(k + 1) * 128], identb), reads=[xnb, bconst], writes=[bankb[tr_bank]])
        for k in range(8):
            P.op("act", lambda e, k=k: e.activation(out=hT_dst[1][:, k, :], in_=pb[:, k * 128:(k + 1) * 128], func=AF.Identity,
                                                    scale=scp[:, j * 8 + k:j * 8 + k + 1], bias=shp[:, j * 8 + k:j * 8 + k + 1]),
                 reads=[bankb[tr_bank], bconst], writes=[hT_dst[0]])

    def phaseA(j, xsrc, S, own, ropetab):
        TB = min(1024, S)
        nt = TB // 128
        xr = Rot([ar.f32(1024) for _ in range(2)], "x")
        xnr = Rot([ar.bf(1024) for _ in range(2)], "xn")
        str_ = Rot([(ar.f32(12), ar.f32(2), ar.f32(1)) for _ in range(2)], "st")
        hTr = Rot([ar.bf(8 * TB) for _ in range(2)], "hT")
        wr = Rot([ar.bf(8 * 512) for _ in range(4)], "w")
        stq = Rot([ar.bf(4 * TB) for _ in range(2)], "stq")
        stt = Rot([ar.bf(nt * 512) for _ in range(3)], "stt")
        rpb = Rot([ar.bf(512) for _ in range(2)], "rpb")
        rtm = Rot([ar.f32(4 * 64) for _ in range(2)], "rtm")
        ropr = Rot([ar.f32(nt * 16) for _ in range(2)], "rop")
        gst = Rot([ar.f32(4 * TB) for _ in range(2)], "gst")
        pbank = [2, 3, 4, 5]
        pbi = [0]
        tbk = [6, 7]
        tbi = [0]
        groups = []
        if own and j == 2:
            names = ["aq", "mq", "mk", "mv", "mo", "mg", "bg"]
        elif own:
            names = ["aq", "ak", "av", "mq", "mk", "mv", "mo", "mg", "bg"]
        else:
            names = ["ak", "av", "mk", "mv", "mg"]
        for nm in names:
            lo, hi = COLS[nm]
            ng = max(1, (hi - lo) // 512)
            for g in range(ng):
                groups.append((nm, g, lo + g * 512, min(512, hi - lo)))
        Kd = KT[j]
        Vd = VV[j]
        for blk in range((S + TB - 1) // TB):
            t0 = blk * TB
            tb = min(TB, S - t0)
            nt = tb // 128
            HB = min(512, tb)
            assert tb % HB == 0
            hT, hTb = hTr.next()
            hTv = hT[:, 0:8 * tb].rearrange("p (k t) -> p k t", k=8)
            rop, ropb = ropr.next()
            ropv = rop[:, 0:nt * 16].rearrange("p (t c) -> p t c", c=16)
            P.dma("sp", ropv, ropetab[t0:t0 + tb, :].rearrange("(t p) c -> p t c", p=128), writes=[ropb])
            for ti in range(nt):
                xt, xb_ = xr.next()
                xn, xnb = xnr.next()
                st, _ = str_.next()
                P.dma("sp", xt, xsrc[t0 + ti * 128:t0 + (ti + 1) * 128, :], writes=[xb_])
                ln_to_hT(xt, xb_, (hTb, hTv[:, :, ti * 128:(ti + 1) * 128]), sc1p1, sh1, j, ti % 2, [], st, xn, xnb)
            for (nm, g, c0, ncol) in groups:
                wt, wb = wr.next()
                wv = wt.rearrange("p (k n) -> p k n", k=8)
                P.dma("sp", wv[:, :, 0:ncol], wb_in[:, :, c0:c0 + ncol], reads=[DB("wb_in")], writes=[wb])
                if nm in ("aq", "ak", "av", "mk", "mv", "mo", "bg"):
                    if nm in ("aq", "ak"):
                        sq, sqb = stq.next()
                        sqv = sq[:, 0:4 * tb].rearrange("p (h t) -> p h t", h=4)
                    else:
                        sblk, sb_ = stt.next()
                        sv = sblk[:, 0:nt * 512].rearrange("p (t c) -> p t c", c=512)
                    for ti in range(nt):
                        pb_i = pbank[pbi[0] % 4]
                        pbi[0] += 1
                        for k in range(8):
                            P.op("pe", lambda e, k=k, ti=ti, pb_i=pb_i, wv=wv, hTv=hTv: e.matmul(bank(pb_i), lhsT=hTv[:, k, ti * 128:(ti + 1) * 128], rhs=wv[:, k, :], start=(k == 0), stop=(k == 7)),
                                 reads=[hTb, wb], writes=[bankb[pb_i]])
                        tok = t0 + ti * 128
                        if nm in ("aq", "ak"):
                            rb_, rbb = rpb.next()
                            tm, tmb = rtm.next()
                            psv = bank(pb_i).rearrange("p (h d) -> p h d", h=8)
                            rv = rb_.rearrange("p (h d) -> p h d", h=8)
                            tmv = tm.rearrange("p (a h d) -> p a h d", a=4, h=8)
                            cosb = ropv[:, ti, 0:8].unsqueeze(1).to_broadcast([128, 8, 8])
                            sinb = ropv[:, ti, 8:16].unsqueeze(1).to_broadcast([128, 8, 8])
                            rd = [bankb[pb_i], ropb]
                            P.op("dve", lambda e, psv=psv, tmv=tmv, cosb=cosb: e.tensor_tensor(out=tmv[:, 0], in0=psv[:, :, 0:8], in1=cosb, op=ALU.mult), reads=rd, writes=[tmb])
                            P.op("dve", lambda e, psv=psv, tmv=tmv, sinb=sinb: e.tensor_tensor(out=tmv[:, 1], in0=psv[:, :, 8:16], in1=sinb, op=ALU.mult), reads=rd, writes=[tmb])
                            P.op("dve", lambda e, psv=psv, tmv=tmv, cosb=cosb: e.tensor_tensor(out=tmv[:, 2], in0=psv[:, :, 8:16], in1=cosb, op=ALU.mult), reads=rd, writes=[tmb])
                            P.op("dve", lambda e, psv=psv, tmv=tmv, sinb=sinb: e.tensor_tensor(out=tmv[:, 3], in0=psv[:, :, 0:8], in1=sinb, op=ALU.mult), reads=rd, writes=[tmb])
                            P.op("dve", lambda e, rv=rv, psv=psv: e.tensor_copy(out=rv[:, :, 16:64], in_=psv[:, :, 16:64]), reads=rd, writes=[rbb])
                            P.op("dve", lambda e, rv=rv, tmv=tmv: e.tensor_tensor(out=rv[:, :, 0:8], in0=tmv[:, 0], in1=tmv[:, 1], op=ALU.subtract), reads=[tmb], writes=[rbb])
                            P.op("dve", lambda e, rv=rv, tmv=tmv: e.tensor_tensor(out=rv[:, :, 8:16], in0=tmv[:, 2], in1=tmv[:, 3], op=ALU.add), reads=[tmb], writes=[rbb])
                            tb_i = tbk[tbi[0] % 2]
                            tbi[0] += 1
                            tpb = bankbf(tb_i)
                            for hh in range(4):
                                P.op("pe", lambda e, hh=hh, tpb=tpb, rb_=rb_: e.transpose(tpb[:, hh * 128:(hh + 1) * 128], rb_[:, hh * 128:(hh + 1) * 128], identb),
                                     reads=[rbb, bconst], writes=[bankb[tb_i]])
                            P.op("act", lambda e, sqv=sqv, tpb=tpb, ti=ti: e.activation(out=sqv[:, :, ti * 128:(ti + 1) * 128], in_=tpb[:, 0:512].rearrange("p (h t) -> p h t", h=4), func=AF.Copy),
                                 reads=[bankb[tb_i]], writes=[sqb])
                        else:
                            s_ = sv[:, ti, :]
                            if nm in ("mo", "bg"):
                                P.op("act", lambda e, s_=s_, pb_i=pb_i: e.activation(out=s_, in_=bank(pb_i), func=AF.Sigmoid), reads=[bankb[pb_i]], writes=[sb_])
                            else:
                                P.op("act", lambda e, s_=s_, pb_i=pb_i: e.activation(out=s_, in_=bank(pb_i), func=AF.Copy), reads=[bankb[pb_i]], writes=[sb_])
                    if nm == "av":
                        for hh in range(4):
                            P.dma("pool", Vd[4 * g + hh, t0:t0 + tb, :].rearrange("(t p) d -> p t d", p=128), sv[:, :, hh * 128:(hh + 1) * 128], reads=[sb_], writes=[DB("VV%d" % j)])
                    elif nm == "mv":
                        dd = mvt[j] if own else mvtc
                        P.dma("pool", dd[t0:t0 + tb, g * 512:(g + 1) * 512].rearrange("(t p) c -> p t c", p=128), sv, reads=[sb_], writes=[DB("mvt%d%d" % (j, own))])
                    elif nm == "mk":
                        dd = mkt[j] if own else mktc
                        P.dma("pool", dd[t0:t0 + tb, :].rearrange("(t p) c -> p t c", p=128), sv, reads=[sb_], writes=[DB("mkt%d%d" % (j, own))])
                    elif nm == "mo":
                        P.dma("pool", smo[j][t0:t0 + tb, g * 512:(g + 1) * 512].rearrange("(t p) c -> p t c", p=128), sv, reads=[sb_], writes=[DB("smo%d" % j)])
                    elif nm == "bg":
                        P.dma("pool", sbg[j][t0:t0 + tb, g * 512:(g + 1) * 512].rearrange("(t p) c -> p t c", p=128), sv, reads=[sb_], writes=[DB("sbg%d" % j)])
                    if nm == "aq":
                        P.dma("pool", QT[j][4 * g:4 * g + 4, :, t0:t0 + tb].rearrange("h p t -> p h t"), sqv, reads=[sqb], writes=[DB("QT%d" % j)])
                    elif nm == "ak":
                        P.dma("pool", Kd[4 * g:4 * g + 4, :, t0:t0 + tb].rearrange("h p t -> p h t"), sqv, reads=[sqb], writes=[DB("KT%d" % j)])
                if (nm == "mq") or (nm == "mk" and own):
                    sq, sqb = stq.next()
                    sqv = sq[:, 0:4 * tb].rearrange("p (h t) -> p h t", h=4)
                    for hd in range(4):
                        for hf in range(tb // HB):
                            pb_i = pbank[pbi[0] % 4]
                            pbi[0] += 1
                            for k in range(8):
                                P.op("pe", lambda e, HB=HB, k=k, hd=hd, hf=hf, pb_i=pb_i, wv=wv, hTv=hTv: e.matmul(bank(pb_i, HB), lhsT=wv[:, k, hd * 128:(hd + 1) * 128], rhs=hTv[:, k, hf * HB:(hf + 1) * HB], start=(k == 0), stop=(k == 7)),
                                     reads=[hTb, wb], writes=[bankb[pb_i]])
                            P.op("dve", lambda e, HB=HB, hd=hd, hf=hf, pb_i=pb_i, sqv=sqv: e.tensor_copy(out=sqv[:, hd, hf * HB:(hf + 1) * HB], in_=bank(pb_i, HB)), reads=[bankb[pb_i]], writes=[sqb])
                    dd = mqT[j] if nm == "mq" else mkT[j]
                    P.dma("pool", dd[:, :, t0:t0 + tb].rearrange("h p t -> p h t"), sqv, reads=[sqb], writes=[DB(("mqT%d" if nm == "mq" else "mkT%d") % j)])
                if nm == "mg":
                    gs, gsb = gst.next()
                    gsv = gs[:, 0:4 * tb].rearrange("p (r t) -> p r t", r=4)
                    for r in range(4):
                        for hf in range(tb // HB):
                            pb_i = pbank[pbi[0] % 4]
                            pbi[0] += 1
                            for k in range(8):
                                P.op("pe", lambda e, HB=HB, k=k, r=r, hf=hf, pb_i=pb_i, wv=wv, hTv=hTv: e.matmul(bank(pb_i, HB)[0:4, :], lhsT=wv[:, k, 4 * r:4 * r + 4], rhs=hTv[:, k, hf * HB:(hf + 1) * HB], start=(k == 0), stop=(k == 7)),
                                     reads=[hTb, wb], writes=[bankb[pb_i]])
                            P.op("act", lambda e, HB=HB, r=r, hf=hf, pb_i=pb_i, gsv=gsv: e.activation(out=gsv[0:4, r, hf * HB:(hf + 1) * HB], in_=bank(pb_i, HB)[0:4, :], func=AF.Identity, bias=bmg[:, r:r + 1]),
                                 reads=[bankb[pb_i], bconst], writes=[gsb])
                    dd = GG[j] if own else GGc
                    P.dma("pool", dd[:, t0:t0 + tb].rearrange("(r h) t -> h r t", h=4), gsv[0:4], reads=[gsb], writes=[DB("GG%d%d" % (j, own))])

    def phaseB(j):
        Sk = Sctx[j]
        nkt = Sk // 128
        Sq = SO[j]
        QBM = min(512, Sq)
        ktr = Rot([ar.bf(Sk) for _ in range(2)], "kt")
        vr = Rot([ar.bf(Sk) for _ in range(2)], "v")
        qr = Rot([ar.bf(QBM) for _ in range(2)], "q")
        ptr = Rot([ar.bf(2 * QBM) for _ in range(3)], "pt")
        accs = [ar.f32(2 * QBM) for _ in range(2)]
        accb = [Buf("accA"), Buf("accB")]
        ocr = Rot([ar.f32(2 * QBM) for _ in range(2)], "oc")
        tfr = Rot([tuple(ar.f32(QBM) for _ in range(3)) for _ in range(2)], "tf")
        yst = Rot([ar.bf(QBM) for _ in range(2)], "yst")
        deferred = []
        sctr = [0]
        sets = {}

        def next_set():
            s = sctr[0] % 3
            sctr[0] += 1
            return s

        def fire(force):
            keep = []
            for item in deferred:
                item[0] -= 1
                if force or item[0] <= 0:
                    item[1]()
                else:
                    keep.append(item)
            deferred[:] = keep

        for h in range(HA):
            kt_, ktb = ktr.next()
            v_, vb = vr.next()
            P.dma("sp", kt_, KT[j][h], reads=[DB("KT%d" % j)], writes=[ktb])
            vv = v_.rearrange("p (t d) -> p t d", d=128)
            P.dma("sp", vv, VV[j][h].rearrange("(t p) d -> p t d", p=128), reads=[DB("VV%d" % j)], writes=[vb])
            for qb in range((Sq + QBM - 1) // QBM):
                q0 = qb * QBM
                QB = min(QBM, Sq - q0)
                q_, qb_ = qr.next()
                q_ = q_[:, 0:QB]
                P.dma("sp", q_, QT[j][h, :, q0:q0 + QB], reads=[DB("QT%d" % j)], writes=[qb_])

                def qk(u, kt_=kt_, ktb=ktb, q_=q_, qb_=qb_, QB=QB):
                    sb = next_set()
                    sets[u] = sb
                    for m in range(2):
                        P.op("pe", lambda e, m=m, sb=sb, u=u: e.matmul(bank(2 * sb + m, QB), lhsT=kt_[64 * m:64 * m + 64, u * 128:(u + 1) * 128], rhs=q_[64 * m:64 * m + 64, :], start=True, stop=True),
                             reads=[ktb, qb_], writes=[bankb[2 * sb + m]])
                pts = {}

                def ex(u, QB=QB):
                    sb = sets.pop(u)
                    pt, ptb = ptr.next()
                    pt = pt[:, 0:2 * QB]
                    pts[u] = (pt, ptb)
                    ptv = pt.rearrange("p (m q) -> p m q", m=2)
                    src = PS[:, 2 * sb * 512:(2 * sb + 2) * 512].rearrange("p (m q) -> p m q", m=2)[:, :, 0:QB]
                    P.op("act", lambda e, ptv=ptv, src=src: e.activation(out=ptv, in_=src, func=AF.Exp, scale=0.125), reads=[bankb[2 * sb], bankb[2 * sb + 1]], writes=[ptb])

                def pv(u, vv=vv, vb=vb, QB=QB):
                    pt, ptb = pts.pop(u)
                    for m in range(2):
                        P.op("pe", lambda e, m=m, u=u, pt=pt: e.matmul(bank(6 + m, QB), lhsT=vv[:, u, :], rhs=pt[:, m * QB:(m + 1) * QB], start=(u == 0), stop=(u == nkt - 1)),
                             reads=[vb, ptb], writes=[bankb[6 + m]])
                    if u in pe_units:
                        for m in range(2):
                            P.op("pe", lambda e, m=m, u=u, pt=pt: e.matmul(bank(6 + m, QB), lhsT=onesb, rhs=pt[:, m * QB:(m + 1) * QB], start=(u == pe_units[0]), stop=False),
                                 reads=[bconst, ptb], writes=[bankb[6 + m]])
                        return
                    ai = dct[0] % 2
                    acc = accs[ai][:, 0:2 * QB]
                    if dct[0] < 2:
                        P.op("dve", lambda e, acc=acc, pt=pt: e.tensor_copy(out=acc, in_=pt), reads=[ptb], writes=[accb[ai]])
                    else:
                        P.op("dve", lambda e, acc=acc, pt=pt: e.tensor_tensor(out=acc, in0=pt, in1=acc, op=ALU.add), reads=[ptb, accb[ai]], writes=[accb[ai]])
                    dct[0] += 1
                dct = [0]
                pe_units = []
                qk(0)
                if nkt > 1:
                    qk(1)
                for u in range(nkt):
                    ex(u)
                    if u + 2 < nkt:
                        qk(u + 2)
                    pv(u)
                    fire(False)
                fire(True)
                es = next_set()
                nacc = min(2, nkt)
                for ai in range(nacc):
                    acc = accs[ai][:, 0:2 * QB]
                    for m in range(2):
                        P.op("pe", lambda e, m=m, acc=acc, QB=QB, ai=ai, es=es: e.matmul(bank(2 * es + m, QB), lhsT=onesf, rhs=acc[:, m * QB:(m + 1) * QB], start=(ai == 0), stop=(ai == nacc - 1)),
                             reads=[bconst, accb[ai]], writes=[bankb[2 * es + m]])
                oc, ocb = ocr.next()
                (r0, r1, sq), tfb = tfr.next()
                r0 = r0[:, 0:QB]
                r1 = r1[:, 0:QB]
                sq = sq[:, 0:QB]
                a0 = oc[:, 0:QB]
                a1 = oc[:, QBM:QBM + QB]
                P.op("act", lambda e, a0=a0, QB=QB: e.activation(out=a0, in_=bank(6, QB), func=AF.Copy), reads=[bankb[6]], writes=[ocb])
                P.op("dve", lambda e, a1=a1, QB=QB: e.tensor_copy(out=a1, in_=bank(7, QB)), reads=[bankb[7]], writes=[ocb])
                P.op("dve", lambda e, r0=r0, QB=QB, es=es: e.reciprocal(out=r0, in_=bank(2 * es, QB)), reads=[bankb[2 * es]], writes=[tfb])
                P.op("dve", lambda e, r1=r1, QB=QB, es=es: e.reciprocal(out=r1, in_=bank(2 * es + 1, QB)), reads=[bankb[2 * es + 1]], writes=[tfb])
                P.op("dve", lambda e, a0=a0, r0=r0: e.tensor_tensor(out=a0, in0=a0, in1=r0, op=ALU.mult), reads=[ocb, tfb], writes=[ocb])
                P.op("dve", lambda e, a1=a1, r1=r1: e.tensor_tensor(out=a1, in0=a1, in1=r1, op=ALU.mult), reads=[ocb, tfb], writes=[ocb])
                P.op("dve", lambda e, a0=a0, a1=a1: e.scalar_tensor_tensor(out=a0, in0=a1, scalar=nlam, in1=a0, op0=ALU.mult, op1=ALU.add), reads=[ocb, bconst], writes=[ocb])
                P.op("dve", lambda e, a0=a0, sq=sq: e.tensor_tensor(out=sq, in0=a0, in1=a0, op=ALU.mult), reads=[ocb], writes=[tfb])

                def part2(a0=a0, r0=r0, sq=sq, ocb=ocb, tfb=tfb, QB=QB, h=h, q0=q0):
                    s2 = 2 * next_set()
                    P.op("pe", lambda e: e.matmul(bank(s2, QB), lhsT=onesf, rhs=sq, start=True, stop=True), reads=[tfb, bconst], writes=[bankb[s2]])
                    P.op("act", lambda e: e.activation(out=r0, in_=bank(s2, QB), func=AF.Sqrt, scale=1.0 / 128.0, bias=EPS), reads=[bankb[s2]], writes=[tfb])
                    P.op("dve", lambda e: e.reciprocal(out=r0, in_=r0), reads=[tfb], writes=[tfb])
                    ys, ysb = yst.next()
                    ys = ys[:, 0:QB]
                    P.op("dve", lambda e: e.scalar_tensor_tensor(out=ys, in0=a0, scalar=gsc, in1=r0, op0=ALU.mult, op1=ALU.mult), reads=[tfb, ocb, bconst], writes=[ysb])
                    P.dma("pool", yaT[j][h, :, q0:q0 + QB], ys, reads=[ysb], writes=[DB("yaT%d" % j)])
                deferred.append([6, part2])
        fire(True)

    def phaseC(j):
        is_p = (j == 2)
        So = SO[j]
        nch = So // 128
        UG = ar.f32(nch * 16)
        UGv = UG.rearrange("p (c q) -> p c q", q=16)
        DECB = ar.f32(8 * nch)
        Cf = [ar.f32(257) for _ in range(8)]
        Cb = [ar.bf(257) for _ in range(8)]
        ini = ar.f32(16)
        Minit = [ini[0:4, 0:1], ini[0:4, 1:2]]
        Binit = [ini[0:4, 2:3], ini[0:4, 3:4]]
        bC = Buf("Cstate")
        bini = Buf("ini")
        bUG = Buf("UG")
        P.op("dve", lambda e: e.memset(ini, 0.0), writes=[bini])
        m0 = ar.mark()
        def c2():
            PC = min(2048, Sc)
            npc = Sc // PC
            It = ar.f32(PC)
            Ft = ar.f32(PC)
            Pt = ar.f32(PC)
            Mt = ar.f32(PC)
            T1 = ar.f32(PC)
            T2 = ar.f32(PC)
            pm = ar.f32(2 * npc)
            bs = ar.f32(2 * npc)
            carry = ar.f32(4)
            tot = ar.f32(4)
            bg_ = Buf("c2g")
            bsm = Buf("c2s")
            P.op("dve", lambda e: e.memset(carry, 0.0), writes=[bsm])
            for d in range(2):
                msk = mb_in if d == 0 else ma_in
                for pc in range(npc):
                    sl = slice(pc * PC, (pc + 1) * PC)
                    P.dma("sp", It[0:4, :], GGc[d * 8:d * 8 + 4, sl], reads=[DB("GG20")], writes=[bg_])
                    P.dma("sp", Ft[0:4, :], GGc[d * 8 + 4:d * 8 + 8, sl], reads=[DB("GG20")], writes=[bg_])
                    P.dma("sp", Mt[0:4, :], msk[:, sl], writes=[bg_])
                    P.op("act", lambda e: e.activation(out=Ft[0:4, :], in_=Ft[0:4, :], func=AF.Exp, scale=-1.0), reads=[bg_], writes=[bg_])
                    P.op("act", lambda e: e.activation(out=Ft[0:4, :], in_=Ft[0:4, :], func=AF.Ln, bias=1.0), reads=[bg_], writes=[bg_])
                    init = 0.0 if pc == 0 else carry[0:4, d:d + 1]
                    P.op("dve", lambda e, init=init: e.tensor_tensor_scan(out=Pt[0:4, :], data0=onesf[0:4, 0:1].to_broadcast([4, PC]), data1=Ft[0:4, :], initial=init, op0=ALU.mult, op1=ALU.add),
                         reads=[bg_, bconst, bsm], writes=[bg_])
                    P.op("dve", lambda e, d=d: e.tensor_copy(out=carry[0:4, d:d + 1], in_=Pt[0:4, PC - 1:PC]), reads=[bg_], writes=[bsm])
                    P.op("dve", lambda e: e.tensor_tensor(out=T1[0:4, :], in0=Ft[0:4, :], in1=Mt[0:4, :], op=ALU.mult), reads=[bg_], writes=[bg_])
                    P.op("dve", lambda e, d=d, pc=pc: e.tensor_reduce(out=bs[0:4, d * npc + pc:d * npc + pc + 1], in_=T1[0:4, :], axis=AX.X, op=ALU.add), reads=[bg_], writes=[bsm])
                    if d == 0:
                        P.op("dve", lambda e: e.tensor_tensor(out=It[0:4, :], in0=It[0:4, :], in1=Pt[0:4, :], op=ALU.add), reads=[bg_], writes=[bg_])
                    else:
                        P.op("dve", lambda e: e.tensor_tensor(out=It[0:4, :], in0=It[0:4, :], in1=Ft[0:4, :], op=ALU.add), reads=[bg_], writes=[bg_])
                        P.op("dve", lambda e: e.tensor_tensor(out=It[0:4, :], in0=It[0:4, :], in1=Pt[0:4, :], op=ALU.subtract), reads=[bg_], writes=[bg_])
                    P.op("dve", lambda e: e.tensor_tensor(out=It[0:4, :], in0=It[0:4, :], in1=Mt[0:4, :], op=ALU.mult), reads=[bg_], writes=[bg_])
                    P.op("dve", lambda e: e.tensor_scalar(out=T2[0:4, :], in0=Mt[0:4, :], scalar1=-NEG, scalar2=NEG, op0=ALU.mult, op1=ALU.add), reads=[bg_], writes=[bg_])
                    P.op("dve", lambda e: e.tensor_tensor(out=It[0:4, :], in0=It[0:4, :], in1=T2[0:4, :], op=ALU.add), reads=[bg_], writes=[bg_])
                    P.op("dve", lambda e, d=d, pc=pc: e.tensor_reduce(out=pm[0:4, d * npc + pc:d * npc + pc + 1], in_=It[0:4, :], axis=AX.X, op=ALU.max), reads=[bg_], writes=[bsm])
                    P.dma("sp", AMs[d, :, sl], It[0:4, :], reads=[bg_], writes=[DB("AMs")])
            bias2 = ar.f32(4)
            for d in range(2):
                P.op("dve", lambda e, d=d: e.tensor_reduce(out=Binit[d], in_=bs[0:4, d * npc:(d + 1) * npc], axis=AX.X, op=ALU.add), reads=[bsm], writes=[bini])
                P.op("dve", lambda e, d=d: e.tensor_reduce(out=Minit[d], in_=pm[0:4, d * npc:(d + 1) * npc], axis=AX.X, op=ALU.max), reads=[bsm], writes=[bini])
            P.op("dve", lambda e: e.tensor_tensor(out=Minit[1], in0=Minit[1], in1=carry[0:4, 1:2], op=ALU.add), reads=[bini, bsm], writes=[bini])
            for d in range(2):
                P.op("dve", lambda e, d=d: e.tensor_scalar(out=Minit[d], in0=Minit[d], scalar1=0.0, scalar2=None, op0=ALU.max), reads=[bini], writes=[bini])
            P.op("dve", lambda e: e.tensor_scalar(out=bias2[0:4, 0:1], in0=Minit[0], scalar1=-1.0, scalar2=LNSC, op0=ALU.mult, op1=ALU.add), reads=[bini], writes=[bsm])
            P.op("dve", lambda e: e.tensor_tensor(out=bias2[0:4, 1:2], in0=carry[0:4, 1:2], in1=Minit[1], op=ALU.subtract), reads=[bini, bsm], writes=[bsm])
            P.op("dve", lambda e: e.tensor_scalar(out=bias2[0:4, 1:2], in0=bias2[0:4, 1:2], scalar1=LNSC, scalar2=None, op0=ALU.add), reads=[bsm], writes=[bsm])
            wtr = Rot([ar.f32(4) for _ in range(3)], "wt")
            kr = Rot([ar.bf(512) for _ in range(3)], "kc")
            var_ = [ar.bf(4 * 257) for _ in range(3)]
            vbufs = [Buf("vc%d" % i) for i in range(3)]
            for t, vb_ in zip(var_, vbufs):
                P.op("dve", lambda e, t=t: e.memset(t, 1.0), writes=[vb_])
            kur = Rot([ar.bf(128) for _ in range(4)], "ku")
            w0 = ar.f32(PC)
            bw = Buf("wrow")
            for d in range(2):
                for pc in range(npc):
                    sl = slice(pc * PC, (pc + 1) * PC)
                    P.dma("sp", w0[0:4, :], AMs[d, :, sl], reads=[DB("AMs")], writes=[bw])
                    P.op("act", lambda e, d=d: e.activation(out=w0[0:4, :], in_=w0[0:4, :], func=AF.Exp, bias=bias2[0:4, d:d + 1]), reads=[bw, bsm], writes=[bw])
                    for cl in range(PC // 128):
                        cc = pc * (PC // 128) + cl
                        wt_, wtb = wtr.next()
                        P.op("pe", lambda e, cl=cl: e.transpose(PS[:, 7 * 512:7 * 512 + 4], w0[0:4, cl * 128:(cl + 1) * 128], identf[0:4, 0:4]), reads=[bw, bconst], writes=[bankb[7]])
                        P.op("dve", lambda e, wt_=wt_: e.tensor_copy(out=wt_, in_=PS[:, 7 * 512:7 * 512 + 4]), reads=[bankb[7]], writes=[wtb])
                        k_, kb_ = kr.next()
                        P.dma("sp", k_, mktc[cc * 128:(cc + 1) * 128, :], reads=[DB("mkt20")], writes=[kb_])
                        vi = cc % 3
                        va = var_[vi].rearrange("p (h f) -> p h f", h=4)
                        P.dma("sp", va[:, :, 0:256], mvtc[cc * 128:(cc + 1) * 128, :].rearrange("p (h f) -> p h f", h=4), reads=[DB("mvt20")], writes=[vbufs[vi]])
                        for hd in range(4):
                            ku, kub = kur.next()
                            P.op("dve", lambda e, ku=ku, k_=k_, hd=hd, wt_=wt_: e.tensor_scalar(out=ku, in0=k_[:, hd * 128:(hd + 1) * 128], scalar1=wt_[:, hd:hd + 1], scalar2=None, op0=ALU.mult),
                                 reads=[kb_, wtb], writes=[kub])
                            P.op("pe", lambda e, ku=ku, va=va, hd=hd, cc=cc: e.matmul(bank(hd, 257), lhsT=ku, rhs=va[:, hd, :], start=(cc == 0), stop=(cc == nchc - 1)),
                                 reads=[kub, vbufs[vi]], writes=[bankb[hd]])
                for hd in range(4):
                    P.op("dve", lambda e, hd=hd, d=d: e.tensor_copy(out=Cf[d * 4 + hd], in_=bank(hd, 257)), reads=[bankb[hd]], writes=[bC])
        if is_p:
            c2()
            barrier_local(m0)
        else:
            for bi in range(8):
                P.op("dve", lambda e, bi=bi: e.memset(Cf[bi], 0.0), writes=[bC])
        It = ar.f32(So)
        Ft = ar.f32(So)
        Pt = ar.f32(So)
        cm = ar.f32(nch)
        Me = ar.f32(nch)
        Me2 = ar.f32(nch)
        Mp = ar.f32(nch)
        dec = [ar.f32(nch), ar.f32(nch)]
        tt = ar.f32(4)
        bg_ = Buf("c1g")
        bsm = Buf("c1s")
        bdec = Buf("dec")
        for d in range(2):
            P.dma("sp", It[0:4, :], GG[j][d * 8:d * 8 + 4, :], reads=[DB("GG%d1" % j)], writes=[bg_])
            P.dma("sp", Ft[0:4, :], GG[j][d * 8 + 4:d * 8 + 8, :], reads=[DB("GG%d1" % j)], writes=[bg_])
            P.op("act", lambda e: e.activation(out=Ft[0:4, :], in_=Ft[0:4, :], func=AF.Exp, scale=-1.0), reads=[bg_], writes=[bg_])
            P.op("act", lambda e: e.activation(out=Ft[0:4, :], in_=Ft[0:4, :], func=AF.Ln, bias=1.0), reads=[bg_], writes=[bg_])
            P.op("dve", lambda e: e.tensor_tensor_scan(out=Pt[0:4, :], data0=onesf[0:4, 0:1].to_broadcast([4, So]), data1=Ft[0:4, :], initial=0.0, op0=ALU.mult, op1=ALU.add),
                 reads=[bg_, bconst], writes=[bg_])
            if d == 0:
                P.op("dve", lambda e: e.tensor_scalar(out=Pt[0:4, :], in0=Pt[0:4, :], scalar1=Binit[0], scalar2=None, op0=ALU.add), reads=[bg_, bini], writes=[bg_])
            else:
                P.op("dve", lambda e: e.tensor_tensor(out=tt[0:4, 0:1], in0=Pt[0:4, So - 1:So], in1=Binit[1], op=ALU.add), reads=[bg_, bini], writes=[bsm])
                P.op("dve", lambda e: e.tensor_tensor(out=Pt[0:4, :], in0=Ft[0:4, :], in1=Pt[0:4, :], op=ALU.subtract), reads=[bg_, bsm], writes=[bg_])
                P.op("dve", lambda e: e.tensor_scalar(out=Pt[0:4, :], in0=Pt[0:4, :], scalar1=tt[0:4, 0:1], scalar2=None, op0=ALU.add), reads=[bg_, bsm], writes=[bg_])
            P.op("dve", lambda e: e.tensor_tensor(out=It[0:4, :], in0=It[0:4, :], in1=Pt[0:4, :], op=ALU.add), reads=[bg_], writes=[bg_])
            P.op("dve", lambda e: e.tensor_reduce(out=cm[0:4, :], in_=It[0:4, :].rearrange("p (c t) -> p c t", t=128), axis=AX.X, op=ALU.max), reads=[bg_], writes=[bsm])
            if d == 0:
                P.op("dve", lambda e: e.tensor_tensor_scan(out=Me[0:4, :], data0=cm[0:4, :], data1=cm[0:4, :], initial=Minit[0], op0=ALU.max, op1=ALU.max), reads=[bsm, bini], writes=[bsm])
                if nch > 1:
                    P.op("dve", lambda e: e.tensor_copy(out=Mp[0:4, 1:nch], in_=Me[0:4, 0:nch - 1]), reads=[bsm], writes=[bsm])
                P.op("dve", lambda e: e.tensor_copy(out=Mp[0:4, 0:1], in_=Minit[0]), reads=[bsm, bini], writes=[bsm])
            else:
                src, dst_ = cm, Me2
                sh = 1
                while sh < nch:
                    P.op("dve", lambda e, src=src, dst_=dst_, sh=sh: e.tensor_tensor(out=dst_[0:4, 0:nch - sh], in0=src[0:4, 0:nch - sh], in1=src[0:4, sh:nch], op=ALU.max), reads=[bsm], writes=[bsm])
                    P.op("dve", lambda e, src=src, dst_=dst_, sh=sh: e.tensor_copy(out=dst_[0:4, nch - sh:nch], in_=src[0:4, nch - sh:nch]), reads=[bsm], writes=[bsm])
                    src, dst_ = dst_, src
                    sh *= 2
                P.op("dve", lambda e, src=src: e.tensor_scalar(out=Me[0:4, :], in0=src[0:4, :], scalar1=Minit[1], scalar2=None, op0=ALU.max), reads=[bsm, bini], writes=[bsm])
                if nch > 1:
                    P.op("dve", lambda e: e.tensor_copy(out=Mp[0:4, 0:nch - 1], in_=Me[0:4, 1:nch]), reads=[bsm], writes=[bsm])
                P.op("dve", lambda e: e.tensor_copy(out=Mp[0:4, nch - 1:nch], in_=Minit[1]), reads=[bsm, bini], writes=[bsm])
            P.op("dve", lambda e: e.tensor_tensor(out=Mp[0:4, :], in0=Mp[0:4, :], in1=Me[0:4, :], op=ALU.subtract), reads=[bsm], writes=[bsm])
            P.op("act", lambda e, d=d: e.activation(out=dec[d][0:4, :], in_=Mp[0:4, :], func=AF.Exp), reads=[bsm], writes=[bdec])
            Meb = Me[0:4, :].unsqueeze(2).to_broadcast([4, nch, 128])
            P.op("dve", lambda e, Meb=Meb: e.tensor_tensor(out=Pt[0:4, :].rearrange("p (c t) -> p c t", t=128), in0=Pt[0:4, :].rearrange("p (c t) -> p c t", t=128), in1=Meb, op=ALU.subtract), reads=[bg_, bsm], writes=[bg_])
            P.op("dve", lambda e, Meb=Meb: e.tensor_tensor(out=It[0:4, :].rearrange("p (c t) -> p c t", t=128), in0=It[0:4, :].rearrange("p (c t) -> p c t", t=128), in1=Meb, op=ALU.subtract), reads=[bg_, bsm], writes=[bg_])
            P.op("act", lambda e: e.activation(out=Ft[0:4, :], in_=Pt[0:4, :], func=AF.Exp), reads=[bg_], writes=[bg_])
            P.op("act", lambda e: e.activation(out=It[0:4, :], in_=It[0:4, :], func=AF.Exp, bias=LNSC), reads=[bg_], writes=[bg_])
            for c in range(nch):
                P.op("pe", lambda e, c=c, d=d: e.transpose(PS[:, d * 512 + c * 8:d * 512 + c * 8 + 4], It[0:4, c * 128:(c + 1) * 128], identf[0:4, 0:4]), reads=[bg_, bconst], writes=[bankb[d]])
                P.op("pe", lambda e, c=c, d=d: e.transpose(PS[:, d * 512 + c * 8 + 4:d * 512 + c * 8 + 8], Ft[0:4, c * 128:(c + 1) * 128], identf[0:4, 0:4]), reads=[bg_, bconst], writes=[bankb[d]])
            for hd in range(4):
                P.op("pe", lambda e, hd=hd, d=d: e.matmul(PS[:, 1024 + (d * 4 + hd) * nch:1024 + (d * 4 + hd + 1) * nch], lhsT=self_[0:4, hd * 128:(hd + 1) * 128], rhs=dec[d][0:4, :], start=True, stop=True),
                     reads=[bdec, bconst], writes=[bankb[2]])
        assert nch * 8 <= 512
        for d in range(2):
            P.op("dve", lambda e, d=d: e.tensor_copy(out=UGv[:, :, d * 8:(d + 1) * 8], in_=PS[:, d * 512:d * 512 + nch * 8].rearrange("p (c q) -> p c q", q=8)), reads=[bankb[d]], writes=[bUG])
        P.op("dve", lambda e: e.tensor_copy(out=DECB, in_=PS[:, 1024:1024 + 8 * nch]), reads=[bankb[2]], writes=[bUG])
        barrier_local(m0)
        qTt = ar.bf(So)
        kTt = ar.bf(So)
        ktk = ar.bf(So)
        vau = ar.bf(nch * 257)
        hsum = ar.f32(nch * 256)
        smt = ar.bf(nch * 256)
        yms = ar.bf(2 * So)
        mnb = ar.f32(1024)
        bh = Buf("headdata")
        bhs = Buf("hsum")
        bym = Buf("yms")
        bmn = Buf("mnb")
        P.dma("sp", mnb, mnorm.partition_broadcast(128), writes=[bmn])
        vav = vau.rearrange("p (c f) -> p c f", f=257)
        ktkv = ktk.rearrange("p (c f) -> p c f", f=128)
        hsv = hsum.rearrange("p (c f) -> p c f", f=256)
        smv = smt.rearrange("p (c f) -> p c f", f=256)
        ymv = yms.rearrange("p (i t) -> p i t", i=2)
        ptr = Rot([ar.bf(128) for _ in range(4)], "ptm")
        kur = Rot([ar.bf(128) for _ in range(4)], "kum")
        dr = Rot([ar.f32(2) for _ in range(4)], "den")
        fin = Rot([(ar.f32(6), ar.f32(2), ar.f32(1), ar.f32(256), ar.bf(256)) for _ in range(2)], "fin")
        for hd in range(HM):
            P.dma("sp", qTt, mqT[j][hd], reads=[DB("mqT%d" % j)], writes=[bh])
            P.dma("sp", kTt, mkT[j][hd], reads=[DB("mkT%d" % j)], writes=[bh])
            P.dma("sp", ktkv, mkt[j][:, hd * 128:(hd + 1) * 128].rearrange("(c p) f -> p c f", p=128), reads=[DB("mkt%d1" % j)], writes=[bh])
            P.op("dve", lambda e: e.memset(vau, 1.0), reads=[], writes=[bh])
            P.dma("sp", vav[:, :, 0:256], mvt[j][:, hd * 256:(hd + 1) * 256].rearrange("(c p) f -> p c f", p=128), reads=[DB("mvt%d1" % j)], writes=[bh])
            P.dma("sp", smv, smo[j][:, hd * 256:(hd + 1) * 256].rearrange("(c p) f -> p c f", p=128), reads=[DB("smo%d" % j)], writes=[bh])
            P.op("dve", lambda e: e.memset(hsum, 0.0), writes=[bhs])
            for i in range(nch):
                for d in range(2):
                    c = i if d == 0 else nch - 1 - i
                    bi = d * 4 + hd
                    ci = bi * nch + c
                    msk = maskF if d == 0 else maskB
                    ucol = UGv[:, c, d * 8 + hd:d * 8 + hd + 1]
                    gcol = UGv[:, c, d * 8 + 4 + hd:d * 8 + 4 + hd + 1]
                    cs = slice(c * 128, (c + 1) * 128)
                    bS, bX, bU = 2, 3 + d, 5 + d
                    P.op("dve", lambda e, bi=bi, ci=ci: e.tensor_scalar(out=Cb[bi], in0=Cf[bi], scalar1=DECB[:, ci:ci + 1], scalar2=None, op0=ALU.mult), reads=[bC, bUG], writes=[bC])
                    P.op("dve", lambda e, bi=bi, ci=ci: e.tensor_scalar(out=Cf[bi], in0=Cf[bi], scalar1=DECB[:, ci:ci + 1], scalar2=None, op0=ALU.mult), reads=[bC, bUG], writes=[bC])
                    P.op("pe", lambda e, cs=cs, d=d: e.matmul(PS[:, bS * 512 + d * 128:bS * 512 + (d + 1) * 128], lhsT=kTt[:, cs], rhs=qTt[:, cs], start=True, stop=True), reads=[bh], writes=[bankb[bS]])
                    pt, ptb = ptr.next()
                    P.op("dve", lambda e, pt=pt, d=d, ucol=ucol, msk=msk: e.scalar_tensor_tensor(out=pt, in0=PS[:, bS * 512 + d * 128:bS * 512 + (d + 1) * 128], scalar=ucol, in1=msk, op0=ALU.mult, op1=ALU.mult),
                         reads=[bankb[bS], bUG, bconst], writes=[ptb])
                    ku, kub = kur.next()
                    P.op("dve", lambda e, ku=ku, c=c, ucol=ucol: e.tensor_scalar(out=ku, in0=ktkv[:, c, :], scalar1=ucol, scalar2=None, op0=ALU.mult), reads=[bh, bUG], writes=[kub])
                    P.op("pe", lambda e, pt=pt, c=c, bX=bX: e.matmul(bank(bX, 257), lhsT=pt, rhs=vav[:, c, :], start=True, stop=False), reads=[ptb, bh], writes=[bankb[bX]])
                    P.op("pe", lambda e, cs=cs, bi=bi, bX=bX: e.matmul(bank(bX, 257), lhsT=qTt[:, cs], rhs=Cb[bi], start=False, stop=True), reads=[bh, bC], writes=[bankb[bX]])
                    P.op("pe", lambda e, ku=ku, c=c, bU=bU: e.matmul(bank(bU, 257), lhsT=ku, rhs=vav[:, c, :], start=True, stop=True), reads=[kub, bh], writes=[bankb[bU]])
                    dn, dnb = dr.next()
                    P.op("dve", lambda e, dn=dn, bX=bX: e.tensor_scalar(out=dn[:, 0:1], in0=bank(bX, 257)[:, 256:257], scalar1=-1.0, scalar2=None, op0=ALU.mult), reads=[bankb[bX]], writes=[dnb])
                    P.op("dve", lambda e, dn=dn, bX=bX: e.tensor_tensor(out=dn[:, 0:1], in0=dn[:, 0:1], in1=bank(bX, 257)[:, 256:257], op=ALU.max), reads=[bankb[bX], dnb], writes=[dnb])
                    P.op("dve", lambda e, dn=dn, gcol=gcol: e.tensor_tensor(out=dn[:, 0:1], in0=dn[:, 0:1], in1=gcol, op=ALU.max), reads=[dnb, bUG], writes=[dnb])
                    P.op("dve", lambda e, dn=dn: e.reciprocal(out=dn[:, 1:2], in_=dn[:, 0:1]), reads=[dnb], writes=[dnb])
                    P.op("dve", lambda e, dn=dn, bX=bX, c=c: e.scalar_tensor_tensor(out=hsv[:, c, :], in0=bank(bX, 256), scalar=dn[:, 1:2], in1=hsv[:, c, :], op0=ALU.mult, op1=ALU.add),
                         reads=[bankb[bX], dnb, bhs], writes=[bhs])
                    P.op("dve", lambda e, bi=bi, bU=bU: e.tensor_tensor(out=Cf[bi], in0=Cf[bi], in1=bank(bU, 257), op=ALU.add), reads=[bC, bankb[bU]], writes=[bC])
            for c in range(nch):
                (stats, mv, rs, hn, yb), fb = fin.next()
                P.op("dve", lambda e, stats=stats, c=c: e.bn_stats(out=stats, in_=hsv[:, c, :]), reads=[bhs], writes=[fb])
                P.op("dve", lambda e, stats=stats, mv=mv: e.bn_aggr(out=mv, in_=stats), reads=[fb], writes=[fb])
                P.op("act", lambda e, rs=rs, mv=mv: e.activation(out=rs, in_=mv[:, 1:2], func=AF.Sqrt, bias=EPS), reads=[fb], writes=[fb])
                P.op("dve", lambda e, rs=rs: e.reciprocal(out=rs, in_=rs), reads=[fb], writes=[fb])
                P.op("dve", lambda e, hn=hn, mv=mv, rs=rs, c=c: e.tensor_scalar(out=hn, in0=hsv[:, c, :], scalar1=mv[:, 0:1], scalar2=rs, op0=ALU.subtract, op1=ALU.mult), reads=[bhs, fb], writes=[fb])
                P.op("dve", lambda e, hn=hn, hd=hd: e.tensor_tensor(out=hn, in0=hn, in1=mnb[:, hd * 256:(hd + 1) * 256], op=ALU.mult), reads=[fb, bmn], writes=[fb])
                P.op("dve", lambda e, hn=hn, yb=yb, c=c: e.tensor_tensor(out=yb, in0=hn, in1=smv[:, c, :], op=ALU.mult), reads=[fb, bh], writes=[fb])
                tb_i = 7
                tpb = bankbf(tb_i)
                for i2 in range(2):
                    P.op("pe", lambda e, i2=i2, yb=yb, tpb=tpb: e.transpose(tpb[:, i2 * 128:(i2 + 1) * 128], yb[:, i2 * 128:(i2 + 1) * 128], identb), reads=[fb, bconst], writes=[bankb[tb_i]])
                P.op("act", lambda e, c=c, tpb=tpb: e.activation(out=ymv[:, :, c * 128:(c + 1) * 128], in_=tpb[:, 0:256].rearrange("p (i t) -> p i t", i=2), func=AF.Copy), reads=[bankb[tb_i]], writes=[bym])
            P.dma("pool", ymT[j][2 * hd:2 * hd + 2].rearrange("i p t -> p i t"), ymv, reads=[bym], writes=[DB("ymT%d" % j)])

    def barrier_local(m):
        barrier()
        ar.reset(m)

    def phaseD1(j, wpa, wpm, wo, bw, g1b, l1g, l1b, bD):
        So = SO[j]
        nch = So // 128
        yar = Rot([ar.bf(8 * 128) for _ in range(3)], "ya")
        ymr = Rot([ar.bf(8 * 128) for _ in range(3)], "ym")
        sgr = Rot([ar.bf(2048) for _ in range(3)], "sg")
        xr = Rot([ar.f32(1024) for _ in range(3)], "xd")
        tr_ = Rot([ar.f32(1024) for _ in range(3)], "td")
        mir = Rot([ar.bf(1024) for _ in range(3)], "mi")
        mTr = Rot([ar.bf(1024) for _ in range(3)], "mT")
        rr = Rot([ar.f32(1024) for _ in range(3)], "rd")
        x1r = Rot([ar.f32(1024) for _ in range(3)], "x1")
        xnr = Rot([ar.bf(1024) for _ in range(3)], "xn2")
        str_ = Rot([(ar.f32(12), ar.f32(2), ar.f32(1)) for _ in range(6)], "st2")
        h2r = Rot([ar.bf(1024) for _ in range(3)], "h2")
        P.dma("sp", g1b, adab[j, 0:1, :].partition_broadcast(128), reads=[DB("adab")], writes=[bD])
        wpav = wpa.rearrange("p (k n) -> p k n", k=8)
        wpmv = wpm.rearrange("p (k n) -> p k n", k=8)
        wov = wo.rearrange("p (k n) -> p k n", k=8)
        for t in range(nch):
            tok = t * 128
            ya, yab = yar.next()
            ym, ymb = ymr.next()
            sg, sgb = sgr.next()
            xt, xb_ = xr.next()
            yav = ya.rearrange("p (h t) -> p h t", h=8)
            ymv = ym.rearrange("p (h t) -> p h t", h=8)
            P.dma("sp", yav, yaT[j][:, :, tok:tok + 128].rearrange("h p t -> p h t"), reads=[DB("yaT%d" % j)], writes=[yab])
            P.dma("sp", ymv, ymT[j][:, :, tok:tok + 128].rearrange("h p t -> p h t"), reads=[DB("ymT%d" % j)], writes=[ymb])
            P.dma("sp", sg, sbg[j][tok:tok + 128, :], reads=[DB("sbg%d" % j)], writes=[sgb])
            P.dma("sp", xt, xs[j][tok:tok + 128, :], writes=[xb_])
            for n in range(2):
                for k in range(8):
                    P.op("pe", lambda e, n=n, k=k, yav=yav: e.matmul(bank(n), lhsT=yav[:, k, :], rhs=wpav[:, k, n * 512:(n + 1) * 512], start=(k == 0), stop=(k == 7)), reads=[yab, bw], writes=[bankb[n]])
            for n in range(2):
                for k in range(8):
                    P.op("pe", lambda e, n=n, k=k, ymv=ymv: e.matmul(bank(2 + n), lhsT=ymv[:, k, :], rhs=wpmv[:, k, n * 512:(n + 1) * 512], start=(k == 0), stop=(k == 7)), reads=[ymb, bw], writes=[bankb[2 + n]])
            tm, tmb = tr_.next()
            mi, mib = mir.next()
            P.op("dve", lambda e, tm=tm, sg=sg: e.tensor_tensor(out=tm, in0=PS[:, 0:1024], in1=sg[:, 0:1024], op=ALU.mult), reads=[bankb[0], bankb[1], sgb], writes=[tmb])
            P.op("dve", lambda e, mi=mi, sg=sg: e.tensor_tensor(out=mi, in0=PS[:, 1024:2048], in1=sg[:, 1024:2048], op=ALU.mult), reads=[bankb[2], bankb[3], sgb], writes=[mib])
            P.op("dve", lambda e, mi=mi, tm=tm: e.tensor_tensor(out=mi, in0=mi, in1=tm, op=ALU.add), reads=[tmb, mib], writes=[mib])
            mT, mTb = mTr.next()
            tpb = bankbf(6)
            for k in range(8):
                P.op("pe", lambda e, k=k, mi=mi: e.transpose(tpb[:, k * 128:(k + 1) * 128], mi[:, k * 128:(k + 1) * 128], identb), reads=[mib, bconst], writes=[bankb[6]])
            P.op("act", lambda e, mT=mT: e.activation(out=mT, in_=tpb, func=AF.Copy), reads=[bankb[6]], writes=[mTb])
            mTv = mT.rearrange("p (k t) -> p k t", k=8)
            for n in range(2):
                for k in range(8):
                    P.op("pe", lambda e, n=n, k=k, mTv=mTv: e.matmul(bank(4 + n), lhsT=mTv[:, k, :], rhs=wov[:, k, n * 512:(n + 1) * 512], start=(k == 0), stop=(k == 7)), reads=[mTb, bw], writes=[bankb[4 + n]])
            r_, rb_ = rr.next()
            P.op("dve", lambda e, r_=r_: e.tensor_tensor(out=r_, in0=PS[:, 2048:3072], in1=g1b, op=ALU.mult), reads=[bankb[4], bankb[5], bD], writes=[rb_])
            P.op("dve", lambda e, r_=r_, xt=xt: e.scalar_tensor_tensor(out=r_, in0=xt, scalar=ALPHA_C, in1=r_, op0=ALU.mult, op1=ALU.add), reads=[xb_, rb_], writes=[rb_])
            x1, x1b = x1r.next()
            st, _ = str_.next()
            stats, mv, rs = st
            P.op("dve", lambda e, stats=stats, r_=r_: e.bn_stats(out=stats[:, 0:6], in_=r_[:, 0:512]), reads=[rb_], writes=[x1b])
            P.op("dve", lambda e, stats=stats, r_=r_: e.bn_stats(out=stats[:, 6:12], in_=r_[:, 512:1024]), reads=[rb_], writes=[x1b])
            P.op("dve", lambda e, stats=stats, mv=mv: e.bn_aggr(out=mv, in_=stats), reads=[x1b], writes=[x1b])
            P.op("act", lambda e, rs=rs, mv=mv: e.activation(out=rs, in_=mv[:, 1:2], func=AF.Sqrt, bias=EPS), reads=[x1b], writes=[x1b])
            P.op("dve", lambda e, rs=rs: e.reciprocal(out=rs, in_=rs), reads=[x1b], writes=[x1b])
            P.op("dve", lambda e, x1=x1, r_=r_, mv=mv, rs=rs: e.tensor_scalar(out=x1, in0=r_, scalar1=mv[:, 0:1], scalar2=rs, op0=ALU.subtract, op1=ALU.mult), reads=[rb_, x1b], writes=[x1b])
            P.op("dve", lambda e, x1=x1: e.tensor_tensor(out=x1, in0=x1, in1=l1g, op=ALU.mult), reads=[x1b, bD], writes=[x1b])
            P.op("dve", lambda e, x1=x1: e.tensor_tensor(out=x1, in0=x1, in1=l1b, op=ALU.add), reads=[x1b, bD], writes=[x1b])
            P.dma("pool", x1s[j][tok:tok + 128, :], x1, reads=[x1b], writes=[DB("x1s%d" % j)])
            xn, xnb = xnr.next()
            st2, _ = str_.next()
            h2, h2b = h2r.next()
            h2v = h2.rearrange("p (k t) -> p k t", k=8)
            ln_to_hT(x1, x1b, (h2b, h2v), sc2p1, sh2, j, 7, [], st2, xn, xnb)
            P.dma("pool", h2T[j][:, :, 1 + tok:1 + tok + 128].rearrange("k p t -> p k t"), h2v, reads=[h2b], writes=[DB("h2T%d" % j)])

    def phaseD2(j, wup, wdn, bw, g2b, l2g, l2b, bD):
        So = SO[j]
        TB2 = min(256, So)
        ntb = TB2 // 128
        hr = Rot([ar.bf(8 * (TB2 + 2)) for _ in range(1)], "h2l")
        zr = Rot([ar.bf(NF * TB2) for _ in range(1)], "zT")
        ar_ = Rot([ar.f32(TB2) for _ in range(4)], "ca")
        br_ = Rot([ar.f32(TB2) for _ in range(4)], "cb")
        uvr = Rot([ar.bf(TB2) for _ in range(4)], "uvs")
        x1r = Rot([ar.f32(1024) for _ in range(1)], "x1l")
        rr = Rot([ar.f32(1024) for _ in range(1)], "r2")
        str_ = Rot([(ar.f32(12), ar.f32(2), ar.f32(1)) for _ in range(2)], "st3")
        P.dma("sp", g2b, adab[j, 1:2, :].partition_broadcast(128), reads=[DB("adab")], writes=[bD])
        wupv = wup.rearrange("p (k n) -> p k n", k=8)
        wdnv = wdn.rearrange("p (f n) -> p f n", f=NF)
        GC = 2.0 * math.sqrt(2.0 / math.pi)
        pbi = 0
        for blk in range(So // TB2):
            t0 = blk * TB2
            hh, hb = hr.next()
            hv = hh.rearrange("p (k t) -> p k t", k=8)
            P.dma("sp", hv, h2T[j][:, :, t0:t0 + TB2 + 2].rearrange("k p t -> p k t"), reads=[DB("h2T%d" % j)], writes=[hb])
            z, zb = zr.next()
            zv = z.rearrange("p (f t) -> p f t", f=NF)
            stash = {}

            def s12(f):
                nonlocal pbi
                bg_i = (pbi % 3) * 2
                pbi += 1
                for k in range(8):
                    P.op("pe", lambda e, f=f, k=k, bg_i=bg_i, hv=hv: e.matmul(bank(bg_i, TB2 + 2), lhsT=wupv[:, k, f * 128:(f + 1) * 128], rhs=hv[:, k, :], start=(k == 0), stop=(k == 7)), reads=[bw, hb], writes=[bankb[bg_i]])
                for k in range(8):
                    P.op("pe", lambda e, f=f, k=k, bg_i=bg_i, hv=hv: e.matmul(bank(bg_i + 1, TB2), lhsT=wupv[:, k, DFF + f * 128:DFF + (f + 1) * 128], rhs=hv[:, k, 1:TB2 + 1], start=(k == 0), stop=(k == 7)), reads=[bw, hb], writes=[bankb[bg_i + 1]])
                a, ab = ar_.next()
                b2, bb = br_.next()
                uvs, uvb = uvr.next()
                stash[f] = (a, ab, b2, bb, uvs, uvb)
                P.op("act", lambda e, uvs=uvs, bg_i=bg_i: e.activation(out=uvs, in_=bank(bg_i + 1, TB2), func=AF.Copy), reads=[bankb[bg_i + 1]], writes=[uvb])
                ug = bank(bg_i, TB2 + 2)
                P.op("dve", lambda e, a=a, ug=ug, f=f: e.tensor_scalar(out=a, in0=ug[:, 0:TB2], scalar1=cwT[:, 3 * f:3 * f + 1], scalar2=cbT[:, f:f + 1], op0=ALU.mult, op1=ALU.add), reads=[bankb[bg_i], bconst], writes=[ab])
                P.op("dve", lambda e, a=a, ug=ug, f=f: e.scalar_tensor_tensor(out=a, in0=ug[:, 1:TB2 + 1], scalar=cwT[:, 3 * f + 1:3 * f + 2], in1=a, op0=ALU.mult, op1=ALU.add), reads=[bankb[bg_i], bconst, ab], writes=[ab])
                P.op("dve", lambda e, a=a, ug=ug, f=f: e.scalar_tensor_tensor(out=a, in0=ug[:, 2:TB2 + 2], scalar=cwT[:, 3 * f + 2:3 * f + 3], in1=a, op0=ALU.mult, op1=ALU.add), reads=[bankb[bg_i], bconst, ab], writes=[ab])
                P.op("act", lambda e, a=a, b2=b2: e.activation(out=b2, in_=a, func=AF.Square, scale=math.sqrt(0.044715)), reads=[ab], writes=[bb])

            def s3(f):
                a, ab, b2, bb, uvs, uvb = stash[f]
                P.op("dve", lambda e, a=a, b2=b2: e.scalar_tensor_tensor(out=b2, in0=b2, scalar=1.0, in1=a, op0=ALU.add, op1=ALU.mult), reads=[ab, bb], writes=[bb])
                P.op("act", lambda e, b2=b2: e.activation(out=b2, in_=b2, func=AF.Sigmoid, scale=GC), reads=[bb], writes=[bb])

            def s4(f):
                a, ab, b2, bb, uvs, uvb = stash.pop(f)
                P.op("dve", lambda e, a=a, b2=b2: e.tensor_tensor(out=b2, in0=b2, in1=a, op=ALU.mult), reads=[ab, bb], writes=[bb])
                P.op("dve", lambda e, b2=b2, f=f, uvs=uvs, zv=zv: e.tensor_tensor(out=zv[:, f, :], in0=uvs, in1=b2, op=ALU.mult), reads=[uvb, bb], writes=[zb])

            for t in range(NF + 2):
                if t < NF:
                    s12(t)
                if 0 <= t - 1 < NF:
                    s3(t - 1)
                if 0 <= t - 2 < NF:
                    s4(t - 2)
            for ti in range(ntb):
                tok = t0 + ti * 128
                x1, x1b = x1r.next()
                P.dma("sp", x1, x1s[j][tok:tok + 128, :], reads=[DB("x1s%d" % j)], writes=[x1b])
                for n in range(2):
                    for f in range(NF):
                        P.op("pe", lambda e, n=n, f=f, ti=ti, zv=zv: e.matmul(bank(6 + n), lhsT=zv[:, f, ti * 128:(ti + 1) * 128], rhs=wdnv[:, f, n * 512:(n + 1) * 512], start=(f == 0), stop=(f == NF - 1)), reads=[zb, bw], writes=[bankb[6 + n]])
                r_, rb_ = rr.next()
                P.op("dve", lambda e, r_=r_: e.tensor_tensor(out=r_, in0=PS[:, 3072:4096], in1=g2b, op=ALU.mult), reads=[bankb[6], bankb[7], bD], writes=[rb_])
                P.op("dve", lambda e, r_=r_, x1=x1: e.scalar_tensor_tensor(out=r_, in0=x1, scalar=ALPHA_C, in1=r_, op0=ALU.mult, op1=ALU.add), reads=[x1b, rb_], writes=[rb_])
                yo, yob = r_, rb_
                (stats, mv, rs), _ = str_.next()
                P.op("dve", lambda e, stats=stats, r_=r_: e.bn_stats(out=stats[:, 0:6], in_=r_[:, 0:512]), reads=[rb_], writes=[yob])
                P.op("dve", lambda e, stats=stats, r_=r_: e.bn_stats(out=stats[:, 6:12], in_=r_[:, 512:1024]), reads=[rb_], writes=[yob])
                P.op("dve", lambda e, stats=stats, mv=mv: e.bn_aggr(out=mv, in_=stats), reads=[yob], writes=[yob])
                P.op("act", lambda e, rs=rs, mv=mv: e.activation(out=rs, in_=mv[:, 1:2], func=AF.Sqrt, bias=EPS), reads=[yob], writes=[yob])
                P.op("dve", lambda e, rs=rs: e.reciprocal(out=rs, in_=rs), reads=[yob], writes=[yob])
                P.op("dve", lambda e, yo=yo, r_=r_, mv=mv, rs=rs: e.tensor_scalar(out=yo, in0=r_, scalar1=mv[:, 0:1], scalar2=rs, op0=ALU.subtract, op1=ALU.mult), reads=[rb_, yob], writes=[yob])
                P.op("dve", lambda e, yo=yo: e.tensor_tensor(out=yo, in0=yo, in1=l2g, op=ALU.mult), reads=[yob, bD], writes=[yob])
                P.op("dve", lambda e, yo=yo: e.tensor_tensor(out=yo, in0=yo, in1=l2b, op=ALU.add), reads=[yob, bD], writes=[yob])
                P.dma("pool", yout[j][tok:tok + 128, :], yo, reads=[yob], writes=[DB("yout")], is_output=True)

    phase0()
    convert_w_in()
    barrier()
    for j in range(NJ):
        phaseA(j, xs[j], SO[j], True, ropeo[j])
        barrier()
    phaseA(2, xctx, Sc, False, ropec)
    barrier()
    for j in range(NJ):
        phaseB(j)
        barrier()
    for j in range(NJ):
        phaseC(j)
        barrier()
    wpa = ar.bf(8 * 1024)
    wpm = ar.bf(8 * 1024)
    wo = ar.bf(8 * 1024)
    g1b = ar.f32(1024)
    l1g = ar.f32(1024)
    l1b = ar.f32(1024)
    bw = Buf("wD1")
    bD = Buf("bD1")
    for wt_, src in ((wpa, w_pa), (wpm, w_pm), (wo, w_out)):
        P.dma("pool", wt_.rearrange("p (k n) -> p k n", k=8), src.rearrange("(k p) n -> p k n", p=128), writes=[bw])
    P.dma("sp", l1g, ln1g.partition_broadcast(128), writes=[bD])
    P.dma("sp", l1b, ln1b.partition_broadcast(128), writes=[bD])
    zt = ar.bf(16)
    bz = Buf("zt")
    P.op("dve", lambda e: e.memset(zt, 0.0), writes=[bz])
    for j in range(NJ):
        P.dma("sp", h2T[j][:, :, 0:1].rearrange("k p t -> p k t"), zt[:, 0:8].rearrange("p (k t) -> p k t", t=1), reads=[bz], writes=[DB("h2T%d" % j)], slow=True)
        P.dma("sp", h2T[j][:, :, SO[j] + 1:SO[j] + 2].rearrange("k p t -> p k t"), zt[:, 8:16].rearrange("p (k t) -> p k t", t=1), reads=[bz], writes=[DB("h2T%d" % j)], slow=True)
    mD = ar.mark()
    for j in range(NJ):
        phaseD1(j, wpa, wpm, wo, bw, g1b, l1g, l1b, bD)
        barrier()
        ar.reset(mD)
    ar.reset()
    wup = ar.bf(8 * 2 * DFF)
    wdn = ar.bf(NF * 1024)
    g2b = ar.f32(1024)
    l2g = ar.f32(1024)
    l2b = ar.f32(1024)
    bw2 = Buf("wD2")
    bD2 = Buf("bD2")
    wupv = wup.rearrange("p (k n) -> p k n", k=8)
    for k in range(8):
        P.dma("pool", wupv[:, k, :], w_up[k * 128:(k + 1) * 128, :], writes=[bw2])
    P.dma("pool", wdn.rearrange("p (f n) -> p f n", f=NF), w_down.rearrange("(f p) n -> p f n", p=128), writes=[bw2])
    P.dma("sp", l2g, ln2g.partition_broadcast(128), writes=[bD2])
    P.dma("sp", l2b, ln2b.partition_broadcast(128), writes=[bD2])
    mD = ar.mark()
    for j in range(NJ):
        phaseD2(j, wup, wdn, bw2, g2b, l2g, l2b, bD2)
        barrier()
        ar.reset(mD)
    P.emit()
    return nc


def rope_tab(pos):
    inv = (500000.0 ** (-np.arange(0, 16, 2, dtype=np.float32) / np.float32(16))).astype(np.float32)
    ang = pos.astype(np.float32)[:, None] * inv[None, :]
    return np.concatenate([np.cos(ang), np.sin(ang)], axis=1).astype(np.float32)


_NC_CACHE = {}


def kernel(x_prompt, x_sample, c_prompt, c_sample, w_ada, b_ada, w_in, lambda_qk, subln_g,
           b_mgate, mnorm_g, w_pa, w_pm, w_out, ln1_g, ln1_b, w_up, conv_w, conv_b, w_down,
           ln2_g, ln2_b):
    f = np.float32
    x_prompt = np.asarray(x_prompt, f)
    x_sample = np.asarray(x_sample, f)
    Bp, Sc, _ = x_prompt.shape
    Bs, So, _ = x_sample.shape
    ncores = 8
    assert Bs == 2 * ncores and Bp * Sc == ncores * So
    nq = Sc // So
    key = (So, Sc)
    if key not in _NC_CACHE:
        _NC_CACHE[key] = build(So, Sc)
    nc = _NC_CACHE[key]
    c_prompt = np.asarray(c_prompt, f)
    c_sample = np.asarray(c_sample, f)
    tri = np.triu(np.ones((128, 128), f))
    sel = np.zeros((4, 4, 128), f)
    for h in range(4):
        sel[h, h, :] = 1.0
    shared = {
        "w_ada": np.ascontiguousarray(np.asarray(w_ada, f)[0]),
        "b_adaT": np.ascontiguousarray(np.asarray(b_ada, f)[0].reshape(48, 128).T),
        "b_ada": np.ascontiguousarray(np.asarray(b_ada, f)[0].reshape(1, -1)),
        "w_in": np.ascontiguousarray(np.asarray(w_in, f)[0]),
        "lambda_qk": np.ascontiguousarray(np.asarray(lambda_qk, f)[0].reshape(1, 256)),
        "subln": np.ascontiguousarray(np.asarray(subln_g, f)[0].reshape(128, 1)),
        "bmgT": np.ascontiguousarray(np.asarray(b_mgate, f)[0].reshape(4, 4).T),
        "mnorm": np.ascontiguousarray(np.asarray(mnorm_g, f)[0].reshape(1, 1024)),
        "w_pa": np.ascontiguousarray(np.asarray(w_pa, f)[0]),
        "w_pm": np.ascontiguousarray(np.asarray(w_pm, f)[0]),
        "w_out": np.ascontiguousarray(np.asarray(w_out, f)[0]),
        "ln1g": np.ascontiguousarray(np.asarray(ln1_g, f)[0].reshape(1, -1)),
        "ln1b": np.ascontiguousarray(np.asarray(ln1_b, f)[0].reshape(1, -1)),
        "w_up": np.ascontiguousarray(np.asarray(w_up, f)[0]),
        "convwT": np.ascontiguousarray(np.asarray(conv_w, f)[0].reshape(3, NF, 128).transpose(2, 1, 0)),
        "convbT": np.ascontiguousarray(np.asarray(conv_b, f)[0].reshape(NF, 128).T),
        "w_down": np.ascontiguousarray(np.asarray(w_down, f)[0]),
        "ln2g": np.ascontiguousarray(np.asarray(ln2_g, f)[0].reshape(1, -1)),
        "ln2b": np.ascontiguousarray(np.asarray(ln2_b, f)[0].reshape(1, -1)),
        "ident": np.eye(128, dtype=f),
        "sel": sel.reshape(4, 512),
        "maskF": tri,
        "maskB": np.ascontiguousarray(tri.T),
        "ropec": rope_tab(np.arange(Sc)),
    }
    in_maps = []
    offs = []
    So2 = So + 256 if Sc > So + 256 else So
    for c in range(ncores):
        pb, pq = c // nq, c % nq
        m = dict(shared)
        m["xs"] = np.ascontiguousarray(np.stack([x_sample[2 * c], x_sample[2 * c + 1]]))
        st2 = min(max(pq * So - 128, 0), Sc - So2)
        offs.append(pq * So - st2)
        m["xs2"] = np.ascontiguousarray(x_prompt[pb, st2:st2 + So2])
        m["xctx"] = np.ascontiguousarray(x_prompt[pb])
        cj = np.stack([c_sample[2 * c], c_sample[2 * c + 1], c_prompt[pb]])
        m["cT"] = np.ascontiguousarray(cj.reshape(3, 8, 128).transpose(2, 1, 0))
        ro = rope_tab(np.arange(So))
        m["ropeo"] = np.ascontiguousarray(np.stack([ro, ro]))
        m["ropeo2"] = rope_tab(np.arange(st2, st2 + So2))
        pos = np.arange(Sc)
        m["mb"] = np.ascontiguousarray(np.tile((pos < st2).astype(f)[None, :], (4, 1)))
        m["ma"] = np.ascontiguousarray(np.tile((pos >= st2 + So2).astype(f)[None, :], (4, 1)))
        in_maps.append(m)
    res = run_bass_kernel_spmd(nc, in_maps, core_ids=list(range(ncores)))
    y_prompt = np.zeros((Bp, Sc, D), f)
    y_sample = np.zeros((Bs, So, D), f)
    for c in range(ncores):
        y = res.results[c]["y"]
        y2 = res.results[c]["y2"]
        pb, pq = c // nq, c % nq
        y_sample[2 * c] = y[0]
        y_sample[2 * c + 1] = y[1]
        y_prompt[pb, pq * So:(pq + 1) * So] = y2[offs[c]:offs[c] + So]
    return (y_prompt, y_sample)
```
